# Optimizing a Trainium2 kernel written in Bass

```python
import math
import jax, jax.numpy as jnp
from jax import lax
import numpy as np

D_MODEL = 1024
BATCH = 4
SEQ = 8192
DEPTH = 2

GRID_W = 64
CTX_LEN = 256
N_MIXERS = 2
FF_HIDDEN = 2816
RW_HEAD = 64
RW_HEADS = D_MODEL // RW_HEAD
DECAY_LORA = 64
AAA_LORA = 64
GATE_LORA = 160
HY_N_FILT = 2
HY_BANDS = 16
HY_EMB = 2 * HY_BANDS + 1
HY_HID = 64
HY_FAST_DECAY = 0.3
HY_SLOW_DECAY = 1.5
HY_TARGET = 1e-2
NORM_EPS = 1e-6
GN_EPS = 64e-5
N_RWKV = (DEPTH + N_MIXERS - 1) // N_MIXERS
N_HYENA = DEPTH // N_MIXERS

kernel_name = 'hybrid_rwkv7_hyena_diffusion_block'

F32 = jnp.float32


def _rmsnorm(x, g):
    x32 = x.astype(F32)
    y = x32 * lax.rsqrt(jnp.mean(x32 * x32, axis=-1, keepdims=True) + NORM_EPS) * g.astype(F32)
    return y.astype(x.dtype)


def _pre(x, g, shift, scale):
    return _rmsnorm(x, g) * (1 + scale) + shift


def _swiglu(h, w_gu, w_down):
    gate, up = jnp.split(h @ w_gu, 2, axis=-1)
    return (jax.nn.silu(gate) * up) @ w_down


def _grid_shift(h):
    B, T, D = h.shape
    rows = T // GRID_W
    q = D // 4
    g = jnp.pad(h.reshape(B, rows, GRID_W, D), ((0, 0), (1, 1), (1, 1), (0, 0)))
    out = jnp.concatenate([g[:, 1:-1, :-2, :q], g[:, 1:-1, 2:, q:2 * q],
                           g[:, :-2, 1:-1, 2 * q:3 * q], g[:, 2:, 1:-1, 3 * q:]], axis=-1)
    return out.reshape(B, T, D)


def _seq_shift(h):
    q = h.shape[-1] // 4
    p = jnp.pad(h, ((0, 0), (1, 1), (0, 0)))
    prev, nxt = p[:, :-2], p[:, 2:]
    return jnp.concatenate([prev[..., :q], nxt[..., q:2 * q], prev[..., 2 * q:3 * q], nxt[..., 3 * q:]], axis=-1)


def _both(a):
    return jnp.stack([a, jnp.flip(a, axis=1)])


def _fb(a):
    return jnp.stack([a[0], jnp.flip(a[1], axis=1)])


def _wkv_scan(S0, w, k, v, kk, ka, r=None):
    tm = lambda a: jnp.moveaxis(a, 2, 0)
    xs = (tm(w), tm(k), tm(v), tm(kk), tm(ka)) + (() if r is None else (tm(r),))

    def step(S, inp):
        w_t, k_t, v_t, kk_t, ka_t = inp[:5]
        sa = jnp.einsum('dbhvk,dbhk->dbhv', S, kk_t)
        S = S * w_t[..., None, :] - sa[..., None] * ka_t[..., None, :] + v_t[..., None] * k_t[..., None, :]
        if r is None:
            return S, None
        return S, jnp.einsum('dbhvk,dbhk->dbhv', S, inp[5])

    S, y = lax.scan(step, S0, xs)
    return S, (None if y is None else jnp.moveaxis(y, 0, 2))


def _rwkv7_sequence(h, shifted, S0, p, with_out):
    (mix, w_rkv, w_o, w0, w1, w2, a0, a1, a2, g1, g2, k_k, k_a, r_k, ln_g, ln_b) = p
    B, T, D = h.shape
    heads = lambda a: a.reshape(a.shape[:-1] + (RW_HEADS, RW_HEAD)).astype(F32)
    xm = h[None] + (shifted - h)[None] * mix[:, None, None, :]
    xr, xw, xk, xv, xa, xg = xm
    k = xk @ w_rkv[1]
    vh = heads(xv @ w_rkv[2])
    w_pre = w0[:, None, None, :] + jnp.einsum('nbtr,nrd->nbtd', jnp.tanh(jnp.einsum('btd,ndr->nbtr', xw, w1)), w2)
    decay = jnp.exp(-jnp.exp(-jax.nn.softplus(-w_pre.astype(F32)) - 0.5))
    a = jax.nn.sigmoid((a0[:, None, None, :] + jnp.einsum('nbtr,nrd->nbtd', jnp.einsum('btd,ndr->nbtr', xa, a1), a2)).astype(F32))
    ah = heads(a)
    kk = heads(k * k_k)
    kk = kk / jnp.maximum(jnp.sqrt(jnp.sum(kk * kk, axis=-1, keepdims=True)), 1e-12)
    k_dir = heads(k)[None] * (1 + (ah - 1) * heads(k_a))
    ka = kk[None] * ah
    rh = heads(xr @ w_rkv[0]) if with_out else None
    S, y = _wkv_scan(S0, _fb(heads(decay)), _fb(k_dir), _both(vh), _both(kk), _fb(ka),
                     None if rh is None else _both(rh))
    if not with_out:
        return None, S
    y = y[0] + jnp.flip(y[1], axis=1)
    mu = jnp.mean(y, axis=-1, keepdims=True)
    var = jnp.mean(jnp.square(y - mu), axis=-1, keepdims=True)
    y = (y - mu) * lax.rsqrt(var + GN_EPS) * heads(ln_g) + heads(ln_b)
    k_bonus = 0.5 * (k_dir[0] + k_dir[1])
    y = y + jnp.sum(rh * k_bonus * r_k.astype(F32), axis=-1, keepdims=True) * vh
    g = jax.nn.sigmoid(xg @ g1) @ g2
    return (y.reshape(B, T, D).astype(h.dtype) * g) @ w_o, S


def _conv3(u, w, b):
    up = jnp.pad(u, ((0, 0), (1, 1), (0, 0)))
    return up[:, :-2] * w[0] + up[:, 1:-1] * w[1] + up[:, 2:] * w[2] + b


def _hyena_filters(L, f_w1, f_b1, f_freq, f_w2, f_b2, f_w3, deltas):
    f32 = lambda a: a.astype(F32)
    t = jnp.linspace(0.0, 1.0, L, dtype=F32)[:, None]
    ang = (2 * math.pi / L) * jnp.arange(L, dtype=F32)[:, None] * jnp.linspace(1e-4, HY_BANDS - 1, HY_BANDS, dtype=F32)[None]
    z = jnp.concatenate([t, jnp.cos(ang), -jnp.sin(ang)], axis=-1)
    hid = jnp.sin(f32(f_freq[0]) * (z @ f32(f_w1) + f32(f_b1)))
    hid = jnp.sin(f32(f_freq[1]) * (hid @ f32(f_w2) + f32(f_b2)))
    filt = (hid @ f32(f_w3)).reshape(L, HY_N_FILT, -1)
    return filt * jnp.exp(-t[:, :, None] * jnp.abs(f32(deltas)))


def _fftconv(u, filt, bias):
    L = u.shape[1]
    n = 2 * L
    y = jnp.fft.irfft(jnp.fft.rfft(u, n=n, axis=1) * jnp.fft.rfft(filt, n=n, axis=0)[None], n=n, axis=1)[:, :L]
    return y + u * bias.astype(F32)


def _hyena_sequence(h, p):
    (w_in, conv_w, conv_b, f_w1, f_b1, f_freq, f_w2, f_b2, f_w3, deltas, bias, w_out) = p
    L = h.shape[1]
    u = _conv3(h @ w_in, conv_w, conv_b).astype(F32)
    x1, x2, v = jnp.split(u, 3, axis=-1)
    filt = _hyena_filters(L, f_w1, f_b1, f_freq, f_w2, f_b2, f_w3, deltas)
    z = x1 * _fftconv(v, filt[:, 0], bias[0])
    z = x2 * _fftconv(z, filt[:, 1], bias[1])
    return z.astype(h.dtype) @ w_out


def setup_inputs(seed: int = 0) -> dict:
    key = jax.random.key(seed)
    ks = iter(jax.random.split(key, 64))
    nrm = lambda shape, s: jax.random.normal(next(ks), shape, F32) * s
    uni = lambda shape, lo, hi: jax.random.uniform(next(ks), shape, F32, lo, hi)
    D, F, NR, NH = D_MODEL, FF_HIDDEN, N_RWKV, N_HYENA
    max_decay = math.log(HY_TARGET) / HY_FAST_DECAY
    min_decay = math.log(HY_TARGET) / HY_SLOW_DECAY
    deltas = jnp.broadcast_to(jnp.linspace(min_decay, max_decay, D, dtype=F32), (NH, HY_N_FILT, D)) + nrm((NH, HY_N_FILT, D), 0.05)
    return {
        'x': nrm((BATCH, SEQ, D), 1.0),
        'c': nrm((BATCH, D), 1.0),
        'ctx': nrm((BATCH, CTX_LEN, D), 1.0),
        'c_ctx': nrm((D,), 1.0),
        'mod_w': nrm((DEPTH, D, 9 * D), 0.5 * D ** -0.5),
        'mod_b': nrm((DEPTH, 9 * D), 0.02),
        'norm_g': 1.0 + nrm((DEPTH, 3, D), 0.02),
        'ffn_w_gu': nrm((DEPTH, 2, D, 2 * F), D ** -0.5),
        'ffn_w_down': nrm((DEPTH, 2, F, D), F ** -0.5),
        'rw_mix': uni((NR, 6, D), 0.0, 1.0),
        'rw_w_rkv': nrm((NR, 3, D, D), D ** -0.5),
        'rw_w_o': nrm((NR, D, D), D ** -0.5),
        'rw_w0': uni((NR, 2, D), -6.5, -1.0),
        'rw_w1': nrm((NR, 2, D, DECAY_LORA), D ** -0.5),
        'rw_w2': nrm((NR, 2, DECAY_LORA, D), 0.5 * DECAY_LORA ** -0.5),
        'rw_a0': nrm((NR, 2, D), 0.1),
        'rw_a1': nrm((NR, 2, D, AAA_LORA), D ** -0.5),
        'rw_a2': nrm((NR, 2, AAA_LORA, D), 0.5 * AAA_LORA ** -0.5),
        'rw_g1': nrm((NR, D, GATE_LORA), D ** -0.5),
        'rw_g2': nrm((NR, GATE_LORA, D), GATE_LORA ** -0.5),
        'rw_k_k': 0.85 + nrm((NR, D), 0.02),
        'rw_k_a': 1.0 + nrm((NR, D), 0.02),
        'rw_r_k': nrm((NR, RW_HEADS, RW_HEAD), 0.1),
        'rw_ln_g': 1.0 + nrm((NR, D), 0.02),
        'rw_ln_b': nrm((NR, D), 0.02),
        'hy_w_in': nrm((NH, D, 3 * D), D ** -0.5),
        'hy_conv_w': nrm((NH, 3, 3 * D), 3 ** -0.5),
        'hy_conv_b': nrm((NH, 3 * D), 0.02),
        'hy_f_w1': nrm((NH, HY_EMB, HY_HID), HY_EMB ** -0.5),
        'hy_f_b1': nrm((NH, HY_HID), 0.1),
        'hy_f_freq': 1.0 + nrm((NH, 2, HY_HID), 0.02),
        'hy_f_w2': nrm((NH, HY_HID, HY_HID), HY_HID ** -0.5),
        'hy_f_b2': nrm((NH, HY_HID), 0.1),
        'hy_f_w3': nrm((NH, HY_HID, HY_N_FILT * D), 0.003),
        'hy_deltas': deltas,
        'hy_bias': nrm((NH, HY_N_FILT, D), 0.1),
        'hy_w_out': nrm((NH, D, D), D ** -0.5),
        'final_g': 1.0 + nrm((D,), 0.02),
    }


def reference(x, c, ctx, c_ctx, mod_w, mod_b, norm_g, ffn_w_gu, ffn_w_down,
              rw_mix, rw_w_rkv, rw_w_o, rw_w0, rw_w1, rw_w2, rw_a0, rw_a1, rw_a2, rw_g1, rw_g2,
              rw_k_k, rw_k_a, rw_r_k, rw_ln_g, rw_ln_b,
              hy_w_in, hy_conv_w, hy_conv_b, hy_f_w1, hy_f_b1, hy_f_freq, hy_f_w2, hy_f_b2, hy_f_w3,
              hy_deltas, hy_bias, hy_w_out, final_g):
    B = x.shape[0]
    rw = (rw_mix, rw_w_rkv, rw_w_o, rw_w0, rw_w1, rw_w2, rw_a0, rw_a1, rw_a2, rw_g1, rw_g2,
          rw_k_k, rw_k_a, rw_r_k, rw_ln_g, rw_ln_b)
    hy = (hy_w_in, hy_conv_w, hy_conv_b, hy_f_w1, hy_f_b1, hy_f_freq, hy_f_w2, hy_f_b2, hy_f_w3,
          hy_deltas, hy_bias, hy_w_out)
    last_rec = ((DEPTH - 1) // N_MIXERS) * N_MIXERS
    xl, xc = x, ctx
    for i in range(DEPTH):
        ctx_live, ctx_full = i <= last_rec, i < last_rec
        ml = [m[:, None, :] for m in jnp.split(jax.nn.silu(c) @ mod_w[i] + mod_b[i], 9, axis=-1)]
        mc = jnp.split(jax.nn.silu(c_ctx) @ mod_w[i] + mod_b[i], 9) if ctx_live else None
        xl = xl + 0.5 * ml[2] * _swiglu(_pre(xl, norm_g[i, 0], ml[0], ml[1]), ffn_w_gu[i, 0], ffn_w_down[i, 0])
        if ctx_live:
            xc = xc + 0.5 * mc[2] * _swiglu(_pre(xc, norm_g[i, 0], mc[0], mc[1]), ffn_w_gu[i, 0], ffn_w_down[i, 0])
        hl = _pre(xl, norm_g[i, 1], ml[3], ml[4])
        hc = _pre(xc, norm_g[i, 1], mc[3], mc[4]) if ctx_live else None
        j = i // N_MIXERS
        if i % N_MIXERS == 0:
            p = tuple(a[j] for a in rw)
            S0 = jnp.zeros((2, B, RW_HEADS, RW_HEAD, RW_HEAD), F32)
            yc, S_ctx = _rwkv7_sequence(hc, _seq_shift(hc), S0, p, ctx_full)
            yl, _ = _rwkv7_sequence(hl, _grid_shift(hl), S_ctx, p, True)
        else:
            p = tuple(a[j] for a in hy)
            yl = _hyena_sequence(hl, p)
            yc = _hyena_sequence(hc, p) if ctx_full else None
        xl = xl + ml[5] * yl
        if ctx_full:
            xc = xc + mc[5] * yc
            xc = xc + 0.5 * mc[8] * _swiglu(_pre(xc, norm_g[i, 2], mc[6], mc[7]), ffn_w_gu[i, 1], ffn_w_down[i, 1])
        xl = xl + 0.5 * ml[8] * _swiglu(_pre(xl, norm_g[i, 2], ml[6], ml[7]), ffn_w_gu[i, 1], ffn_w_down[i, 1])
    return _rmsnorm(xl, final_g)
```

```python
import math
from contextlib import ExitStack

import numpy as np
import concourse.bass as bass
import concourse.mybir as mybir
from concourse.bass_utils import run_bass_kernel_spmd

F32 = mybir.dt.float32
AF = mybir.ActivationFunctionType
ALU = mybir.AluOpType
AX = mybir.AxisListType

D = 1024
KC = D // 128
FF = 2816
FC = FF // 128
NCORES = 8
NORM_EPS = 1e-6
GN_EPS = 64e-5

ENGS = ("pe", "act", "dve", "pool", "sp")
EPOCH = 30000
DMA_EPOCH = 1800
ARENA_KB = 204


class Res:
    __slots__ = ("w", "rs", "name")

    def __init__(self, name=""):
        self.w = None
        self.rs = {}
        self.name = name


class Prog:
    def __init__(self, nc, n_dma=8):
        self.nc = nc
        self.streams = {e: [] for e in ENGS}
        self.cnt = {e: 0 for e in ENGS}
        self.known = {e: {} for e in ENGS}
        self.n_dma = n_dma
        self.dma_rr = {}
        self.dma_cnt = {}
        self.semkeys = {}
        self.stack = ExitStack()
        self.last_dma = {}
        self.sb_off = 0
        self.sb_id = 0
        self.arena = None
        self.n_cc = 0
        self._banks = None

    def sb(self, name, shape, dtype=F32):
        if self.arena is None:
            self.arena = self.stack.enter_context(self.nc.sbuf_tensor("arena", [128, ARENA_KB * 256], F32))
        n = 1
        for d in shape[1:]:
            n *= d
        n = (n + 7) // 8 * 8
        off = self.sb_off
        assert off + n <= ARENA_KB * 256, (name, off, n)
        self.sb_off = off + n
        v = self.arena[0:shape[0], off:off + n]
        dims = " ".join(f"d{i}" for i in range(1, len(shape)))
        if len(shape) > 2:
            kw = {f"d{i}": shape[i] for i in range(2, len(shape))}
            v = v[:, 0:int(np.prod(shape[1:]))].rearrange(f"p ({dims}) -> p {dims}", **kw)
        else:
            v = v[:, 0:shape[1]]
        return v

    def mark(self):
        return self.sb_off

    def release(self, m):
        self.sb_off = m

    def banks(self):
        if self._banks is None:
            self._banks = [self.stack.enter_context(self.nc.psum_tensor(f"pp_bank{i}", [128, 512], F32))
                           for i in range(8)]
        return self._banks

    def _key(self, key):
        if key not in self.semkeys:
            self.semkeys[key] = None
        return key

    def op(self, eng, fn, reads=(), writes=(), dma=False):
        if dma:
            eng = "sp"
        elif eng == "pool":
            eng = "dve"
        waits = {}
        known = self.known[eng]

        def need(dep):
            if dep is None:
                return
            key, val = dep
            if eng == "pe" and key[0] == "pe":
                return
            if known.get(key, 0) >= val:
                return
            if waits.get(key, 0) < val:
                waits[key] = val

        for r in reads:
            need(r.w)
        for w in writes:
            need(w.w)
            for k, v in w.rs.items():
                need((k, v))
        if dma:
            i = self.dma_rr.get(eng, 0)
            self.dma_rr[eng] = (i + 1) % self.n_dma
            n = self.dma_cnt.get((eng, i), 0)
            ep, idx = divmod(n, DMA_EPOCH)
            key = self._key(("dma", eng, i, ep))
            if idx > 0:
                need((key, idx * 16))
            elif ep > 0:
                need((("dma", eng, i, ep - 1), DMA_EPOCH * 16))
            self.dma_cnt[(eng, i)] = n + 1
            done = (key, (idx + 1) * 16)
            self.last_dma[(eng, i, ep)] = done
        else:
            n = self.cnt[eng]
            ep, idx = divmod(n, EPOCH)
            key = self._key((eng, ep))
            self.cnt[eng] = n + 1
            done = (key, idx + 1)
        for k, v in waits.items():
            known[k] = v
        self.streams[eng].append((list(waits.items()), fn, done, dma))
        for r in reads:
            r.rs[done[0]] = done[1]
        for w in writes:
            w.w = done
            w.rs = {}
        return done

    def cc(self, fn, reads=(), writes=()):
        eng = "pool"
        waits = {}
        known = self.known[eng]

        def need(dep):
            if dep is None:
                return
            key, val = dep
            if known.get(key, 0) >= val:
                return
            if waits.get(key, 0) < val:
                waits[key] = val
        for r in reads:
            need(r.w)
        for w in writes:
            need(w.w)
            for k, v in w.rs.items():
                need((k, v))
        if self.n_cc > 0:
            need((("cc", self.n_cc), 1))
        self.n_cc += 1
        key = self._key(("cc", self.n_cc))
        done = (key, 1)
        self.last_dma[("cc", self.n_cc)] = done
        for k, v in waits.items():
            known[k] = v
        self.streams[eng].append((list(waits.items()), fn, done, "cc"))
        for r in reads:
            r.rs[done[0]] = done[1]
        for w in writes:
            w.w = done
            w.rs = {}
        return done

    def finish(self):
        waits = [(k, v) for k, v in self.last_dma.values()]
        self.streams["sp"].append((waits, None, None, False))

    def emit(self):
        nc = self.nc
        sems = {}
        for key in self.semkeys:
            nm = "s_" + "_".join(str(x) for x in key)
            sems[key] = self.stack.enter_context(nc.semaphore(nm))
        streams = self.streams

        def run(name, e):
            for waits, fn, done, dma in streams[name]:
                for k, v in waits:
                    e.wait_ge(sems[k], v)
                if fn is None:
                    continue
                ins = fn(e)
                if dma == "cc":
                    ins.then_inc(sems[done[0]])
                else:
                    ins.then_inc(sems[done[0]], 16 if dma else 1)

        with nc.Block() as block:
            @block.tensor
            def _(e):
                run("pe", e)

            @block.scalar
            def _(e):
                run("act", e)

            @block.vector
            def _(e):
                run("dve", e)

            @block.gpsimd
            def _(e):
                run("pool", e)

            @block.sync
            def _(e):
                run("sp", e)
        self.stack.close()


class Ring:
    def __init__(self, tiles):
        self.tiles = tiles
        self.res = [Res() for _ in tiles]
        self.i = 0

    def next(self):
        i = self.i
        self.i = (i + 1) % len(self.tiles)
        return self.tiles[i], self.res[i]


def fm(a):
    return np.ascontiguousarray(a.T)


def wlay(w):
    K, N = w.shape
    kc, oc = K // 128, N // 128
    return np.ascontiguousarray(w.reshape(kc, 128, oc, 128).transpose(2, 1, 0, 3))


def wgulay(w):
    a = wlay(w)
    return np.ascontiguousarray(np.stack([a[:FC], a[FC:]], axis=2))


def pvec(v):
    return np.ascontiguousarray(v.reshape(-1, 128).T)


class RowCtx:
    def __init__(self, P, TB):
        self.P = P
        self.TB = TB
        nc = P.nc
        self.ones = P.sb("ones", [128, 128])
        self.r_ones = Res()
        P.op("pool", lambda e: e.memset(self.ones[:], 1.0), writes=[self.r_ones])
        self.psum = Ring(P.banks())
        self.wgu = Ring([P.sb(f"wgu{i}", [128, 2, KC, 128]) for i in range(3)])
        self.wdn = Ring([P.sb(f"wdn{i}", [128, FC, 128]) for i in range(2)])
        self.wsq = Ring([P.sb(f"wsq{i}", [128, KC, 128]) for i in range(3)])
        self.dq = 0

    def dma_eng(self):
        self.dq += 1
        return "sp" if self.dq % 2 else "pool"


def load_vecs(P, dram_ap, n, name):
    t = P.sb(name, [128, n])
    r = Res(name)
    P.op("sp", lambda e: e.dma_start(out=t[:], in_=dram_ap), writes=[r], dma=True)
    return t, r


def emit_mod(C, modw_d, modb_d, sc_t, sc_r, ncol, name):
    P = C.P
    mod = P.sb(name, [128, 72, ncol])
    modb, r_modb = load_vecs(P, modb_d, 72, name + "_b")
    r_mod = Res(name)
    for j in range(72):
        wt, wr = C.wsq.next()
        P.op(C.dma_eng(), lambda e, wt=wt, j=j: e.dma_start(out=wt[:], in_=modw_d[j]), writes=[wr], dma=True)
        pt, pr = C.psum.next()
        for c in range(KC):
            P.op("pe", lambda e, pt=pt, wt=wt, c=c: e.matmul(pt[:, 0:ncol], wt[:, c, :], sc_t[:, c, :],
                                                             start=(c == 0), stop=(c == KC - 1)),
                 reads=[wr, sc_r], writes=[pr])
        P.op("dve", lambda e, pt=pt, j=j: e.tensor_scalar(mod[:, j, :], pt[:, 0:ncol], modb[:, j:j + 1], None,
                                                          ALU.add),
             reads=[pr, r_modb], writes=[r_mod])
    return mod, r_mod


def emit_modvecs(C, mod, r_mod, col, normg, r_normg, name):
    P = C.P
    gs = P.sb(name + "_gs", [128, 3, KC])
    hg = P.sb(name + "_hg", [128, 3, KC])
    r = Res(name)
    for k in range(3):
        sc = mod[:, (3 * k + 1) * KC:(3 * k + 2) * KC, col]
        gt = mod[:, (3 * k + 2) * KC:(3 * k + 3) * KC, col]
        P.op("dve", lambda e, k=k, sc=sc: e.scalar_tensor_tensor(gs[:, k, :], sc, 1.0, normg[:, k * KC:(k + 1) * KC],
                                                                 ALU.add, ALU.mult),
             reads=[r_mod, r_normg], writes=[r])
        P.op("dve", lambda e, k=k, gt=gt: e.tensor_scalar(hg[:, k, :], gt, 1.0 if k == 1 else 0.5, None, ALU.mult),
             reads=[r_mod], writes=[r])
    return gs, hg, r


def emit_rstd(C, X, r_X, SQ, r_SQ, R, r_R, nt):
    P = C.P
    pt, pr = C.psum.next()
    for c in range(KC):
        P.op("act" if c % 2 else "dve",
             (lambda e, c=c: e.activation(SQ[:, c, :nt], X[:, c, :nt], AF.Square)) if c % 2 else
             (lambda e, c=c: e.tensor_tensor(SQ[:, c, :nt], X[:, c, :nt], X[:, c, :nt], ALU.mult)),
             reads=[r_X], writes=[r_SQ[c]])
    for c in range(KC):
        P.op("pe", lambda e, c=c: e.matmul(pt[:, :nt], C.ones[:], SQ[:, c, :nt], start=(c == 0), stop=(c == KC - 1)),
             reads=[C.r_ones, r_SQ[c]], writes=[pr])
    P.op("act", lambda e: e.activation(R[:, :nt], pt[:, :nt], AF.Sqrt, bias=C.epsb[:, 0:1], scale=1.0 / D),
         reads=[pr, C.r_epsb], writes=[r_R])
    P.op("dve", lambda e: e.reciprocal(R[:, :nt], R[:, :nt]), reads=[r_R], writes=[r_R])


def emit_pre(C, X, r_X, XN, r_XN, R, r_R, gs, sh, r_vec, nt):
    P = C.P
    for c in range(KC):
        P.op("dve", lambda e, c=c: e.scalar_tensor_tensor(XN[:, c, :nt], X[:, c, :nt], gs[:, c:c + 1], R[:, :nt],
                                                          ALU.mult, ALU.mult),
             reads=[r_X, r_R, r_vec], writes=[r_XN[c]])
        P.op("act", lambda e, c=c: e.activation(XN[:, c, :nt], XN[:, c, :nt], AF.Identity, bias=sh[:, c:c + 1]),
             reads=[r_XN[c], r_vec], writes=[r_XN[c]])


def emit_ffn(C, XN, r_XN, H, r_H, X, r_X, hg, r_vec, wgu_d, wdn_d, nt, wres=()):
    P = C.P
    for fo in range(FC):
        wt, wr = C.wgu.next()
        P.op(C.dma_eng(), lambda e, wt=wt, fo=fo: e.dma_start(out=wt[:], in_=wgu_d[fo]), reads=list(wres), writes=[wr], dma=True)
        pg, prg = C.psum.next()
        pu, pru = C.psum.next()
        for c in range(KC):
            P.op("pe", lambda e, pg=pg, wt=wt, c=c: e.matmul(pg[:, :nt], wt[:, 0, c, :], XN[:, c, :nt],
                                                             start=(c == 0), stop=(c == KC - 1)),
                 reads=[wr, r_XN[c]], writes=[prg])
        for c in range(KC):
            P.op("pe", lambda e, pu=pu, wt=wt, c=c: e.matmul(pu[:, :nt], wt[:, 1, c, :], XN[:, c, :nt],
                                                             start=(c == 0), stop=(c == KC - 1)),
                 reads=[wr, r_XN[c]], writes=[pru])
        P.op("act", lambda e, pg=pg, fo=fo: e.activation(H[:, fo, :nt], pg[:, :nt], AF.Silu),
             reads=[prg], writes=[r_H[fo]])
        P.op("dve", lambda e, pu=pu, fo=fo: e.tensor_tensor(H[:, fo, :nt], H[:, fo, :nt], pu[:, :nt], ALU.mult),
             reads=[pru, r_H[fo]], writes=[r_H[fo]])
    for do in range(KC):
        wt, wr = C.wdn.next()
        P.op(C.dma_eng(), lambda e, wt=wt, do=do: e.dma_start(out=wt[:], in_=wdn_d[do]), reads=list(wres), writes=[wr], dma=True)
        po, pro = C.psum.next()
        for f in range(FC):
            P.op("pe", lambda e, po=po, wt=wt, f=f: e.matmul(po[:, :nt], wt[:, f, :], H[:, f, :nt],
                                                             start=(f == 0), stop=(f == FC - 1)),
                 reads=[wr, r_H[f]], writes=[pro])
        P.op("dve", lambda e, po=po, do=do: e.scalar_tensor_tensor(X[:, do, :nt], po[:, :nt], hg[:, do:do + 1],
                                                                   X[:, do, :nt], ALU.mult, ALU.add),
             reads=[pro, r_vec, r_X], writes=[r_X])


def emit_proj(C, XIN, r_XIN, w_d, n_out, kc, evac, nt, wres=()):
    P = C.P
    for o in range(n_out):
        wt, wr = C.wsq.next()
        P.op(C.dma_eng(), lambda e, wt=wt, o=o: e.dma_start(out=wt[:, :kc, :], in_=w_d[o]), reads=list(wres), writes=[wr], dma=True)
        pt, pr = C.psum.next()
        for c in range(kc):
            P.op("pe", lambda e, pt=pt, wt=wt, c=c: e.matmul(pt[:, :nt], wt[:, c, :], XIN[:, c, :nt],
                                                             start=(c == 0), stop=(c == kc - 1)),
                 reads=[wr, r_XIN[c]], writes=[pr])
        evac(o, pt, pr)


def row_common(P, TB):
    C = RowCtx(P, TB)
    C.epsb = P.sb("epsb", [128, 1])
    C.r_epsb = Res()
    P.op("pool", lambda e: e.memset(C.epsb[:], NORM_EPS), writes=[C.r_epsb])
    C.X = P.sb("X", [128, KC, TB]); C.r_X = Res("X")
    C.XN = P.sb("XN", [128, KC, TB]); C.r_XN = [Res() for _ in range(KC)]
    C.SQ = P.sb("SQ", [128, KC, TB]); C.r_SQ = [Res() for _ in range(KC)]
    C.H = P.sb("H", [128, FC, TB]); C.r_H = [Res() for _ in range(FC)]
    C.R = P.sb("R", [128, TB]); C.r_R = Res("R")
    return C


def emit_silu_c(P, C, cvec_d, ncol):
    t = P.sb("sc", [128, KC, ncol])
    r = Res("sc")
    P.op("sp", lambda e: e.dma_start(out=t[:], in_=cvec_d), writes=[r], dma=True)
    P.op("act", lambda e: e.activation(t[:], t[:], AF.Silu), reads=[r], writes=[r])
    return t, r


def build_l1(NT, NCTX, TB=512):
    nc = bass.Bass("TRN2", target_bir_lowering=False)
    xT = nc.dram_tensor("xT", [D, NT], F32, kind="ExternalInput").ap()
    cxT = nc.dram_tensor("cxT", [D, NCTX], F32, kind="ExternalInput").ap()
    cvec = nc.dram_tensor("cvec", [128, KC, 2], F32, kind="ExternalInput").ap()
    modw = nc.dram_tensor("modw", [72, 128, KC, 128], F32, kind="ExternalInput").ap()
    modb = nc.dram_tensor("modb", [128, 72], F32, kind="ExternalInput").ap()
    normg = nc.dram_tensor("normg", [128, 3 * KC], F32, kind="ExternalInput").ap()
    wgu = nc.dram_tensor("wgu", [FC, 128, 2, KC, 128], F32, kind="ExternalInput").ap()
    wdn = nc.dram_tensor("wdn", [KC, 128, FC, 128], F32, kind="ExternalInput").ap()
    xo = nc.dram_tensor("xo", [D, NT], F32, kind="ExternalOutput").ap()
    ho = nc.dram_tensor("ho", [D, NT], F32, kind="ExternalOutput").ap()
    hco = nc.dram_tensor("hco", [D, NCTX], F32, kind="ExternalOutput").ap()
    modo = nc.dram_tensor("modo", [128, 72, 2], F32, kind="ExternalOutput").ap()

    P = Prog(nc)
    C = row_common(P, TB)
    sc, r_sc = emit_silu_c(P, C, cvec, 2)
    ng, r_ng = load_vecs(P, normg, 3 * KC, "normg")
    mod, r_mod = emit_mod(C, modw, modb, sc, r_sc, 2, "mod")
    P.op("sp", lambda e: e.dma_start(out=modo, in_=mod[:]), reads=[r_mod], dma=True)
    vecs = [emit_modvecs(C, mod, r_mod, col, ng, r_ng, f"mv{col}") for col in range(2)]

    def block(src, t0, nt, col, xdst, hdst):
        gs, hg, r_vec = vecs[col]
        X, r_X = C.X, C.r_X
        xs = src.rearrange("(c p) t -> p c t", p=128)
        P.op("sp", lambda e: e.dma_start(out=X[:, :, :nt], in_=xs[:, :, t0:t0 + nt]), writes=[r_X], dma=True)
        sh = lambda k: mod[:, (3 * k) * KC:(3 * k + 1) * KC, col]
        emit_rstd(C, X, r_X, C.SQ, C.r_SQ, C.R, C.r_R, nt)
        emit_pre(C, X, r_X, C.XN, C.r_XN, C.R, C.r_R, gs[:, 0, :], sh(0), r_vec, nt)
        emit_ffn(C, C.XN, C.r_XN, C.H, C.r_H, X, r_X, hg[:, 0, :], r_vec, wgu, wdn, nt)
        if xdst is not None:
            xd = xdst.rearrange("(c p) t -> p c t", p=128)
            P.op("sp", lambda e: e.dma_start(out=xd[:, :, t0:t0 + nt], in_=X[:, :, :nt]), reads=[r_X], dma=True)
        emit_rstd(C, X, r_X, C.SQ, C.r_SQ, C.R, C.r_R, nt)
        emit_pre(C, X, r_X, C.XN, C.r_XN, C.R, C.r_R, gs[:, 1, :], sh(1), r_vec, nt)
        hd = hdst.rearrange("(c p) t -> p c t", p=128)
        P.op("sp", lambda e: e.dma_start(out=hd[:, :, t0:t0 + nt], in_=C.XN[:, :, :nt]), reads=C.r_XN, dma=True)

    for t0 in range(0, NT, TB):
        block(xT, t0, min(TB, NT - t0), 0, xo, ho)
    for t0 in range(0, NCTX, TB):
        block(cxT, t0, min(TB, NCTX - t0), 1, None, hco)
    P.finish()
    P.emit()
    return nc


def const_masks():
    i = np.arange(128)
    same = (i[:, None] // 64) == (i[None, :] // 64)
    su = same & (i[:, None] < i[None, :])
    sl = same & (i[:, None] > i[None, :])
    iu = same & (i[:, None] <= i[None, :])
    il = same & (i[:, None] >= i[None, :])
    ident = i[:, None] == i[None, :]
    return np.ascontiguousarray(np.stack([su, sl, iu, il, same, ident], axis=1).astype(np.float32))


SU, SL, IU, IL, BO, IDENT = range(6)
HC_ = 4
DEC_C = -math.exp(-0.5)


class EW:
    def __init__(self):
        self.i = 0

    def __call__(self):
        self.i += 1
        return "dve" if self.i % 2 else "pool"


def rwkv_alloc(P, G):
    S = type("S", (), {})()
    S.G = G
    S.cm = P.sb("cm", [128, 6, 128]); S.r_cm = Res("cm")
    S.psum = Ring(P.banks())
    S.ew = EW()
    return S


def build_rwkv_prep(P, S, W, hsrc, nt_total, is_ctx, t_base, dr, T_lat):
    G = S.G
    ew = S.ew
    HALO = 64
    A = S.A
    def group(g0):
        nt = min(G, nt_total - g0)
        nch = nt // 64
        Hin = A["Hin"]
        lo = max(0, g0 - HALO); hi = min(nt_total, g0 + nt + HALO)
        P.op("pool", lambda e: e.memset(Hin[:], 0.0), writes=A["r_Hc"])
        for c in range(KC):
            rc = A["r_Hc"][c]
            for (poff, pn, pap) in hsrc(c, lo, hi - lo):
                d0 = lo - (g0 - HALO) + poff
                P.op("sp" if c % 2 else "pool",
                     lambda e, c=c, d0=d0, pn=pn, pap=pap: e.dma_start(out=Hin[:, c, d0:d0 + pn], in_=pap),
                     writes=[rc], dma=True)
        DF, r_DF = A["DF"], A["r_DF"]
        for c in range(KC):
            rc = A["r_Hc"][c]
            q = c // 2
            if is_ctx:
                off = -1 if q % 2 == 0 else 1
            else:
                off = (-1, 1, -64, 64)[q]
            hv = Hin[:, c, HALO:HALO + nt]
            sv = Hin[:, c, HALO + off:HALO + off + nt]
            P.op(ew(), lambda e, c=c, hv=hv, sv=sv: e.tensor_tensor(DF[:, c, :nt], sv, hv, ALU.subtract),
                 reads=[rc], writes=[r_DF[c]])
            if (not is_ctx) and q < 2:
                col = 0 if q == 0 else 63
                dv = DF[:, c, :nt].rearrange("p (j k) -> p j k", k=64)[:, :, col:col + 1]
                hv3 = hv.rearrange("p (j k) -> p j k", k=64)[:, :, col:col + 1]
                P.op(ew(), lambda e, dv=dv, hv3=hv3: e.tensor_scalar(dv, hv3, -1.0, None, ALU.mult),
                     reads=[rc, r_DF[c]], writes=[r_DF[c]])
        XM, r_XM = A["XM"], A["r_XM"]

        def mixj(j):
            for c in range(KC):
                P.op("dve", lambda e, c=c, j=j: e.scalar_tensor_tensor(XM[:, c, :nt], DF[:, c, :nt],
                                                                      W["mix"][:, j, c:c + 1],
                                                                      Hin[:, c, HALO:HALO + nt], ALU.mult, ALU.add),
                     reads=[r_DF[c], A["r_Hc"][c], W["r"]], writes=[r_XM[c]])

        def proj_fm(wt, n_out, evac, kparts=128, mcols=128):
            for o in range(n_out):
                pt, pr = S.psum.next()
                for c in range(KC):
                    P.op("pe", lambda e, pt=pt, o=o, c=c: e.matmul(pt[:mcols, :nt], wt[:, c, o * mcols:(o + 1) * mcols],
                                                                    XM[:, c, :nt], start=(c == 0), stop=(c == KC - 1)),
                         reads=[W["r"], r_XM[c]], writes=[pr])
                evac(o, pt, pr)

        Rf, Kf, Vf = A["Rf"], A["Kf"], A["Vf"]
        r_Rf, r_Kf, r_Vf = A["r_Rf"], A["r_Kf"], A["r_Vf"]
        mixj(0)
        proj_fm(W["wr"], HC_, lambda o, pt, pr: P.op("act", lambda e: e.activation(Rf[:, o, :nt], pt[:, :nt], AF.Copy),
                                                     reads=[pr], writes=[r_Rf[o]]))
        mixj(1)
        HW, r_HW = A["HW"], A["r_HW"]
        for n in range(2):
            pt, pr = S.psum.next()
            for c in range(KC):
                P.op("pe", lambda e, pt=pt, n=n, c=c: e.matmul(pt[:64, :nt], W["w1"][:, n, c, :], XM[:, c, :nt],
                                                               start=(c == 0), stop=(c == KC - 1)),
                     reads=[W["r"], r_XM[c]], writes=[pr])
            P.op("act", lambda e, pt=pt, n=n: e.activation(HW[:64, n, :nt], pt[:64, :nt], AF.Tanh),
                 reads=[pr], writes=[r_HW[n]])
        mixj(2)
        proj_fm(W["wk"], HC_, lambda o, pt, pr: P.op("act", lambda e: e.activation(Kf[:, o, :nt], pt[:, :nt], AF.Copy),
                                                     reads=[pr], writes=[r_Kf[o]]))
        mixj(3)
        proj_fm(W["wv"], HC_, lambda o, pt, pr: P.op("act", lambda e: e.activation(Vf[:, o, :nt], pt[:, :nt], AF.Copy),
                                                     reads=[pr], writes=[r_Vf[o]]))
        VT, r_VT = A["VT"], A["r_VT"]
        for tt in range(nt // 128):
            pt, pr = S.psum.next()
            for c in range(KC):
                P.op("pe", lambda e, pt=pt, tt=tt, c=c: e.matmul(pt[:, :], XM[:, c, tt * 128:(tt + 1) * 128],
                                                                 W["wv"][:, c, :], start=(c == 0), stop=(c == KC - 1)),
                     reads=[W["r"], r_XM[c]], writes=[pr])
            P.op("dve", lambda e, pt=pt, tt=tt: e.tensor_copy(VT[:, tt, :], pt[:, :]), reads=[pr], writes=[r_VT])
        vtd = dr["VT"]
        P.op("sp", lambda e, g0=g0, nt=nt: e.dma_start(
            out=vtd[t_base + g0:t_base + g0 + nt, :].rearrange("(j p) c -> p j c", p=128), in_=VT[:, :nt // 128, :]),
             reads=[r_VT], dma=True)
        mixj(4)
        HA, r_HA = A["HA"], A["r_HA"]
        for n in range(2):
            pt, pr = S.psum.next()
            for c in range(KC):
                P.op("pe", lambda e, pt=pt, n=n, c=c: e.matmul(pt[:64, :nt], W["a1"][:, n, c, :], XM[:, c, :nt],
                                                               start=(c == 0), stop=(c == KC - 1)),
                     reads=[W["r"], r_XM[c]], writes=[pr])
            P.op("act", lambda e, pt=pt, n=n: e.activation(HA[:64, n, :nt], pt[:64, :nt], AF.Copy),
                 reads=[pr], writes=[r_HA[n]])
        if not is_ctx:
            mixj(5)
            HG, r_HG = A["HG"], A["r_HG"]
            for part, (m0, m1) in enumerate(((0, 128), (128, 160))):
                pt, pr = S.psum.next()
                for c in range(KC):
                    P.op("pe", lambda e, pt=pt, c=c, m0=m0, m1=m1: e.matmul(pt[:m1 - m0, :nt], W["g1"][:, c, m0:m1],
                                                                             XM[:, c, :nt], start=(c == 0),
                                                                             stop=(c == KC - 1)),
                         reads=[W["r"], r_XM[c]], writes=[pr])
                P.op("act", lambda e, pt=pt, part=part, m0=m0, m1=m1: e.activation(HG[:m1 - m0, part, :nt],
                                                                                  pt[:m1 - m0, :nt], AF.Sigmoid),
                     reads=[pr], writes=[r_HG[part]])
            GG, r_GG = A["GG"], A["r_GG"]
            for o in range(HC_):
                pt, pr = S.psum.next()
                P.op("pe", lambda e, pt=pt, o=o: e.matmul(pt[:, :nt], W["g2a"][:, o * 128:(o + 1) * 128], HG[:, 0, :nt],
                                                          start=True, stop=False),
                     reads=[W["r"], r_HG[0]], writes=[pr])
                P.op("pe", lambda e, pt=pt, o=o: e.matmul(pt[:, :nt], W["g2b"][:32, o * 128:(o + 1) * 128],
                                                          HG[:32, 1, :nt], start=False, stop=True),
                     reads=[W["r"], r_HG[1]], writes=[pr])
                P.op("act", lambda e, pt=pt, o=o: e.activation(GG[:, o, :nt], pt[:, :nt], AF.Copy),
                     reads=[pr], writes=[r_GG])
            P.op("sp", lambda e, g0=g0, nt=nt: e.dma_start(out=dr["GG"][:, :, g0:g0 + nt], in_=GG[:, :, :nt]),
                 reads=[r_GG], dma=True)
        KK, r_KK = A["KK"], A["r_KK"]
        T1, r_T1 = A["T1"], A["r_T1"]
        for o in range(HC_):
            P.op(ew(), lambda e, o=o: e.tensor_scalar(KK[:, o, :nt], Kf[:, o, :nt], W["kk_"][:, o:o + 1], None, ALU.mult),
                 reads=[r_Kf[o], W["r"]], writes=[r_KK[o]])
            P.op(ew(), lambda e, o=o: e.tensor_tensor(T1[:, o, :nt], KK[:, o, :nt], KK[:, o, :nt], ALU.mult),
                 reads=[r_KK[o]], writes=[r_T1[o]])
            pt, pr = S.psum.next()
            P.op("pe", lambda e, pt=pt, o=o: e.matmul(pt[:, :nt], S.cm[:, BO, :], T1[:, o, :nt], start=True, stop=True),
                 reads=[S.r_cm, r_T1[o]], writes=[pr])
            P.op("act", lambda e, pt=pt, o=o: e.activation(T1[:, o, :nt], pt[:, :nt], AF.Sqrt),
                 reads=[pr, r_T1[o]], writes=[r_T1[o]])
            P.op("dve", lambda e, o=o: e.tensor_scalar(T1[:, o, :nt], T1[:, o, :nt], 1e-12, None, ALU.max),
                 reads=[r_T1[o]], writes=[r_T1[o]])
            P.op("dve", lambda e, o=o: e.reciprocal(T1[:, o, :nt], T1[:, o, :nt]), reads=[r_T1[o]], writes=[r_T1[o]])
            P.op(ew(), lambda e, o=o: e.tensor_tensor(KK[:, o, :nt], KK[:, o, :nt], T1[:, o, :nt], ALU.mult),
                 reads=[r_T1[o], r_KK[o]], writes=[r_KK[o]])
        AN, r_AN = A["AN"], A["r_AN"]
        KD, r_KD = A["KD"], A["r_KD"]
        LW, r_LW = A["LW"], A["r_LW"]
        def dirpart(n):
            for tt in range(nt // 128):
                pt, pr = S.psum.next()
                P.op("pe", lambda e, pt=pt, n=n, tt=tt: e.matmul(pt[:, :], HW[:64, n, tt * 128:(tt + 1) * 128],
                                                                 W["w2"][:64, n, :], start=True, stop=False),
                     reads=[W["r"], r_HW[n]], writes=[pr])
                P.op("pe", lambda e, pt=pt, n=n: e.matmul(pt[:, :], W["ones1"][0:1, :], W["w0"][0:1, n, :],
                                                          start=False, stop=True),
                     reads=[W["r"]], writes=[pr])
                P.op("act", lambda e, pt=pt, tt=tt: e.activation(LW[:, tt, :], pt[:, :], AF.Sigmoid),
                     reads=[pr], writes=[r_LW[tt]])
                P.op("dve", lambda e, tt=tt: e.tensor_scalar(LW[:, tt, :], LW[:, tt, :], DEC_C, None, ALU.mult),
                     reads=[r_LW[tt]], writes=[r_LW[tt]])
            tin, tex = (IU, SU) if n == 0 else (IL, SL)
            def chunkpart(o):
                pa, pra = S.psum.next()
                P.op("pe", lambda e, pa=pa, n=n, o=o: e.matmul(pa[:, :nt], W["a2"][:64, n, o * 128:(o + 1) * 128],
                                                               HA[:64, n, :nt], start=True, stop=True),
                     reads=[W["r"], r_HA[n]], writes=[pra])
                P.op("act", lambda e, pa=pa, n=n, o=o: e.activation(AN[:, o, :nt], pa[:, :nt], AF.Sigmoid,
                                                                    bias=W["a0"][:, n, o:o + 1]),
                     reads=[pra, W["r"]], writes=[r_AN[o]])
                pi, pri = S.psum.next()
                px, prx = S.psum.next()
                for tt in range(nt // 128):
                    P.op("pe", lambda e, pi=pi, tt=tt, o=o: e.matmul(pi[:, tt * 128:(tt + 1) * 128],
                                                                     LW[:, tt, o * 128:(o + 1) * 128], S.cm[:, tin, :],
                                                                     start=True, stop=True),
                         reads=[S.r_cm, r_LW[tt]], writes=[pri])
                    P.op("pe", lambda e, px=px, tt=tt, o=o: e.matmul(px[:, tt * 128:(tt + 1) * 128],
                                                                     LW[:, tt, o * 128:(o + 1) * 128], S.cm[:, tex, :],
                                                                     start=True, stop=True),
                         reads=[S.r_cm, r_LW[tt]], writes=[prx])
                (EI, EX, EV), r_E = A["Ering"].next()
                P.op("act", lambda e, pi=pi: e.activation(EI[:, :nt], pi[:, :nt], AF.Exp), reads=[pri], writes=[r_E[0]])
                P.op("act", lambda e, px=px: e.activation(EX[:, :nt], px[:, :nt], AF.Exp), reads=[prx], writes=[r_E[1]])
                P.op("act", lambda e, pi=pi: e.activation(EV[:, :nt], pi[:, :nt], AF.Exp, scale=-1.0),
                     reads=[pri], writes=[r_E[2]])
                O4, r_O4 = A["O4ring"].next()
                P.op(ew(), lambda e, o=o: e.tensor_tensor(O4[:, 0, :nt], KK[:, o, :nt], EX[:, :nt], ALU.mult),
                     reads=[r_KK[o], r_E[1]], writes=[r_O4[0]])
                P.op(ew(), lambda e, o=o: e.tensor_tensor(O4[:, 1, :nt], KK[:, o, :nt], AN[:, o, :nt], ALU.mult),
                     reads=[r_KK[o], r_AN[o]], writes=[r_O4[1]])
                P.op("dve", lambda e: e.scalar_tensor_tensor(O4[:, 1, :nt], O4[:, 1, :nt], -1.0, EV[:, :nt],
                                                            ALU.mult, ALU.mult),
                     reads=[r_O4[1], r_E[2]], writes=[r_O4[1]])
                P.op(ew(), lambda e, o=o, n=n: e.tensor_scalar(KD[:, n, o, :nt], AN[:, o, :nt], W["ka_"][:, o:o + 1],
                                                               W["oka_"][:, o:o + 1], ALU.mult, ALU.add),
                     reads=[r_AN[o], W["r"]], writes=[r_KD[n][o]])
                P.op(ew(), lambda e, o=o, n=n: e.tensor_tensor(KD[:, n, o, :nt], KD[:, n, o, :nt], Kf[:, o, :nt], ALU.mult),
                     reads=[r_KD[n][o], r_Kf[o]], writes=[r_KD[n][o]])
                P.op(ew(), lambda e, o=o, n=n: e.tensor_tensor(O4[:, 2, :nt], KD[:, n, o, :nt], EV[:, :nt], ALU.mult),
                     reads=[r_KD[n][o], r_E[2]], writes=[r_O4[2]])
                P.op(ew(), lambda e, o=o: e.tensor_tensor(O4[:, 3, :nt], Rf[:, o, :nt], EI[:, :nt], ALU.mult),
                     reads=[r_Rf[o], r_E[0]], writes=[r_O4[3]])
                WCt, r_WC = A["WC"], A["r_WC"]
                ecol = 63 if n == 0 else 0
                ev3 = EI[:, :nt].rearrange("p (j k) -> p j k", k=64)[:, :, ecol:ecol + 1]
                P.op(ew(), lambda e, ev3=ev3: e.tensor_copy(WCt[:, :nch].rearrange("p (j k) -> p j k", k=1), ev3),
                     reads=[r_E[0]], writes=[r_WC])
                c0 = (t_base + g0) // 64
                P.op("sp", lambda e, n=n, o=o, c0=c0, nch=nch: e.dma_start(out=dr["WC"][n, o, :, c0:c0 + nch],
                                                                           in_=WCt[:, :nch]),
                     reads=[r_WC], dma=True)
                for q in range(4):
                    P.op("sp" if q % 2 else "pool",
                         lambda e, n=n, o=o, q=q, g0=g0, nt=nt: e.dma_start(
                             out=dr["OPS"][n, q, o, :, t_base + g0:t_base + g0 + nt], in_=O4[:, q, :nt]),
                         reads=[r_O4[q]], dma=True)
            for o in range(HC_):
                chunkpart(o)
        for n in range(2):
            dirpart(n)
        if not is_ctx:
            BV, r_BV = A["BV"], A["r_BV"]
            for o in range(HC_):
                P.op(ew(), lambda e, o=o: e.tensor_tensor(T1[:, o, :nt], KD[:, 0, o, :nt], KD[:, 1, o, :nt], ALU.add),
                     reads=[r_KD[0][o], r_KD[1][o], r_T1[o]], writes=[r_T1[o]])
                P.op("dve", lambda e, o=o: e.scalar_tensor_tensor(T1[:, o, :nt], T1[:, o, :nt], W["hrk_"][:, o:o + 1],
                                                                 Rf[:, o, :nt], ALU.mult, ALU.mult),
                     reads=[r_T1[o], r_Rf[o], W["r"]], writes=[r_T1[o]])
                pt, pr = S.psum.next()
                P.op("pe", lambda e, pt=pt, o=o: e.matmul(pt[:, :nt], S.cm[:, BO, :], T1[:, o, :nt], start=True, stop=True),
                     reads=[S.r_cm, r_T1[o]], writes=[pr])
                P.op("dve", lambda e, pt=pt, o=o: e.tensor_tensor(BV[:, o, :nt], pt[:, :nt], Vf[:, o, :nt], ALU.mult),
                     reads=[pr, r_Vf[o]], writes=[r_BV])
            P.op("sp", lambda e, g0=g0, nt=nt: e.dma_start(out=dr["BV"][:, :, g0:g0 + nt], in_=BV[:, :, :nt]),
                 reads=[r_BV], dma=True)

    for g0 in range(0, nt_total, G):
        group(g0)


def barrier(P):
    targets = {}
    for e in ENGS:
        n = P.cnt[e]
        if n > 0:
            ep, idx = divmod(n - 1, EPOCH)
            targets[(e, ep)] = idx + 1
    for _k, (k, v) in P.last_dma.items():
        targets[k] = max(targets.get(k, 0), v)
    for e in ENGS:
        waits = []
        for k, v in targets.items():
            if e == "pe" and k[0] == "pe":
                continue
            if P.known[e].get(k, 0) < v:
                waits.append((k, v))
                P.known[e][k] = v
        P.streams[e].append((waits, None, None, False))


def mk_ring(tiles, nres):
    r = Ring(tiles)
    r.res = [[Res() for _ in range(nres)] for _ in tiles]
    return r


def rwkv_prep_alloc(P, S, G):
    A = {}
    A["Hin"] = P.sb("Hin", [128, KC, G + 128]); A["r_Hc"] = [Res() for _ in range(KC)]
    A["DF"] = P.sb("DF", [128, KC, G]); A["r_DF"] = [Res() for _ in range(KC)]
    A["XM"] = P.sb("XM", [128, KC, G]); A["r_XM"] = [Res() for _ in range(KC)]
    for nm in ("Rf", "Kf", "Vf", "KK", "T1", "AN"):
        A[nm] = P.sb(nm, [128, HC_, G]); A["r_" + nm] = [Res() for _ in range(HC_)]
    A["VT"] = P.sb("VT", [128, G // 128, 512]); A["r_VT"] = Res()
    A["HW"] = P.sb("HW", [64, 2, G]); A["r_HW"] = [Res(), Res()]
    A["HA"] = P.sb("HA", [64, 2, G]); A["r_HA"] = [Res(), Res()]
    A["HG"] = P.sb("HG", [128, 2, G]); A["r_HG"] = [Res(), Res()]
    A["GG"] = P.sb("GG", [128, HC_, G]); A["r_GG"] = Res()
    A["BV"] = P.sb("BV", [128, HC_, G]); A["r_BV"] = Res()
    A["KD"] = P.sb("KD", [128, 2, HC_, G]); A["r_KD"] = [[Res() for _ in range(HC_)] for _ in range(2)]
    A["LW"] = P.sb("LW", [128, G // 128, 512]); A["r_LW"] = [Res() for _ in range(G // 128)]
    A["Ering"] = mk_ring([tuple(P.sb(f"E{i}{j}", [128, G]) for j in range(3)) for i in range(2)], 3)
    A["O4ring"] = mk_ring([P.sb(f"O4{i}", [128, 4, G]) for i in range(2)], 4)
    A["WC"] = P.sb("WCt", [128, G // 64]); A["r_WC"] = Res()
    S.A = A


def rwkv_load_weights(P, S, wd):
    W = {"r": Res("W")}
    shapes = dict(mix=[128, 6, KC], wr=[128, KC, 512], wk=[128, KC, 512], wv=[128, KC, 512],
                  w1=[128, 2, KC, 64], a1=[128, 2, KC, 64], w2=[64, 2, 512], a2=[64, 2, 512],
                  w0=[1, 2, 512], a0=[128, 2, HC_], g1=[128, KC, 160], g2a=[128, 512], g2b=[32, 512],
                  kk_=[128, HC_], ka_=[128, HC_], rk=[128, HC_], lng=[128, HC_], lnb=[128, HC_])
    i = 0
    order = ["lng", "lnb"] + [k for k in shapes if k not in ("lng", "lnb")]
    for nm in order:
        shp = shapes[nm]
        t = P.sb("w_" + nm, shp)
        W[nm] = t
        i += 1
        P.op("sp" if i % 2 else "pool", lambda e, t=t, nm=nm: e.dma_start(out=t[:], in_=wd[nm]), writes=[Res()],
             dma=True)
        if nm == "lnb":
            S.m_persist = P.mark()
    P.op("sp", lambda e: e.dma_start(out=S.cm[:], in_=wd["cm"]), writes=[S.r_cm], dma=True)
    barrier(P)
    W["oka_"] = P.sb("w_oka", [128, HC_]); W["hrk_"] = P.sb("w_hrk", [128, HC_]); W["ones1"] = P.sb("w_ones1", [1, 128])
    P.op("dve", lambda e: e.tensor_scalar(W["oka_"][:], W["ka_"][:], -1.0, 1.0, ALU.mult, ALU.add), writes=[W["r"]])
    P.op("dve", lambda e: e.tensor_scalar(W["hrk_"][:], W["rk"][:], 0.5, None, ALU.mult), writes=[W["r"]])
    P.op("pool", lambda e: e.memset(W["ones1"][:], 1.0), writes=[W["r"]])
    return W


def build_rwkv_scan(P, S, dr, NC_CTX, NC_LAT):
    NCT = NC_CTX + NC_LAT
    CG = 2
    chains = [(n, o) for n in range(2) for o in range(HC_)]
    order = [list(range(NCT)),
             list(range(NC_CTX - 1, -1, -1)) + list(range(NCT - 1, NC_CTX - 1, -1))]
    cm = S.cm
    ps = Ring(P.banks())

    def ps_next():
        t, r = ps.next()
        return t[:, 0:128], r

    OPB = {}
    for ch in chains:
        tiles = [P.sb(f"opb{ch[0]}{ch[1]}{i}", [128, 4, CG, 128]) for i in range(2)]
        OPB[ch] = mk_ring(tiles, 1)
        for t in tiles:
            P.op("pool", lambda e, t=t: e.memset(t[:], 0.0), writes=[])
    VB = {}
    for ch in chains:
        tiles = [P.sb(f"vb{ch[0]}{ch[1]}{i}", [128, CG, 128]) for i in range(2)]
        VB[ch] = mk_ring(tiles, 1)
        for t in tiles:
            P.op("pool", lambda e, t=t: e.memset(t[:], 0.0), writes=[])
    WCB = {ch: mk_ring([P.sb(f"wcb{ch[0]}{ch[1]}{i}", [128, CG]) for i in range(2)], 1) for ch in chains}
    tmp = {}
    rtmp = {}
    for ch in chains:
        for nm in ["M0", "M1", "M2", "M3", "M4", "M5", "Na", "Nb", "BT", "PT", "NQT", "KHT", "NKAHT", "U", "ST"]:
            tmp[ch, nm] = P.sb(f"t{nm}{ch[0]}{ch[1]}", [128, 128])
            rtmp[ch, nm] = Res()
        P.op("pool", lambda e, t=tmp[ch, "ST"]: e.memset(t[:], 0.0), writes=[rtmp[ch, "ST"]])
    YT = {ch: mk_ring([P.sb(f"yt{ch[0]}{ch[1]}{i}", [128, 64]) for i in range(2)], 1) for ch in chains}
    barrier(P)
    cur = {}
    curv = {}
    dq = [0]

    def dma_q():
        dq[0] += 1
        return "sp" if dq[0] % 2 else "pool"

    def load_v(ch, lo, n):
        o = ch[1]
        vt, vr = VB[ch].next()
        vr = vr[0]
        for h in range(2):
            src = dr["VT"][lo * 64:(lo + n) * 64, o * 128 + h * 64:o * 128 + h * 64 + 64].rearrange(
                "(j s) v -> s j v", s=64)
            P.op(dma_q(), lambda e, vt=vt, h=h, src=src, n=n: e.dma_start(out=vt[h * 64:(h + 1) * 64, :n, h * 64:(h + 1) * 64],
                                                                        in_=src), writes=[vr], dma=True)
        return vt, vr

    for i in range(NCT):
        units = []
        for ch in chains:
            n, o = ch
            chunk = order[n][i]
            is_ctx = chunk < NC_CTX
            if is_ctx:
                base = 0; lim = NC_CTX
            else:
                base = NC_CTX; lim = NCT
            lo = base + ((chunk - base) // CG) * CG
            cnt = min(CG, lim - lo)
            if ch not in cur or cur[ch][0] != lo:
                ot, orr = OPB[ch].next(); orr = orr[0]
                wt, wr = WCB[ch].next(); wr = wr[0]
                for q in range(4):
                    for h in range(2):
                        src = dr["OPS"][n, q, o, h * 64:(h + 1) * 64, lo * 64:(lo + cnt) * 64].rearrange(
                            "p (j t) -> p j t", t=64)
                        P.op(dma_q(), lambda e, ot=ot, q=q, h=h, src=src, cnt=cnt: e.dma_start(
                            out=ot[h * 64:(h + 1) * 64, q, :cnt, h * 64:(h + 1) * 64], in_=src), writes=[orr], dma=True)
                P.op(dma_q(), lambda e, wt=wt, n=n, o=o, lo=lo, cnt=cnt: e.dma_start(out=wt[:, :cnt],
                                                                                 in_=dr["WC"][n, o, :, lo:lo + cnt]),
                     writes=[wr], dma=True)
                vt, vr = load_v(ch, lo, cnt)
                cur[ch] = (lo, ot, orr, wt, wr, vt, vr)
            lo, ot, orr, wt, wr, vt, vr = cur[ch]
            j = chunk - lo
            u = dict(ch=ch, n=n, o=o, chunk=chunk, is_ctx=is_ctx, j=j, orr=orr, vr=vr, wr=wr,
                     KKT=ot[:, 0, j, :], NKAH=ot[:, 1, j, :], KH=ot[:, 2, j, :], RT=ot[:, 3, j, :],
                     V=vt[:, j, :], WC=wt[:, j:j + 1])
            units.append(u)

        def T(u, nm):
            return tmp[u["ch"], nm]

        def R(u, nm):
            return rtmp[u["ch"], nm]

        def mm_evac(lhs_nm, rhs_nm, out_nm, mask=None, cond=lambda u: True):
            for u in units:
                if not cond(u):
                    continue
                pt, pr = ps_next()

                def opnd(nm):
                    if nm in ("KKT", "NKAH", "KH", "RT"):
                        return u[nm], u["orr"]
                    if nm == "I":
                        return cm[:, IDENT, :], S.r_cm
                    return T(u, nm)[:], R(u, nm)
                la, lr = opnd(lhs_nm)
                ra, rr = opnd(rhs_nm)
                P.op("pe", lambda e, pt=pt, la=la, ra=ra: e.matmul(pt, la, ra, start=True, stop=True),
                     reads=[lr, rr], writes=[pr])
                out = T(u, out_nm)
                if mask is None:
                    P.op("act", lambda e, pt=pt, out=out: e.activation(out[:], pt, AF.Copy),
                         reads=[pr], writes=[R(u, out_nm)])
                else:
                    mk = mask(u)
                    P.op("dve", lambda e, pt=pt, out=out, mk=mk: e.tensor_tensor(out[:], pt, cm[:, mk, :], ALU.mult),
                         reads=[pr, S.r_cm], writes=[R(u, out_nm)])

        ms = lambda u: SU if u["n"] == 0 else SL
        mst = lambda u: SL if u["n"] == 0 else SU
        mi = lambda u: IU if u["n"] == 0 else IL
        lat = lambda u: not u["is_ctx"]
        mm_evac("NKAH", "KKT", "M0", ms)
        mm_evac("KKT", "NKAH", "Na", mst)
        mm_evac("KH", "KKT", "BT", ms)
        mm_evac("KH", "RT", "PT", mi, lat)
        mm_evac("NKAH", "RT", "NQT", mi, lat)
        mm_evac("KH", "I", "KHT")
        mm_evac("NKAH", "I", "NKAHT")
        ncur, nnxt = "Na", "Nb"
        for lvl in range(5):
            mm_evac(ncur, f"M{lvl}", f"M{lvl + 1}")
            if lvl < 4:
                mm_evac(f"M{lvl}", ncur, nnxt)
                ncur, nnxt = nnxt, ncur
        xs = []
        for u in units:
            pt, pr = ps_next()
            P.op("pe", lambda e, pt=pt, u=u: e.matmul(pt, u["KKT"], T(u, "ST")[:], start=True, stop=False),
                 reads=[u["orr"], R(u, "ST")], writes=[pr])
            P.op("pe", lambda e, pt=pt, u=u: e.matmul(pt, T(u, "BT")[:], u["V"], start=False, stop=True),
                 reads=[R(u, "BT"), u["vr"]], writes=[pr])
            P.op("act", lambda e, pt=pt, u=u: e.activation(T(u, "U")[:], pt, AF.Copy), reads=[pr], writes=[R(u, "U")])
        for lvl in range(6):
            for u in units:
                pt, pr = ps_next()
                P.op("pe", lambda e, pt=pt, u=u, lvl=lvl: e.matmul(pt, T(u, f"M{lvl}")[:], T(u, "U")[:], start=True, stop=True),
                     reads=[R(u, f"M{lvl}"), R(u, "U")], writes=[pr])
                P.op("dve", lambda e, pt=pt, u=u: e.tensor_tensor(T(u, "U")[:], T(u, "U")[:], pt, ALU.add),
                     reads=[pr, R(u, "U")], writes=[R(u, "U")])
        for u in units:
            if u["is_ctx"]:
                continue
            pt, pr = ps_next()
            P.op("pe", lambda e, pt=pt, u=u: e.matmul(pt, T(u, "ST")[:], u["RT"], start=True, stop=False),
                 reads=[u["orr"], R(u, "ST")], writes=[pr])
            P.op("pe", lambda e, pt=pt, u=u: e.matmul(pt, u["V"], T(u, "PT")[:], start=False, stop=False),
                 reads=[u["vr"], R(u, "PT")], writes=[pr])
            P.op("pe", lambda e, pt=pt, u=u: e.matmul(pt, T(u, "U")[:], T(u, "NQT")[:], start=False, stop=True),
                 reads=[R(u, "U"), R(u, "NQT")], writes=[pr])
            yt, yr = YT[u["ch"]].next(); yr = yr[0]
            for h in range(2):
                P.op("act", lambda e, pt=pt, yt=yt, h=h: e.activation(yt[h * 64:(h + 1) * 64, :],
                                                                     pt[h * 64:(h + 1) * 64, h * 64:(h + 1) * 64], AF.Copy),
                     reads=[pr], writes=[yr])
            t0 = (u["chunk"] - NC_CTX) * 64
            P.op(dma_q(), lambda e, yt=yt, u=u, t0=t0: e.dma_start(out=dr["Y"][u["n"], u["o"], :, t0:t0 + 64], in_=yt[:]),
                 reads=[yr], dma=True)
        for u in units:
            pt, pr = ps_next()
            P.op("pe", lambda e, pt=pt, u=u: e.matmul(pt, T(u, "KHT")[:], u["V"], start=True, stop=False),
                 reads=[R(u, "KHT"), u["vr"]], writes=[pr])
            P.op("pe", lambda e, pt=pt, u=u: e.matmul(pt, T(u, "NKAHT")[:], T(u, "U")[:], start=False, stop=True),
                 reads=[R(u, "NKAHT"), R(u, "U")], writes=[pr])
            P.op("dve", lambda e, pt=pt, u=u: e.tensor_tensor(T(u, "ST")[:], T(u, "ST")[:], pt, ALU.add),
                 reads=[pr, R(u, "ST")], writes=[R(u, "ST")])
            P.op("dve", lambda e, u=u: e.tensor_scalar(T(u, "ST")[:], T(u, "ST")[:], u["WC"], None, ALU.mult),
                 reads=[R(u, "ST"), u["wr"]], writes=[R(u, "ST")])


def build_rwkv_post(P, S, W, dr, T_lat, ygdst, G=512):
    bufs = mk_ring([tuple(P.sb(f"pb{i}{j}", [128, G]) for j in range(5)) for i in range(2)], 5)
    epsg = P.sb("epsg", [128, 1]); r_eps = Res()
    P.op("pool", lambda e: e.memset(epsg[:], GN_EPS), writes=[r_eps])
    def one(g0, nt, o):
        if True:
            (Y0, Y1, Bv, Gg, Tm), rr = bufs.next()
            srcs = [dr["Y"][0, o, :, g0:g0 + nt], dr["Y"][1, o, :, g0:g0 + nt], dr["BV"][:, o, g0:g0 + nt],
                    dr["GG"][:, o, g0:g0 + nt]]
            for k, (t, s_) in enumerate(zip((Y0, Y1, Bv, Gg), srcs)):
                P.op("sp" if k % 2 else "pool", lambda e, t=t, s_=s_: e.dma_start(out=t[:, :nt], in_=s_),
                     writes=[rr[k]], dma=True)
            P.op("dve", lambda e: e.tensor_tensor(Y0[:, :nt], Y0[:, :nt], Y1[:, :nt], ALU.add),
                 reads=[rr[0], rr[1]], writes=[rr[0]])
            pm, prm = S.psum.next()
            P.op("pe", lambda e, pm=pm: e.matmul(pm[:, :nt], S.cm[:, BO, :], Y0[:, :nt], start=True, stop=True),
                 reads=[S.r_cm, rr[0]], writes=[prm])
            P.op("dve", lambda e, pm=pm: e.scalar_tensor_tensor(Y0[:, :nt], pm[:, :nt], -1.0 / 64, Y0[:, :nt],
                                                                ALU.mult, ALU.add),
                 reads=[prm, rr[0]], writes=[rr[0]])
            P.op("pool", lambda e: e.tensor_tensor(Tm[:, :nt], Y0[:, :nt], Y0[:, :nt], ALU.mult),
                 reads=[rr[0]], writes=[rr[4]])
            pv, prv = S.psum.next()
            P.op("pe", lambda e, pv=pv: e.matmul(pv[:, :nt], S.cm[:, BO, :], Tm[:, :nt], start=True, stop=True),
                 reads=[S.r_cm, rr[4]], writes=[prv])
            P.op("act", lambda e, pv=pv: e.activation(Tm[:, :nt], pv[:, :nt], AF.Sqrt, bias=epsg[:, 0:1], scale=1.0 / 64),
                 reads=[prv, r_eps, rr[4]], writes=[rr[4]])
            P.op("dve", lambda e: e.reciprocal(Tm[:, :nt], Tm[:, :nt]), reads=[rr[4]], writes=[rr[4]])
            P.op("dve", lambda e, o=o: e.scalar_tensor_tensor(Y0[:, :nt], Y0[:, :nt], W["lng"][:, o:o + 1], Tm[:, :nt],
                                                              ALU.mult, ALU.mult),
                 reads=[rr[0], rr[4], W["r"]], writes=[rr[0]])
            P.op("dve", lambda e, o=o: e.scalar_tensor_tensor(Y0[:, :nt], Y0[:, :nt], W["lnb"][:, o:o + 1], Bv[:, :nt],
                                                              ALU.add, ALU.add),
                 reads=[rr[0], rr[2], W["r"]], writes=[rr[0]])
            P.op("pool", lambda e: e.tensor_tensor(Y0[:, :nt], Y0[:, :nt], Gg[:, :nt], ALU.mult),
                 reads=[rr[0], rr[3]], writes=[rr[0]])
            P.op("sp", lambda e, o=o, g0=g0, nt=nt: e.dma_start(out=ygdst(o, g0, nt), in_=Y0[:, :nt]),
                 reads=[rr[0]], dma=True)

    for g0 in range(0, T_lat, G):
        for o in range(HC_):
            one(g0, min(G, T_lat - g0), o)


def rwkv_host_weights(inp, half):
    c0 = 512 * half
    cs = slice(c0, c0 + 512)
    f = lambda a: np.ascontiguousarray(a.astype(np.float32))
    w_rkv = inp["rw_w_rkv"][0]
    lay = lambda w: f(w.reshape(KC, 128, -1).transpose(1, 0, 2))
    d = {}
    d["mix"] = f(inp["rw_mix"][0].reshape(6, KC, 128).transpose(2, 0, 1))
    d["wr"] = lay(w_rkv[0][:, cs]); d["wk"] = lay(w_rkv[1][:, cs]); d["wv"] = lay(w_rkv[2][:, cs])
    d["w1"] = f(inp["rw_w1"][0].reshape(2, KC, 128, 64).transpose(2, 0, 1, 3))
    d["a1"] = f(inp["rw_a1"][0].reshape(2, KC, 128, 64).transpose(2, 0, 1, 3))
    d["w2"] = f(inp["rw_w2"][0][:, :, cs].transpose(1, 0, 2))
    d["a2"] = f(inp["rw_a2"][0][:, :, cs].transpose(1, 0, 2))
    d["w0"] = f(inp["rw_w0"][0][:, cs][None])
    d["a0"] = f(inp["rw_a0"][0][:, cs].reshape(2, HC_, 128).transpose(2, 0, 1))
    d["g1"] = lay(inp["rw_g1"][0])
    d["g2a"] = f(inp["rw_g2"][0][:128, cs]); d["g2b"] = f(inp["rw_g2"][0][128:, cs])
    pv = lambda v: f(v[cs].reshape(HC_, 128).T)
    d["kk_"] = pv(inp["rw_k_k"][0]); d["ka_"] = pv(inp["rw_k_a"][0]); d["rk"] = pv(inp["rw_r_k"][0].reshape(-1))
    d["lng"] = pv(inp["rw_ln_g"][0]); d["lnb"] = pv(inp["rw_ln_b"][0])
    d["cm"] = const_masks()
    return d


RW_SHAPES = dict(mix=[128, 6, KC], wr=[128, KC, 512], wk=[128, KC, 512], wv=[128, KC, 512],
                 w1=[128, 2, KC, 64], a1=[128, 2, KC, 64], w2=[64, 2, 512], a2=[64, 2, 512],
                 w0=[1, 2, 512], a0=[128, 2, HC_], g1=[128, KC, 160], g2a=[128, 512], g2b=[32, 512],
                 kk_=[128, HC_], ka_=[128, HC_], rk=[128, HC_], lng=[128, HC_], lnb=[128, HC_], cm=[128, 6, 128])


def rwkv_phase(P, nc, wd, hl_src, hc_src, T_lat, T_ctx, ygdst, G=256):
    NTOT = T_ctx + T_lat
    dr = {}
    dr["OPS"] = nc.dram_tensor("rw_ops", [2, 4, HC_, 128, NTOT], F32).ap()
    dr["VT"] = nc.dram_tensor("rw_vt", [NTOT, 512], F32).ap()
    dr["WC"] = nc.dram_tensor("rw_wc", [2, HC_, 128, NTOT // 64], F32).ap()
    dr["GG"] = nc.dram_tensor("rw_gg", [128, HC_, T_lat], F32).ap()
    dr["BV"] = nc.dram_tensor("rw_bv", [128, HC_, T_lat], F32).ap()
    dr["Y"] = nc.dram_tensor("rw_y", [2, HC_, 128, T_lat], F32).ap()
    m0 = P.mark()
    S = rwkv_alloc(P, G)
    W = rwkv_load_weights(P, S, wd)
    m1 = P.mark()
    rwkv_prep_alloc(P, S, G)
    build_rwkv_prep(P, S, W, hc_src, T_ctx, True, 0, dr, T_lat)
    build_rwkv_prep(P, S, W, hl_src, T_lat, False, T_ctx, dr, T_lat)
    barrier(P)
    P.release(S.m_persist)
    build_rwkv_scan(P, S, dr, T_ctx // 64, T_lat // 64)
    barrier(P)
    P.release(S.m_persist)
    build_rwkv_post(P, S, W, dr, T_lat, ygdst)
    barrier(P)
    P.release(m0)


def build_rwkv_test(T_lat, T_ctx):
    nc = bass.Bass("TRN2", target_bir_lowering=False)
    hl = nc.dram_tensor("hl", [D, T_lat], F32, kind="ExternalInput").ap()
    hc = nc.dram_tensor("hc", [D, T_ctx], F32, kind="ExternalInput").ap()
    wd = {nm: nc.dram_tensor("rw_" + nm, shp, F32, kind="ExternalInput").ap() for nm, shp in RW_SHAPES.items()}
    yg = nc.dram_tensor("yg", [HC_, 128, T_lat], F32, kind="ExternalOutput").ap()
    P = Prog(nc)
    hl3 = hl.rearrange("(c p) t -> c p t", p=128)
    hc3 = hc.rearrange("(c p) t -> c p t", p=128)
    rwkv_phase(P, nc, wd, lambda c, t0, n: [(0, n, hl3[c, :, t0:t0 + n])], lambda c, t0, n: [(0, n, hc3[c, :, t0:t0 + n])],
               T_lat, T_ctx, lambda o, g0, nt: yg[o, :, g0:g0 + nt])
    P.finish()
    P.emit()
    return nc


LSEQ = 8192
NFFT = 2 * LSEQ
HY_BANDS = 16


def hyena_consts():
    f64 = np.float64
    n1 = np.arange(64, dtype=f64)[:, None]
    k = np.arange(128, dtype=f64)[None, :]
    n = np.arange(128, dtype=f64)[:, None]
    d = {}
    a = 2 * np.pi * n1 * k / 128
    d["FA"] = np.concatenate([np.cos(a), -np.sin(a)], axis=1)
    tw = 2 * np.pi * n * k / NFFT
    d["TWC"] = np.stack([np.cos(tw), np.cos(tw)], axis=1)
    d["TWS"] = np.stack([np.sin(tw), np.sin(tw)], axis=1)
    b = 2 * np.pi * n * k / 128
    d["FBC"] = np.cos(b); d["FBS"] = np.sin(b); d["FBNS"] = -np.sin(b)
    d["INV1"] = np.concatenate([np.cos(b), np.sin(b)], axis=1)
    d["INV2"] = np.concatenate([-np.sin(b), np.cos(b)], axis=1)
    c = 2 * np.pi * np.arange(128, dtype=f64)[:, None] * np.arange(64, dtype=f64)[None, :] / 128
    d["CNN"] = np.cos(c) / NFFT; d["NSNN"] = -np.sin(c) / NFFT
    f32 = np.float32
    L = LSEQ
    t = np.linspace(0.0, 1.0, L, dtype=f32)[:, None]
    ang = f32(2 * math.pi / L) * np.arange(L, dtype=f32)[:, None] * np.linspace(1e-4, HY_BANDS - 1, HY_BANDS, dtype=f32)[None]
    zf = np.concatenate([t, np.cos(ang), -np.sin(ang)], axis=-1).astype(f32)
    d["ZFT"] = zf.T
    d["TPOS"] = t.T
    return {k_: np.ascontiguousarray(v.astype(np.float32)) for k_, v in d.items()}


HYC_SHAPES = dict(FA=[64, 256], TWC=[128, 2, 128], TWS=[128, 2, 128], FBC=[128, 128], FBS=[128, 128],
                  FBNS=[128, 128], INV1=[128, 256], INV2=[128, 256], CNN=[128, 64], NSNN=[128, 64],
                  ZFT=[33, LSEQ], TPOS=[1, LSEQ])
HYW_SHAPES = dict(win=[128, KC, 1536], convw=[128, 3, 12], convb=[128, 12], fw1=[33, 64], fb1=[64, 1], ffreq=[64, 2],
                  fw2=[64, 64], fb2=[64, 1], fw3=[64, 1024], deltas=[128, 8], bias=[1, 1024])


def hyena_host_weights(inp, half):
    c0 = 512 * half
    f = lambda a: np.ascontiguousarray(np.asarray(a, dtype=np.float32))
    cols = np.concatenate([np.arange(c0, c0 + 512) + 1024 * i for i in range(3)])
    d = {}
    w_in = inp["hy_w_in"][0][:, cols]
    d["win"] = f(w_in.reshape(KC, 128, 1536).transpose(1, 0, 2))
    d["convw"] = f(inp["hy_conv_w"][0][:, cols].reshape(3, 12, 128).transpose(2, 0, 1))
    d["convb"] = f(inp["hy_conv_b"][0][cols].reshape(12, 128).T)
    d["fw1"] = f(inp["hy_f_w1"][0]); d["fb1"] = f(inp["hy_f_b1"][0][:, None])
    d["ffreq"] = f(inp["hy_f_freq"][0].T)
    d["fw2"] = f(inp["hy_f_w2"][0]); d["fb2"] = f(inp["hy_f_b2"][0][:, None])
    fcols = np.concatenate([np.arange(c0, c0 + 512) + 1024 * i for i in range(2)])
    d["fw3"] = f(inp["hy_f_w3"][0][:, fcols])
    d["deltas"] = f(inp["hy_deltas"][0][:, c0:c0 + 512].reshape(8, 128).T)
    d["bias"] = f(inp["hy_bias"][0][:, c0:c0 + 512].reshape(1, 1024))
    return d


def hyena_phase(P, nc, wd, cd, h_src, zdst):
    L = LSEQ
    m0 = P.mark()
    ew = EW()
    banks = Ring(P.banks())
    Ud = nc.dram_tensor("hy_ud", [12, 128, L + 2], F32).ap()
    UC = nc.dram_tensor("hy_uc", [12 * 128, L], F32).ap()
    FD = nc.dram_tensor("hy_fd", [8 * 128, L], F32).ap()
    Wt = {}
    rW = Res("hyW")
    i = 0
    for nm, shp in list(HYW_SHAPES.items()) + list(HYC_SHAPES.items()):
        if nm in ("ZFT", "TPOS", "bias"):
            continue
        t = P.sb("hy_" + nm, shp)
        Wt[nm] = t
        src = wd[nm] if nm in wd else cd[nm]
        i += 1
        P.op("sp" if i % 2 else "pool", lambda e, t=t, src=src: e.dma_start(out=t[:], in_=src), writes=[Res()], dma=True)
    biasr = P.sb("hy_biasr", [64, 1024])
    P.op("sp", lambda e: e.dma_start(out=biasr[:], in_=wd["bias"].partition_broadcast(64)), writes=[Res()], dma=True)
    negpi = P.sb("hy_negpi", [128, 1])
    P.op("pool", lambda e: e.memset(negpi[:], -math.pi), writes=[Res()])
    barrier(P)
    fbs = P.sb("hy_fbs", [64, 2])
    nabsd = P.sb("hy_nabsd", [128, 8])
    P.op("dve", lambda e: e.tensor_tensor(fbs[:, 0:1], Wt["ffreq"][:, 0:1], Wt["fb1"][:, 0:1], ALU.mult), writes=[rW])
    P.op("dve", lambda e: e.tensor_tensor(fbs[:, 1:2], Wt["ffreq"][:, 1:2], Wt["fb2"][:, 0:1], ALU.mult), writes=[rW])
    P.op("dve", lambda e: e.tensor_scalar(nabsd[:], Wt["deltas"][:], -1.0, None, ALU.mult), writes=[rW])
    P.op("dve", lambda e: e.tensor_tensor(nabsd[:], nabsd[:], Wt["deltas"][:], ALU.max), reads=[rW], writes=[rW])
    P.op("dve", lambda e: e.tensor_scalar(nabsd[:], nabsd[:], -1.0, None, ALU.mult), reads=[rW], writes=[rW])
    m1 = P.mark()
    TBK = 512
    XIN = P.sb("hy_xin", [128, KC, TBK]); r_XIN = [Res() for _ in range(KC)]
    UB = mk_ring([P.sb(f"hy_ub{i}", [128, 12, TBK]) for i in range(2)], 1)
    zt = P.sb("hy_zero", [128, 12, 1])
    P.op("pool", lambda e: e.memset(zt[:], 0.0), writes=[rW])
    P.op("sp", lambda e: e.dma_start(out=Ud[:, :, 0:1].rearrange("q p o -> p q o"), in_=zt[:], allow_slow_non_contiguous=True), reads=[rW], dma=True)
    P.op("sp", lambda e: e.dma_start(out=Ud[:, :, L + 1:L + 2].rearrange("q p o -> p q o"), in_=zt[:], allow_slow_non_contiguous=True), reads=[rW], dma=True)

    def d1_block(t0):
        for c in range(KC):
            P.op("sp" if c % 2 else "pool", lambda e, c=c: e.dma_start(out=XIN[:, c, :], in_=h_src(c, t0, TBK)),
                 writes=[r_XIN[c]], dma=True)
        ub, rub = UB.next(); rub = rub[0]
        for q in range(12):
            pt, pr = banks.next()
            for c in range(KC):
                P.op("pe", lambda e, pt=pt, q=q, c=c: e.matmul(pt[:, :], Wt["win"][:, c, q * 128:(q + 1) * 128], XIN[:, c, :],
                                                                start=(c == 0), stop=(c == KC - 1)),
                     reads=[r_XIN[c], rW], writes=[pr])
            P.op("act" if q % 2 else "dve",
                 (lambda e, pt=pt, q=q: e.activation(ub[:, q, :], pt[:, :], AF.Copy)) if q % 2 else
                 (lambda e, pt=pt, q=q: e.tensor_copy(ub[:, q, :], pt[:, :])), reads=[pr], writes=[rub])
        P.op("sp", lambda e: e.dma_start(out=Ud[:, :, 1 + t0:1 + t0 + TBK].rearrange("q p t -> p q t"), in_=ub[:]),
             reads=[rub], dma=True)
    for t0 in range(0, L, TBK):
        d1_block(t0)
    barrier(P)
    P.release(m1)
    CB = mk_ring([(P.sb(f"hy_cu{i}", [128, L + 2]), P.sb(f"hy_co{i}", [128, L])) for i in range(2)], 2)

    def conv_chunk(q):
        (cu, co), rr = CB.next()
        P.op("sp", lambda e: e.dma_start(out=cu[:], in_=Ud[q]), writes=[rr[0]], dma=True)
        cw = Wt["convw"]
        P.op("dve", lambda e: e.tensor_scalar(co[:], cu[:, 0:L], cw[:, 0, q:q + 1], Wt["convb"][:, q:q + 1], ALU.mult, ALU.add),
             reads=[rr[0], rW], writes=[rr[1]])
        P.op("dve", lambda e: e.scalar_tensor_tensor(co[:], cu[:, 1:L + 1], cw[:, 1, q:q + 1], co[:], ALU.mult, ALU.add),
             reads=[rr[0], rr[1]], writes=[rr[1]])
        P.op("dve", lambda e: e.scalar_tensor_tensor(co[:], cu[:, 2:L + 2], cw[:, 2, q:q + 1], co[:], ALU.mult, ALU.add),
             reads=[rr[0], rr[1]], writes=[rr[1]])
        P.op("pool", lambda e: e.dma_start(out=UC[q * 128:(q + 1) * 128, :], in_=co[:]), reads=[rr[1]], dma=True)
    for q in range(12):
        conv_chunk(q)
    barrier(P)
    P.release(m1)
    zft = P.sb("hy_zft", [33, TBK]); r_zft = Res()
    tps = P.sb("hy_tps", [128, TBK]); r_tps = Res()
    h1 = P.sb("hy_h1", [64, TBK]); r_h1 = Res()
    h2 = P.sb("hy_h2", [64, TBK]); r_h2 = Res()
    FB_ = mk_ring([(P.sb(f"hy_fw{i}", [128, TBK]), P.sb(f"hy_ff{i}", [128, TBK])) for i in range(2)], 2)
    TWO_PI = 2 * math.pi
    MAGIC = 12582912.0
    PI_LO = 3.1415925
    rr_ = P.sb("hy_rr", [64, TBK]); r_rr = Res()

    def sin_layer(pt, pr, dst, r_dst, k):
        P.op("dve", lambda e: e.tensor_scalar(dst[:], pt[:64, :], Wt["ffreq"][:, k:k + 1], fbs[:, k:k + 1], ALU.mult, ALU.add),
             reads=[pr, rW], writes=[r_dst])
        P.op("dve", lambda e: e.tensor_scalar(rr_[:], dst[:], 1.0 / TWO_PI, MAGIC, ALU.mult, ALU.add), reads=[r_dst], writes=[r_rr])
        P.op("dve", lambda e: e.tensor_scalar(rr_[:], rr_[:], MAGIC, None, ALU.subtract), reads=[r_rr], writes=[r_rr])
        P.op("dve", lambda e: e.scalar_tensor_tensor(dst[:], rr_[:], -TWO_PI, dst[:], ALU.mult, ALU.add), reads=[r_rr, r_dst], writes=[r_dst])
        P.op("dve", lambda e: e.tensor_scalar(dst[:], dst[:], PI_LO, -PI_LO, ALU.min, ALU.max), reads=[r_dst], writes=[r_dst])
        P.op("act", lambda e: e.activation(dst[:], dst[:], AF.Sin), reads=[r_dst], writes=[r_dst])

    def filt_block(t0):
        P.op("sp", lambda e: e.dma_start(out=zft[:], in_=cd["ZFT"][:, t0:t0 + TBK]), writes=[r_zft], dma=True)
        P.op("pool", lambda e: e.dma_start(out=tps[:], in_=cd["TPOS"][:, t0:t0 + TBK].partition_broadcast(128)),
             writes=[r_tps], dma=True)
        pt, pr = banks.next()
        P.op("pe", lambda e: e.matmul(pt[:64, :], Wt["fw1"][:, :], zft[:, :], start=True, stop=True), reads=[r_zft, rW], writes=[pr])
        sin_layer(pt, pr, h1, r_h1, 0)
        pt2, pr2 = banks.next()
        P.op("pe", lambda e: e.matmul(pt2[:64, :], Wt["fw2"][:, :], h1[:, :], start=True, stop=True), reads=[r_h1, rW], writes=[pr2])
        sin_layer(pt2, pr2, h2, r_h2, 1)
        for cc in range(8):
            (fw, ff), rr = FB_.next()
            pt3, pr3 = banks.next()
            P.op("pe", lambda e, cc=cc, pt3=pt3: e.matmul(pt3[:, :], Wt["fw3"][:, cc * 128:(cc + 1) * 128], h2[:, :], start=True, stop=True),
                 reads=[r_h2, rW], writes=[pr3])
            P.op("act", lambda e, cc=cc, fw=fw: e.activation(fw[:], tps[:], AF.Exp, scale=nabsd[:, cc:cc + 1]),
                 reads=[r_tps, rW], writes=[rr[0]])
            P.op("dve", lambda e, pt3=pt3, fw=fw, ff=ff: e.tensor_tensor(ff[:], pt3[:, :], fw[:], ALU.mult),
                 reads=[pr3, rr[0]], writes=[rr[1]])
            P.op("sp" if cc % 2 else "pool", lambda e, cc=cc, ff=ff: e.dma_start(out=FD[cc * 128:(cc + 1) * 128, t0:t0 + TBK], in_=ff[:]),
                 reads=[rr[1]], dma=True)
    for t0 in range(0, L, TBK):
        filt_block(t0)
    barrier(P)
    P.release(m1)
    CGR = 8
    IN_ = mk_ring([tuple(P.sb(f"hy_in{i}{j}", [64, CGR, 128]) for j in range(5)) for i in range(2)], 5)
    ZO = mk_ring([P.sb(f"hy_zo{i}", [64, CGR, 128]) for i in range(2)], 1)
    PA = P.sb("hy_pa", [128, 2, 2, 128]); r_PA = Res()
    Y1 = P.sb("hy_y1", [128, 2, 2, 128]); r_Y1 = Res()
    TA = P.sb("hy_ta", [128, 2, 128]); TBb = P.sb("hy_tb", [128, 2, 128]); r_T = [Res(), Res()]
    HT = [P.sb(f"hy_ht{f}", [128, 2, 2, 128]) for f in range(2)]; r_HT = [Res(), Res()]
    XT = P.sb("hy_xt", [128, 2, 2, 128]); r_XT = Res()
    ZZ = P.sb("hy_zz", [128, 2, 2, 128]); r_ZZ = Res()
    PC = P.sb("hy_pc", [128, 2, 2, 128]); r_PC = Res()
    TT = P.sb("hy_tt", [128, 2, 2, 128]); r_TT = Res()
    Z1 = P.sb("hy_z1", [64, 2, 128]); r_Z1 = Res()
    TE = P.sb("hy_te", [64, 2, 128]); r_TE = Res()

    def fwd_fft(src, r_src, dst, r_dst):
        pt, pr = banks.next()
        for s_ in range(2):
            P.op("pe", lambda e, s_=s_: e.matmul(pt[:, s_ * 256:(s_ + 1) * 256], src(s_), Wt["FA"][:, :], start=True, stop=True),
                 reads=[r_src, rW], writes=[pr])
        P.op("act", lambda e: e.activation(PA[:].rearrange("p s c k -> p (s c k)"), pt[:, :], AF.Copy), reads=[pr], writes=[r_PA])
        re = PA[:, :, 0, :]; im = PA[:, :, 1, :]
        c_, s2 = Wt["TWC"][:], Wt["TWS"][:]
        P.op(ew(), lambda e: e.tensor_tensor(TA[:], re, c_, ALU.mult), reads=[r_PA, rW], writes=[r_T[0]])
        P.op(ew(), lambda e: e.tensor_tensor(TBb[:], im, s2, ALU.mult), reads=[r_PA, rW], writes=[r_T[1]])
        P.op(ew(), lambda e: e.tensor_tensor(Y1[:, 0, :, :], TA[:], TBb[:], ALU.add), reads=r_T, writes=[r_Y1])
        P.op(ew(), lambda e: e.tensor_tensor(TA[:], im, c_, ALU.mult), reads=[r_PA, rW, r_Y1], writes=[r_T[0]])
        P.op(ew(), lambda e: e.tensor_tensor(TBb[:], re, s2, ALU.mult), reads=[r_PA, rW, r_Y1], writes=[r_T[1]])
        P.op(ew(), lambda e: e.tensor_tensor(Y1[:, 1, :, :], TA[:], TBb[:], ALU.subtract), reads=r_T, writes=[r_Y1])
        y_re = Y1[:, 0, :, :].rearrange("p s k -> p (s k)"); y_im = Y1[:, 1, :, :].rearrange("p s k -> p (s k)")
        pb, prb = banks.next()
        P.op("pe", lambda e: e.matmul(pb[:, 0:256], Wt["FBC"][:, :], y_re, start=True, stop=False), reads=[r_Y1, rW], writes=[prb])
        P.op("pe", lambda e: e.matmul(pb[:, 0:256], Wt["FBS"][:, :], y_im, start=False, stop=True), reads=[r_Y1, rW], writes=[prb])
        P.op("pe", lambda e: e.matmul(pb[:, 256:512], Wt["FBC"][:, :], y_im, start=True, stop=False), reads=[r_Y1, rW], writes=[prb])
        P.op("pe", lambda e: e.matmul(pb[:, 256:512], Wt["FBNS"][:, :], y_re, start=False, stop=True), reads=[r_Y1, rW], writes=[prb])
        P.op("act", lambda e: e.activation(dst[:].rearrange("p c s k -> p (c s k)"), pb[:, :], AF.Copy), reads=[prb], writes=[r_dst])

    def conv(src, r_src, f, out_psum_cb):
        fwd_fft(src, r_src, XT, r_XT)
        H = HT[f]
        xr, xi = XT[:, 0, :, :], XT[:, 1, :, :]
        hr, hi = H[:, 0, :, :], H[:, 1, :, :]
        P.op(ew(), lambda e: e.tensor_tensor(TA[:], xr, hr, ALU.mult), reads=[r_XT, r_HT[f]], writes=[r_T[0]])
        P.op(ew(), lambda e: e.tensor_tensor(TBb[:], xi, hi, ALU.mult), reads=[r_XT, r_HT[f]], writes=[r_T[1]])
        P.op(ew(), lambda e: e.tensor_tensor(ZZ[:, 0, :, :], TA[:], TBb[:], ALU.subtract), reads=r_T, writes=[r_ZZ])
        P.op(ew(), lambda e: e.tensor_tensor(TA[:], xr, hi, ALU.mult), reads=[r_XT, r_HT[f], r_ZZ], writes=[r_T[0]])
        P.op(ew(), lambda e: e.tensor_tensor(TBb[:], xi, hr, ALU.mult), reads=[r_XT, r_HT[f], r_ZZ], writes=[r_T[1]])
        P.op(ew(), lambda e: e.tensor_tensor(ZZ[:, 1, :, :], TA[:], TBb[:], ALU.add), reads=r_T, writes=[r_ZZ])
        pc, prc = banks.next()
        for s_ in range(2):
            P.op("pe", lambda e, s_=s_: e.matmul(pc[:, s_ * 256:(s_ + 1) * 256], ZZ[:, 0, s_, :], Wt["INV1"][:, :], start=True, stop=False),
                 reads=[r_ZZ, rW], writes=[prc])
            P.op("pe", lambda e, s_=s_: e.matmul(pc[:, s_ * 256:(s_ + 1) * 256], ZZ[:, 1, s_, :], Wt["INV2"][:, :], start=False, stop=True),
                 reads=[r_ZZ, rW], writes=[prc])
        P.op("act", lambda e: e.activation(PC[:].rearrange("p s c k -> p (s c k)"), pc[:, :], AF.Copy), reads=[prc], writes=[r_PC])
        re = PC[:, :, 0, :]; im = PC[:, :, 1, :]
        c_, s2 = Wt["TWC"][:], Wt["TWS"][:]
        P.op(ew(), lambda e: e.tensor_tensor(TA[:], re, c_, ALU.mult), reads=[r_PC, rW], writes=[r_T[0]])
        P.op(ew(), lambda e: e.tensor_tensor(TBb[:], im, s2, ALU.mult), reads=[r_PC, rW], writes=[r_T[1]])
        P.op(ew(), lambda e: e.tensor_tensor(TT[:, 0, :, :], TA[:], TBb[:], ALU.subtract), reads=r_T, writes=[r_TT])
        P.op(ew(), lambda e: e.tensor_tensor(TA[:], re, s2, ALU.mult), reads=[r_PC, rW, r_TT], writes=[r_T[0]])
        P.op(ew(), lambda e: e.tensor_tensor(TBb[:], im, c_, ALU.mult), reads=[r_PC, rW, r_TT], writes=[r_T[1]])
        P.op(ew(), lambda e: e.tensor_tensor(TT[:, 1, :, :], TA[:], TBb[:], ALU.add), reads=r_T, writes=[r_TT])
        pd, prd = banks.next()
        P.op("pe", lambda e: e.matmul(pd[:64, 0:256], Wt["CNN"][:, :], TT[:, 0, :, :].rearrange("p s k -> p (s k)"), start=True, stop=False),
             reads=[r_TT, rW], writes=[prd])
        P.op("pe", lambda e: e.matmul(pd[:64, 0:256], Wt["NSNN"][:, :], TT[:, 1, :, :].rearrange("p s k -> p (s k)"), start=False, stop=True),
             reads=[r_TT, rW], writes=[prd])
        out_psum_cb(pd, prd)

    def group(g):
        ch0 = g * CGR
        (F0, F1, X1, X2, Vv), rin = IN_.next()
        srcs = [FD[ch0:ch0 + CGR, :], FD[512 + ch0:512 + ch0 + CGR, :], UC[ch0:ch0 + CGR, :], UC[512 + ch0:512 + ch0 + CGR, :],
                UC[1024 + ch0:1024 + ch0 + CGR, :]]
        for k, (t, s_) in enumerate(zip((F0, F1, X1, X2, Vv), srcs)):
            P.op("sp" if k % 2 else "pool", lambda e, t=t, s_=s_: e.dma_start(out=t[:], in_=s_.rearrange("c (a b) -> a c b", b=128)),
                 writes=[rin[k]], dma=True)
        zo, rzo = ZO.next(); rzo = rzo[0]

        def pair(pi):
            c_ = pi * 2
            fwd_fft(lambda s_: F0[:, c_ + s_, :], rin[0], HT[0], r_HT[0])
            fwd_fft(lambda s_: F1[:, c_ + s_, :], rin[1], HT[1], r_HT[1])

            def ep1(pd, prd):
                for s_ in range(2):
                    col = ch0 + c_ + s_
                    P.op("dve", lambda e, s_=s_, col=col: e.scalar_tensor_tensor(TE[:, s_, :], Vv[:, c_ + s_, :], biasr[:, col:col + 1],
                                                                                 pd[:64, s_ * 128:(s_ + 1) * 128], ALU.mult, ALU.add),
                         reads=[rin[4], prd], writes=[r_TE])
                P.op("pool", lambda e: e.tensor_tensor(Z1[:], TE[:], X1[:, c_:c_ + 2, :], ALU.mult), reads=[r_TE, rin[2]], writes=[r_Z1])
            conv(lambda s_: Vv[:, c_ + s_, :], rin[4], 0, ep1)

            def ep2(pd, prd):
                for s_ in range(2):
                    col = 512 + ch0 + c_ + s_
                    P.op("dve", lambda e, s_=s_, col=col: e.scalar_tensor_tensor(TE[:, s_, :], Z1[:, s_, :], biasr[:, col:col + 1],
                                                                                 pd[:64, s_ * 128:(s_ + 1) * 128], ALU.mult, ALU.add),
                         reads=[r_Z1, prd], writes=[r_TE])
                P.op("pool", lambda e: e.tensor_tensor(zo[:, c_:c_ + 2, :], TE[:], X2[:, c_:c_ + 2, :], ALU.mult),
                     reads=[r_TE, rin[3]], writes=[rzo])
            conv(lambda s_: Z1[:, s_, :], r_Z1, 1, ep2)
        for pi in range(CGR // 2):
            pair(pi)
        P.op("sp", lambda e: e.dma_start(out=zdst[ch0:ch0 + CGR, :].rearrange("c (a b) -> a c b", b=128), in_=zo[:]),
             reads=[rzo], dma=True)
    for g in range(512 // CGR):
        group(g)
    barrier(P)
    P.release(m0)


def build_hyena_test():
    nc = bass.Bass("TRN2", target_bir_lowering=False)
    hl = nc.dram_tensor("hl", [D, LSEQ], F32, kind="ExternalInput").ap()
    wd = {nm: nc.dram_tensor("hy_" + nm, shp, F32, kind="ExternalInput").ap() for nm, shp in HYW_SHAPES.items()}
    cd = {nm: nc.dram_tensor("hc_" + nm, shp, F32, kind="ExternalInput").ap() for nm, shp in HYC_SHAPES.items()}
    zo = nc.dram_tensor("zo", [512, LSEQ], F32, kind="ExternalOutput").ap()
    P = Prog(nc)
    hl3 = hl.rearrange("(c p) t -> c p t", p=128)
    hyena_phase(P, nc, wd, cd, lambda c, t0, n: hl3[c, :, t0:t0 + n], zo)
    P.finish()
    P.emit()
    return nc


NT_CORE = 4096
NCTX_CORE = 128
T_LAT = 8192
T_CTX = 256
PAIRS = [[0, 1], [2, 3], [4, 5], [6, 7]]
SHARED = dict(modw0=(72, 128, KC, 128), modw1=(72, 128, KC, 128),
              wgu00=(FC, 128, 2, KC, 128), wgu01=(FC, 128, 2, KC, 128), wgu10=(FC, 128, 2, KC, 128), wgu11=(FC, 128, 2, KC, 128),
              wdn00=(KC, 128, FC, 128), wdn01=(KC, 128, FC, 128), wdn10=(KC, 128, FC, 128), wdn11=(KC, 128, FC, 128),
              wo=(KC, 128, KC, 128), wout=(KC, 128, KC, 128))
_LET = "abcdefgh"


def gather_shared(P, nc):
    out = {}
    for nm, shp in SHARED.items():
        n = int(np.prod(shp))
        m = n // 8 // 128
        src = nc.dram_tensor("sh_" + nm, [128, m], F32, kind="ExternalInput").ap()
        bnc = nc.dram_tensor("shb_" + nm, [128, m], F32)
        full = nc.dram_tensor("shg_" + nm, [8 * 128, m], F32)
        r1, r2 = Res(), Res()
        P.op("sp", lambda e, bnc=bnc, src=src: e.dma_start(out=bnc.ap(), in_=src), writes=[r1], dma=True)
        P.cc(lambda e, bnc=bnc, full=full: e.collective_compute("AllGather", ALU.bypass, replica_groups=[list(range(8))],
                                                               ins=[bnc.ap().opt()], outs=[full.ap().opt()]),
             reads=[r1], writes=[r2])
        dims = " ".join(_LET[i] for i in range(len(shp)))
        kw = {_LET[i]: shp[i] for i in range(1, len(shp))}
        view = full.ap().rearrange("x y -> (x y)").rearrange(f"({dims}) -> {dims}", **kw)
        out[nm] = (view, r2)
    return out


def pair_gather(P, nc, name, loc, shape):
    full = nc.dram_tensor(name, [2 * shape[0]] + list(shape[1:]), F32)
    r = Res()
    P.cc(lambda e: e.collective_compute("AllGather", ALU.bypass, replica_groups=PAIRS,
                                        ins=[loc.ap().opt()], outs=[full.ap().opt()]), writes=[r])
    return full, r


def with_res(P, res_list):
    return list(res_list)


def build_full(stop=None):
    nc = bass.Bass("TRN2", target_bir_lowering=False)
    TB = 512
    ext = lambda nm, shp: nc.dram_tensor(nm, list(shp), F32, kind="ExternalInput").ap()
    xT = ext("xT", [D, NT_CORE]); cxT = ext("cxT", [D, NCTX_CORE]); cvec = ext("cvec", [128, KC, 2])
    seld = ext("sel", [128, 2])
    modb = [ext("modb0", [128, 72]), ext("modb1", [128, 72])]
    normg = [ext("normg0", [128, 3 * KC]), ext("normg1", [128, 3 * KC])]
    finalg = ext("finalg", [128, KC])
    rwd = {nm: ext("rw_" + nm, shp) for nm, shp in RW_SHAPES.items()}
    hyd = {nm: ext("hy_" + nm, shp) for nm, shp in HYW_SHAPES.items()}
    hcd = {nm: ext("hc_" + nm, shp) for nm, shp in HYC_SHAPES.items()}
    outT = nc.dram_tensor("outT", [D, NT_CORE], F32, kind="ExternalOutput").ap()
    XLd = nc.dram_tensor("i_xl", [D, NT_CORE], F32).ap()
    HLloc = nc.dram_tensor("i_hl", [D, NT_CORE], F32)
    HCloc = nc.dram_tensor("i_hc", [D, NCTX_CORE], F32)
    YGloc = nc.dram_tensor("i_yg", [512, T_LAT], F32)
    HL1loc = nc.dram_tensor("i_hl1", [D, NT_CORE], F32)
    ZDloc = nc.dram_tensor("i_zd", [512, T_LAT], F32)

    P = Prog(nc)
    SH = gather_shared(P, nc)
    barrier(P)
    if stop == "W":
        for k_, nm in enumerate(("wout", "wo", "modw1", "modw0")):
            v_, r_ = SH[nm]
            P.op("sp", lambda e, v_=v_, k_=k_: e.dma_start(out=outT[:, k_ * 1024:(k_ + 1) * 1024].rearrange("(a p) (c k) -> a p c k", p=128, k=128),
                                                        in_=v_[0:8]), reads=[r_], dma=True)
        P.finish(); P.emit()
        return nc

    def row_setup(layer, ncol):
        C = row_common(P, TB)
        sc, r_sc = emit_silu_c(P, C, cvec, 2)
        ng, r_ng = load_vecs(P, normg[layer], 3 * KC, f"normg{layer}")
        mw, r_mw = SH[f"modw{layer}"]
        C.P_extra = [r_mw]
        mod, r_mod = emit_mod_g(C, mw, r_mw, modb[layer], sc, r_sc, 2, f"mod{layer}")
        vecs = [emit_modvecs(C, mod, r_mod, col, ng, r_ng, f"mv{layer}{col}") for col in range(ncol)]
        return C, mod, vecs

    def ffn(C, gs_k, sh_k, hg_k, r_vec, wname, nt):
        wg, r_wg = SH["wgu" + wname]
        wd_, r_wd = SH["wdn" + wname]
        emit_rstd(C, C.X, C.r_X, C.SQ, C.r_SQ, C.R, C.r_R, nt)
        emit_pre(C, C.X, C.r_X, C.XN, C.r_XN, C.R, C.r_R, gs_k, sh_k, r_vec, nt)
        emit_ffn(C, C.XN, C.r_XN, C.H, C.r_H, C.X, C.r_X, hg_k, r_vec, wg, wd_, nt, wres=[r_wg, r_wd])

    def shv(mod, k, col):
        return mod[:, (3 * k) * KC:(3 * k + 1) * KC, col]

    mA = P.mark()
    C, mod0, vecs0 = row_setup(0, 2)
    r_hl, r_hc, r_xl = Res(), Res(), Res()

    def blockA(src, t0, nt, col, xdst, hdst, r_hdst):
        gs, hg, r_vec = vecs0[col]
        xs = src.rearrange("(c p) t -> p c t", p=128)
        P.op("sp", lambda e: e.dma_start(out=C.X[:, :, :nt], in_=xs[:, :, t0:t0 + nt]), writes=[C.r_X], dma=True)
        ffn(C, gs[:, 0, :], shv(mod0, 0, col), hg[:, 0, :], r_vec, "00", nt)
        if xdst is not None:
            xd = xdst.rearrange("(c p) t -> p c t", p=128)
            P.op("sp", lambda e: e.dma_start(out=xd[:, :, t0:t0 + nt], in_=C.X[:, :, :nt]), reads=[C.r_X], writes=[r_xl], dma=True)
        emit_rstd(C, C.X, C.r_X, C.SQ, C.r_SQ, C.R, C.r_R, nt)
        emit_pre(C, C.X, C.r_X, C.XN, C.r_XN, C.R, C.r_R, gs[:, 1, :], shv(mod0, 1, col), r_vec, nt)
        hd = hdst.rearrange("(c p) t -> p c t", p=128)
        P.op("sp", lambda e: e.dma_start(out=hd[:, :, t0:t0 + nt], in_=C.XN[:, :, :nt]), reads=C.r_XN, writes=[r_hdst], dma=True)
    for t0 in range(0, NT_CORE, TB):
        blockA(xT, t0, TB, 0, XLd, HLloc.ap(), r_hl)
    blockA(cxT, 0, NCTX_CORE, 1, None, HCloc.ap(), r_hc)
    barrier(P)
    P.release(mA)
    HLg, r_HLg = pair_gather(P, nc, "g_hl", HLloc, [D, NT_CORE])
    HCg, r_HCg = pair_gather(P, nc, "g_hc", HCloc, [D, NCTX_CORE])
    barrier(P)
    if stop == "A":
        hw_ = NT_CORE // 2
        P.op("sp", lambda e: e.dma_start(out=outT[:, 0:hw_], in_=HLg.ap()[0:D, 0:hw_]), dma=True)
        P.op("sp", lambda e: e.dma_start(out=outT[:, hw_:2 * hw_], in_=HLg.ap()[D:2 * D, 0:hw_]), dma=True)
        P.finish(); P.emit()
        return nc
    hl4 = HLg.ap().rearrange("(h c p) t -> h c p t", h=2, p=128)
    hc4 = HCg.ap().rearrange("(h c p) t -> h c p t", h=2, p=128)

    def pieces(v4, hs):
        def f(c, t0, n):
            out = []
            t = t0
            while t < t0 + n:
                h = t // hs
                e_ = min(t0 + n, (h + 1) * hs)
                out.append((t - t0, e_ - t, v4[h, c, :, t - h * hs:e_ - h * hs]))
                t = e_
            return out
        return f
    ygl = YGloc.ap().rearrange("(o p) t -> o p t", p=128)
    rwkv_phase(P, nc, rwd, pieces(hl4, NT_CORE), pieces(hc4, NCTX_CORE), T_LAT, T_CTX,
               lambda o, g0, nt: ygl[o, :, g0:g0 + nt])
    YGg, r_YGg = pair_gather(P, nc, "g_yg", YGloc, [512, T_LAT])
    barrier(P)
    mC = P.mark()
    C, mod0, vecs0 = row_setup(0, 1)
    selt, r_sel = load_vecs(P, seld, 2, "sel")
    ng1, r_ng1 = load_vecs(P, normg[1], 3 * KC, "normg1b")
    sc1, r_sc1 = emit_silu_c(P, C, cvec, 2)
    mw1, r_mw1 = SH["modw1"]
    mod1, r_mod1 = emit_mod_g(C, mw1, r_mw1, modb[1], sc1, r_sc1, 2, "mod1c")
    vecs1 = [emit_modvecs(C, mod1, r_mod1, 0, ng1, r_ng1, "mv1c")]
    BL = mk_ring([(P.sb(f"bl{i}a", [128, TB]), P.sb(f"bl{i}b", [128, TB])) for i in range(2)], 2)

    def blend_in(G_ap, t0, nt):
        g3 = G_ap.rearrange("(c p) t -> c p t", p=128)
        for c in range(KC):
            (a0, a1), rr = BL.next()
            P.op("sp", lambda e, c=c, a0=a0: e.dma_start(out=a0[:, :nt], in_=g3[c, :, t0:t0 + nt]), writes=[rr[0]], dma=True)
            P.op("pool", lambda e, c=c, a1=a1: e.dma_start(out=a1[:, :nt], in_=g3[c, :, NT_CORE + t0:NT_CORE + t0 + nt]),
                 writes=[rr[1]], dma=True)
            P.op("pool", lambda e, a1=a1: e.tensor_scalar(a1[:, :nt], a1[:, :nt], selt[:, 1:2], None, ALU.mult),
                 reads=[rr[1], r_sel], writes=[rr[1]])
            P.op("dve", lambda e, c=c, a0=a0, a1=a1: e.scalar_tensor_tensor(C.XN[:, c, :nt], a0[:, :nt], selt[:, 0:1], a1[:, :nt],
                                                                            ALU.mult, ALU.add),
                 reads=[rr[0], rr[1], r_sel], writes=[C.r_XN[c]])

    def mixer_out(wname, gate, r_vec, nt):
        wv_, r_wv = SH[wname]

        def evac(o, pt, pr):
            P.op("dve", lambda e: e.scalar_tensor_tensor(C.X[:, o, :nt], pt[:, :nt], gate[:, o:o + 1], C.X[:, o, :nt], ALU.mult, ALU.add),
                 reads=[pr, r_vec, C.r_X], writes=[C.r_X])
        emit_proj(C, C.XN, C.r_XN, wv_, KC, KC, evac, nt, wres=[r_wv])
    xld3 = XLd.rearrange("(c p) t -> p c t", p=128)
    hl1d = HL1loc.ap().rearrange("(c p) t -> p c t", p=128)
    r_hl1 = Res()

    def blockC(t0):
        nt = TB
        gs0, hg0, rv0 = vecs0[0]
        gs1, hg1, rv1 = vecs1[0]
        P.op("sp", lambda e: e.dma_start(out=C.X[:, :, :nt], in_=xld3[:, :, t0:t0 + nt]), reads=[r_xl], writes=[C.r_X], dma=True)
        blend_in(YGg.ap(), t0, nt)
        mixer_out("wo", hg0[:, 1, :], rv0, nt)
        ffn(C, gs0[:, 2, :], shv(mod0, 2, 0), hg0[:, 2, :], rv0, "01", nt)
        ffn(C, gs1[:, 0, :], shv(mod1, 0, 0), hg1[:, 0, :], rv1, "10", nt)
        P.op("sp", lambda e: e.dma_start(out=xld3[:, :, t0:t0 + nt], in_=C.X[:, :, :nt]), reads=[C.r_X], writes=[r_xl], dma=True)
        emit_rstd(C, C.X, C.r_X, C.SQ, C.r_SQ, C.R, C.r_R, nt)
        emit_pre(C, C.X, C.r_X, C.XN, C.r_XN, C.R, C.r_R, gs1[:, 1, :], shv(mod1, 1, 0), rv1, nt)
        P.op("sp", lambda e: e.dma_start(out=hl1d[:, :, t0:t0 + nt], in_=C.XN[:, :, :nt]), reads=C.r_XN, writes=[r_hl1], dma=True)
    for t0 in range(0, NT_CORE, TB):
        blockC(t0)
    barrier(P)
    P.release(mC)
    HL1g, r_HL1g = pair_gather(P, nc, "g_hl1", HL1loc, [D, NT_CORE])
    barrier(P)
    h14 = HL1g.ap().rearrange("(h c p) t -> h c p t", h=2, p=128)
    hyena_phase(P, nc, hyd, hcd, lambda c, t0, n: h14[t0 // NT_CORE, c, :, t0 % NT_CORE:t0 % NT_CORE + n], ZDloc.ap())
    ZDg, r_ZDg = pair_gather(P, nc, "g_zd", ZDloc, [512, T_LAT])
    barrier(P)
    mE = P.mark()
    C = row_common(P, TB)
    selt, r_sel = load_vecs(P, seld, 2, "sel2")
    ng1, r_ng1 = load_vecs(P, normg[1], 3 * KC, "normg1e")
    fg, r_fg = load_vecs(P, finalg, KC, "finalg")
    sc1, r_sc1 = emit_silu_c(P, C, cvec, 2)
    mod1, r_mod1 = emit_mod_g(C, mw1, r_mw1, modb[1], sc1, r_sc1, 2, "mod1e")
    vecs1 = [emit_modvecs(C, mod1, r_mod1, 0, ng1, r_ng1, "mv1e")]
    BL = mk_ring([(P.sb(f"bm{i}a", [128, TB]), P.sb(f"bm{i}b", [128, TB])) for i in range(2)], 2)
    od3 = outT.rearrange("(c p) t -> p c t", p=128)

    def blockE(t0):
        nt = TB
        gs1, hg1, rv1 = vecs1[0]
        P.op("sp", lambda e: e.dma_start(out=C.X[:, :, :nt], in_=xld3[:, :, t0:t0 + nt]), reads=[r_xl], writes=[C.r_X], dma=True)
        blend_in(ZDg.ap(), t0, nt)
        mixer_out("wout", hg1[:, 1, :], rv1, nt)
        ffn(C, gs1[:, 2, :], shv(mod1, 2, 0), hg1[:, 2, :], rv1, "11", nt)
        emit_rstd(C, C.X, C.r_X, C.SQ, C.r_SQ, C.R, C.r_R, nt)
        for c in range(KC):
            P.op("dve", lambda e, c=c: e.scalar_tensor_tensor(C.XN[:, c, :nt], C.X[:, c, :nt], fg[:, c:c + 1], C.R[:, :nt], ALU.mult, ALU.mult),
                 reads=[C.r_X, C.r_R, r_fg], writes=[C.r_XN[c]])
        P.op("sp", lambda e: e.dma_start(out=od3[:, :, t0:t0 + nt], in_=C.XN[:, :, :nt]), reads=C.r_XN, dma=True)
    for t0 in range(0, NT_CORE, TB):
        blockE(t0)
    P.finish()
    P.emit()
    return nc


def emit_mod_g(C, modw_v, r_modw, modb_d, sc_t, sc_r, ncol, name):
    P = C.P
    mod = P.sb(name, [128, 72, ncol])
    modb, r_modb = load_vecs(P, modb_d, 72, name + "_b")
    r_mod = Res(name)
    for j in range(72):
        wt, wr = C.wsq.next()
        P.op(C.dma_eng(), lambda e, wt=wt, j=j: e.dma_start(out=wt[:], in_=modw_v[j]), reads=[r_modw], writes=[wr], dma=True)
        pt, pr = C.psum.next()
        for c in range(KC):
            P.op("pe", lambda e, pt=pt, wt=wt, c=c: e.matmul(pt[:, 0:ncol], wt[:, c, :], sc_t[:, c, :],
                                                             start=(c == 0), stop=(c == KC - 1)),
                 reads=[wr, sc_r], writes=[pr])
        P.op("dve", lambda e, pt=pt, j=j: e.tensor_scalar(mod[:, j, :], pt[:, 0:ncol], modb[:, j:j + 1], None, ALU.add),
             reads=[pr, r_modb], writes=[r_mod])
    return mod, r_mod


_NC_CACHE = {}


def kernel_fused(**inp):
    inp = {k: np.asarray(v) for k, v in inp.items()}
    f = lambda a: np.ascontiguousarray(np.asarray(a, dtype=np.float32))
    shared = {}
    for l in range(2):
        shared[f"modw{l}"] = wlay(inp["mod_w"][l])
        for i in range(2):
            shared[f"wgu{l}{i}"] = wgulay(inp["ffn_w_gu"][l, i])
            shared[f"wdn{l}{i}"] = wlay(inp["ffn_w_down"][l, i])
    shared["wo"] = wlay(inp["rw_w_o"][0])
    shared["wout"] = wlay(inp["hy_w_out"][0])
    shards = {nm: a.reshape(8, 128, -1) for nm, a in shared.items()}
    hyc = hyena_consts()
    rww = [rwkv_host_weights(inp, j) for j in range(2)]
    hyw = [hyena_host_weights(inp, j) for j in range(2)]
    in_maps = []
    for c in range(NCORES):
        b, j = c // 2, c % 2
        m = {}
        m["xT"] = fm(inp["x"][b, j * NT_CORE:(j + 1) * NT_CORE])
        m["cxT"] = fm(inp["ctx"][b, j * NCTX_CORE:(j + 1) * NCTX_CORE])
        m["cvec"] = f(np.stack([pvec(inp["c"][b]), pvec(inp["c_ctx"])], axis=-1))
        m["sel"] = f(np.tile(np.array([[1.0 - j, float(j)]], np.float32), (128, 1)))
        for l in range(2):
            m[f"modb{l}"] = pvec(f(inp["mod_b"][l]))
            m[f"normg{l}"] = pvec(f(inp["norm_g"][l]).reshape(-1))
        m["finalg"] = pvec(f(inp["final_g"]))
        for nm, a in shards.items():
            m["sh_" + nm] = f(a[c])
        for nm, a in rww[j].items():
            m["rw_" + nm] = a
        for nm, a in hyw[j].items():
            m["hy_" + nm] = a
        for nm, a in hyc.items():
            m["hc_" + nm] = a
        in_maps.append(m)
    if "nc" not in _NC_CACHE:
        import os
        _NC_CACHE["nc"] = build_full(os.environ.get("KSTOP"))
    res = run_bass_kernel_spmd(_NC_CACHE["nc"], in_maps, core_ids=list(range(NCORES)))
    out = np.empty((4, T_LAT, D), np.float32)
    for c in range(NCORES):
        b, j = c // 2, c % 2
        out[b, j * NT_CORE:(j + 1) * NT_CORE, :] = res.results[c]["outT"].T
    return out


def _ffn_ext(C, gs_k, sh_k, hg_k, r_vec, wg, wd_, nt):
    emit_rstd(C, C.X, C.r_X, C.SQ, C.r_SQ, C.R, C.r_R, nt)
    emit_pre(C, C.X, C.r_X, C.XN, C.r_XN, C.R, C.r_R, gs_k, sh_k, r_vec, nt)
    emit_ffn(C, C.XN, C.r_XN, C.H, C.r_H, C.X, C.r_X, hg_k, r_vec, wg, wd_, nt)


def _shv(mod, k, col):
    return mod[:, (3 * k) * KC:(3 * k + 1) * KC, col]


def _mixer_out(C, P, wv_, gate, r_vec, nt):
    def evac(o, pt, pr):
        P.op("dve", lambda e: e.scalar_tensor_tensor(C.X[:, o, :nt], pt[:, :nt], gate[:, o:o + 1], C.X[:, o, :nt], ALU.mult, ALU.add),
             reads=[pr, r_vec, C.r_X], writes=[C.r_X])
    emit_proj(C, C.XN, C.r_XN, wv_, KC, KC, evac, nt)


def build_lC(NT, TB=512):
    nc = bass.Bass("TRN2", target_bir_lowering=False)
    ext = lambda nm, shp: nc.dram_tensor(nm, list(shp), F32, kind="ExternalInput").ap()
    xT = ext("xT", [D, NT]); ygT = ext("ygT", [D, NT]); cvec = ext("cvec", [128, KC, 2])
    mod0d = ext("mod0", [128, 72, 2]); modw1 = ext("modw1", [72, 128, KC, 128]); modb1 = ext("modb1", [128, 72])
    normg0 = ext("normg0", [128, 3 * KC]); normg1 = ext("normg1", [128, 3 * KC])
    wo = ext("wo", [KC, 128, KC, 128])
    wgu01 = ext("wgu01", [FC, 128, 2, KC, 128]); wdn01 = ext("wdn01", [KC, 128, FC, 128])
    wgu10 = ext("wgu10", [FC, 128, 2, KC, 128]); wdn10 = ext("wdn10", [KC, 128, FC, 128])
    xo = nc.dram_tensor("xo", [D, NT], F32, kind="ExternalOutput").ap()
    h1o = nc.dram_tensor("h1o", [D, NT], F32, kind="ExternalOutput").ap()
    mod1o = nc.dram_tensor("mod1o", [128, 72, 2], F32, kind="ExternalOutput").ap()
    P = Prog(nc)
    C = row_common(P, TB)
    mod0 = P.sb("mod0t", [128, 72, 2]); r_mod0 = Res()
    P.op("sp", lambda e: e.dma_start(out=mod0[:], in_=mod0d), writes=[r_mod0], dma=True)
    ng0, r_ng0 = load_vecs(P, normg0, 3 * KC, "ng0")
    ng1, r_ng1 = load_vecs(P, normg1, 3 * KC, "ng1")
    sc, r_sc = emit_silu_c(P, C, cvec, 2)
    mod1, r_mod1 = emit_mod(C, modw1, modb1, sc, r_sc, 2, "mod1")
    P.op("sp", lambda e: e.dma_start(out=mod1o, in_=mod1[:]), reads=[r_mod1], dma=True)
    gs0, hg0, rv0 = emit_modvecs(C, mod0, r_mod0, 0, ng0, r_ng0, "mv0")
    gs1, hg1, rv1 = emit_modvecs(C, mod1, r_mod1, 0, ng1, r_ng1, "mv1")
    xs = xT.rearrange("(c p) t -> p c t", p=128); ys = ygT.rearrange("(c p) t -> p c t", p=128)
    xd = xo.rearrange("(c p) t -> p c t", p=128); hd = h1o.rearrange("(c p) t -> p c t", p=128)

    def block(t0, nt):
        P.op("sp", lambda e: e.dma_start(out=C.X[:, :, :nt], in_=xs[:, :, t0:t0 + nt]), writes=[C.r_X], dma=True)
        P.op("sp", lambda e: e.dma_start(out=C.XN[:, :, :nt], in_=ys[:, :, t0:t0 + nt]), writes=C.r_XN, dma=True)
        _mixer_out(C, P, wo, hg0[:, 1, :], rv0, nt)
        _ffn_ext(C, gs0[:, 2, :], _shv(mod0, 2, 0), hg0[:, 2, :], rv0, wgu01, wdn01, nt)
        _ffn_ext(C, gs1[:, 0, :], _shv(mod1, 0, 0), hg1[:, 0, :], rv1, wgu10, wdn10, nt)
        P.op("sp", lambda e: e.dma_start(out=xd[:, :, t0:t0 + nt], in_=C.X[:, :, :nt]), reads=[C.r_X], dma=True)
        emit_rstd(C, C.X, C.r_X, C.SQ, C.r_SQ, C.R, C.r_R, nt)
        emit_pre(C, C.X, C.r_X, C.XN, C.r_XN, C.R, C.r_R, gs1[:, 1, :], _shv(mod1, 1, 0), rv1, nt)
        P.op("sp", lambda e: e.dma_start(out=hd[:, :, t0:t0 + nt], in_=C.XN[:, :, :nt]), reads=C.r_XN, dma=True)
    for t0 in range(0, NT, TB):
        block(t0, min(TB, NT - t0))
    P.finish()
    P.emit()
    return nc


def build_lE(NT, TB=512):
    nc = bass.Bass("TRN2", target_bir_lowering=False)
    ext = lambda nm, shp: nc.dram_tensor(nm, list(shp), F32, kind="ExternalInput").ap()
    xT = ext("xT", [D, NT]); zT = ext("zT", [D, NT])
    mod1d = ext("mod1", [128, 72, 2]); normg1 = ext("normg1", [128, 3 * KC]); finalg = ext("finalg", [128, KC])
    wout = ext("wout", [KC, 128, KC, 128])
    wgu11 = ext("wgu11", [FC, 128, 2, KC, 128]); wdn11 = ext("wdn11", [KC, 128, FC, 128])
    outT = nc.dram_tensor("outT", [D, NT], F32, kind="ExternalOutput").ap()
    P = Prog(nc)
    C = row_common(P, TB)
    mod1 = P.sb("mod1t", [128, 72, 2]); r_mod1 = Res()
    P.op("sp", lambda e: e.dma_start(out=mod1[:], in_=mod1d), writes=[r_mod1], dma=True)
    ng1, r_ng1 = load_vecs(P, normg1, 3 * KC, "ng1")
    fg, r_fg = load_vecs(P, finalg, KC, "fg")
    gs1, hg1, rv1 = emit_modvecs(C, mod1, r_mod1, 0, ng1, r_ng1, "mv1")
    xs = xT.rearrange("(c p) t -> p c t", p=128); zs = zT.rearrange("(c p) t -> p c t", p=128)
    od = outT.rearrange("(c p) t -> p c t", p=128)

    def block(t0, nt):
        P.op("sp", lambda e: e.dma_start(out=C.X[:, :, :nt], in_=xs[:, :, t0:t0 + nt]), writes=[C.r_X], dma=True)
        P.op("sp", lambda e: e.dma_start(out=C.XN[:, :, :nt], in_=zs[:, :, t0:t0 + nt]), writes=C.r_XN, dma=True)
        _mixer_out(C, P, wout, hg1[:, 1, :], rv1, nt)
        _ffn_ext(C, gs1[:, 2, :], _shv(mod1, 2, 0), hg1[:, 2, :], rv1, wgu11, wdn11, nt)
        emit_rstd(C, C.X, C.r_X, C.SQ, C.r_SQ, C.R, C.r_R, nt)
        for c in range(KC):
            P.op("dve", lambda e, c=c: e.scalar_tensor_tensor(C.XN[:, c, :nt], C.X[:, c, :nt], fg[:, c:c + 1], C.R[:, :nt], ALU.mult, ALU.mult),
                 reads=[C.r_X, C.r_R, r_fg], writes=[C.r_XN[c]])
        P.op("sp", lambda e: e.dma_start(out=od[:, :, t0:t0 + nt], in_=C.XN[:, :, :nt]), reads=C.r_XN, dma=True)
    for t0 in range(0, NT, TB):
        block(t0, min(TB, NT - t0))
    P.finish()
    P.emit()
    return nc


def kernel_multi(**inp):
    inp = {k: np.asarray(v) for k, v in inp.items()}
    f = lambda a: np.ascontiguousarray(np.asarray(a, dtype=np.float32))
    cores = list(range(NCORES))
    run = lambda nc, ims: run_bass_kernel_spmd(nc, ims, core_ids=cores).results
    cv = [f(np.stack([pvec(inp["c"][c // 2]), pvec(inp["c_ctx"])], axis=-1)) for c in cores]
    ng = [pvec(f(inp["norm_g"][l]).reshape(-1)) for l in range(2)]
    wA = dict(modw=wlay(inp["mod_w"][0]), modb=pvec(f(inp["mod_b"][0])), normg=ng[0],
              wgu=wgulay(inp["ffn_w_gu"][0, 0]), wdn=wlay(inp["ffn_w_down"][0, 0]))
    ims = []
    for c in cores:
        b, j = c // 2, c % 2
        m = dict(wA)
        m["xT"] = fm(inp["x"][b, j * NT_CORE:(j + 1) * NT_CORE]); m["cxT"] = fm(inp["ctx"][b, j * NCTX_CORE:(j + 1) * NCTX_CORE])
        m["cvec"] = cv[c]
        ims.append(m)
    rA = run(build_l1(NT_CORE, NCTX_CORE), ims)
    del wA, ims
    rww = [rwkv_host_weights(inp, j) for j in range(2)]
    ims = []
    for c in cores:
        b, j = c // 2, c % 2
        m = {"rw_" + k: v for k, v in rww[j].items()}
        m["hl"] = np.ascontiguousarray(np.concatenate([rA[2 * b]["ho"], rA[2 * b + 1]["ho"]], axis=1))
        m["hc"] = np.ascontiguousarray(np.concatenate([rA[2 * b]["hco"], rA[2 * b + 1]["hco"]], axis=1))
        ims.append(m)
    rB = run(build_rwkv_test(T_LAT, T_CTX), ims)
    del ims
    wC = dict(modw1=wlay(inp["mod_w"][1]), modb1=pvec(f(inp["mod_b"][1])), normg0=ng[0], normg1=ng[1],
              wo=wlay(inp["rw_w_o"][0]), wgu01=wgulay(inp["ffn_w_gu"][0, 1]), wdn01=wlay(inp["ffn_w_down"][0, 1]),
              wgu10=wgulay(inp["ffn_w_gu"][1, 0]), wdn10=wlay(inp["ffn_w_down"][1, 0]))
    ims = []
    for c in cores:
        b, j = c // 2, c % 2
        m = dict(wC)
        m["xT"] = rA[c]["xo"]
        yg = np.concatenate([rB[2 * b]["yg"].reshape(512, T_LAT), rB[2 * b + 1]["yg"].reshape(512, T_LAT)], axis=0)
        m["ygT"] = np.ascontiguousarray(yg[:, j * NT_CORE:(j + 1) * NT_CORE])
        m["cvec"] = cv[c]; m["mod0"] = rA[c]["modo"]
        ims.append(m)
    del rB
    rC = run(build_lC(NT_CORE), ims)
    del wC, ims, rA
    hyc = {"hc_" + k: v for k, v in hyena_consts().items()}
    hyw = [hyena_host_weights(inp, j) for j in range(2)]
    ims = []
    for c in cores:
        b, j = c // 2, c % 2
        m = {"hy_" + k: v for k, v in hyw[j].items()}
        m.update(hyc)
        m["hl"] = np.ascontiguousarray(np.concatenate([rC[2 * b]["h1o"], rC[2 * b + 1]["h1o"]], axis=1))
        ims.append(m)
    rD = run(build_hyena_test(), ims)
    del ims
    wE = dict(normg1=ng[1], finalg=pvec(f(inp["final_g"])), wout=wlay(inp["hy_w_out"][0]),
              wgu11=wgulay(inp["ffn_w_gu"][1, 1]), wdn11=wlay(inp["ffn_w_down"][1, 1]))
    ims = []
    for c in cores:
        b, j = c // 2, c % 2
        m = dict(wE)
        m["xT"] = rC[c]["xo"]; m["mod1"] = rC[c]["mod1o"]
        z = np.concatenate([rD[2 * b]["zo"], rD[2 * b + 1]["zo"]], axis=0)
        m["zT"] = np.ascontiguousarray(z[:, j * NT_CORE:(j + 1) * NT_CORE])
        ims.append(m)
    del rD
    rE = run(build_lE(NT_CORE), ims)
    out = np.empty((4, T_LAT, D), np.float32)
    for c in cores:
        b, j = c // 2, c % 2
        out[b, j * NT_CORE:(j + 1) * NT_CORE, :] = rE[c]["outT"].T
    return out


def kernel(**inp):
    return kernel_multi(**inp)
```

```python
import math
from contextlib import ExitStack

import numpy as np
import concourse.bass as bass
import concourse.mybir as mybir
from concourse.bass_utils import run_bass_kernel_spmd

F32 = mybir.dt.float32
AF = mybir.ActivationFunctionType
ALU = mybir.AluOpType
AX = mybir.AxisListType

D = 1024
KC = D // 128
FF = 2816
FC = FF // 128
NCORES = 8
NORM_EPS = 1e-6
GN_EPS = 64e-5

ENGS = ("pe", "act", "dve", "pool", "sp")
EPOCH = 30000
DMA_EPOCH = 1800
ARENA_KB = 204
NO_POOL = True


class Res:
    __slots__ = ("w", "rs", "name")

    def __init__(self, name=""):
        self.w = None
        self.rs = {}
        self.name = name


class Prog:
    def __init__(self, nc, n_dma=8):
        self.nc = nc
        self.streams = {e: [] for e in ENGS}
        self.cnt = {e: 0 for e in ENGS}
        self.known = {e: {} for e in ENGS}
        self.n_dma = n_dma
        self.dma_rr = {}
        self.dma_cnt = {}
        self.semkeys = {}
        self.stack = ExitStack()
        self.last_dma = {}
        self.sb_off = 0
        self.sb_id = 0
        self.arena = None
        self.n_cc = 0
        self._banks = None

    def sb(self, name, shape, dtype=F32):
        if self.arena is None:
            self.arena = self.stack.enter_context(self.nc.sbuf_tensor("arena", [128, ARENA_KB * 256], F32))
        n = 1
        for d in shape[1:]:
            n *= d
        n = (n + 7) // 8 * 8
        off = self.sb_off
        assert off + n <= ARENA_KB * 256, (name, off, n)
        self.sb_off = off + n
        v = self.arena[0:shape[0], off:off + n]
        dims = " ".join(f"d{i}" for i in range(1, len(shape)))
        if len(shape) > 2:
            kw = {f"d{i}": shape[i] for i in range(2, len(shape))}
            v = v[:, 0:int(np.prod(shape[1:]))].rearrange(f"p ({dims}) -> p {dims}", **kw)
        else:
            v = v[:, 0:shape[1]]
        return v

    def mark(self):
        return self.sb_off

    def release(self, m):
        self.sb_off = m

    def banks(self):
        if self._banks is None:
            self._banks = [self.stack.enter_context(self.nc.psum_tensor(f"pp_bank{i}", [128, 512], F32))
                           for i in range(8)]
        return self._banks

    def _key(self, key):
        if key not in self.semkeys:
            self.semkeys[key] = None
        return key

    def op(self, eng, fn, reads=(), writes=(), dma=False):
        if NO_POOL:
            if dma:
                eng = "sp"
            elif eng == "pool":
                eng = "dve"
        waits = {}
        known = self.known[eng]

        def need(dep):
            if dep is None:
                return
            key, val = dep
            if eng == "pe" and key[0] == "pe":
                return
            if known.get(key, 0) >= val:
                return
            if waits.get(key, 0) < val:
                waits[key] = val

        for r in reads:
            need(r.w)
        for w in writes:
            need(w.w)
            for k, v in w.rs.items():
                need((k, v))
        if dma:
            i = self.dma_rr.get(eng, 0)
            self.dma_rr[eng] = (i + 1) % self.n_dma
            n = self.dma_cnt.get((eng, i), 0)
            ep, idx = divmod(n, DMA_EPOCH)
            key = self._key(("dma", eng, i, ep))
            if idx > 0:
                need((key, idx * 16))
            elif ep > 0:
                need((("dma", eng, i, ep - 1), DMA_EPOCH * 16))
            self.dma_cnt[(eng, i)] = n + 1
            done = (key, (idx + 1) * 16)
            self.last_dma[(eng, i, ep)] = done
        else:
            n = self.cnt[eng]
            ep, idx = divmod(n, EPOCH)
            key = self._key((eng, ep))
            self.cnt[eng] = n + 1
            done = (key, idx + 1)
        for k, v in waits.items():
            known[k] = v
        self.streams[eng].append((list(waits.items()), fn, done, dma))
        for r in reads:
            r.rs[done[0]] = done[1]
        for w in writes:
            w.w = done
            w.rs = {}
        return done

    def cc(self, fn, reads=(), writes=()):
        eng = "pool"
        waits = {}
        known = self.known[eng]

        def need(dep):
            if dep is None:
                return
            key, val = dep
            if known.get(key, 0) >= val:
                return
            if waits.get(key, 0) < val:
                waits[key] = val
        for r in reads:
            need(r.w)
        for w in writes:
            need(w.w)
            for k, v in w.rs.items():
                need((k, v))
        if self.n_cc > 0:
            need((("cc", self.n_cc), 1))
        self.n_cc += 1
        key = self._key(("cc", self.n_cc))
        done = (key, 1)
        self.last_dma[("cc", self.n_cc)] = done
        for k, v in waits.items():
            known[k] = v
        self.streams[eng].append((list(waits.items()), fn, done, "cc"))
        for r in reads:
            r.rs[done[0]] = done[1]
        for w in writes:
            w.w = done
            w.rs = {}
        return done

    def finish(self):
        waits = [(k, v) for k, v in self.last_dma.values()]
        self.streams["sp"].append((waits, None, None, False))

    def emit(self):
        nc = self.nc
        sems = {}
        for key in self.semkeys:
            nm = "s_" + "_".join(str(x) for x in key)
            sems[key] = self.stack.enter_context(nc.semaphore(nm))
        streams = self.streams

        def run(name, e):
            for waits, fn, done, dma in streams[name]:
                for k, v in waits:
                    e.wait_ge(sems[k], v)
                if fn is None:
                    continue
                ins = fn(e)
                if dma == "cc":
                    ins.then_inc(sems[done[0]])
                else:
                    ins.then_inc(sems[done[0]], 16 if dma else 1)

        with nc.Block() as block:
            @block.tensor
            def _(e):
                run("pe", e)

            @block.scalar
            def _(e):
                run("act", e)

            @block.vector
            def _(e):
                run("dve", e)

            @block.gpsimd
            def _(e):
                run("pool", e)

            @block.sync
            def _(e):
                run("sp", e)
        self.stack.close()


class Ring:
    def __init__(self, tiles):
        self.tiles = tiles
        self.res = [Res() for _ in tiles]
        self.i = 0

    def next(self):
        i = self.i
        self.i = (i + 1) % len(self.tiles)
        return self.tiles[i], self.res[i]


def fm(a):
    return np.ascontiguousarray(a.T)


def wlay(w):
    K, N = w.shape
    kc, oc = K // 128, N // 128
    return np.ascontiguousarray(w.reshape(kc, 128, oc, 128).transpose(2, 1, 0, 3))


def wgulay(w):
    a = wlay(w)
    return np.ascontiguousarray(np.stack([a[:FC], a[FC:]], axis=2))


def pvec(v):
    return np.ascontiguousarray(v.reshape(-1, 128).T)


class RowCtx:
    def __init__(self, P, TB):
        self.P = P
        self.TB = TB
        nc = P.nc
        self.ones = P.sb("ones", [128, 128])
        self.r_ones = Res()
        P.op("pool", lambda e: e.memset(self.ones[:], 1.0), writes=[self.r_ones])
        self.psum = Ring(P.banks())
        self.wgu = Ring([P.sb(f"wgu{i}", [128, 2, KC, 128]) for i in range(3)])
        self.wdn = Ring([P.sb(f"wdn{i}", [128, FC, 128]) for i in range(2)])
        self.wsq = Ring([P.sb(f"wsq{i}", [128, KC, 128]) for i in range(3)])
        self.dq = 0

    def dma_eng(self):
        self.dq += 1
        return "sp" if self.dq % 2 else "pool"


def load_vecs(P, dram_ap, n, name):
    t = P.sb(name, [128, n])
    r = Res(name)
    P.op("sp", lambda e: e.dma_start(out=t[:], in_=dram_ap), writes=[r], dma=True)
    return t, r


def emit_mod(C, modw_d, modb_d, sc_t, sc_r, ncol, name):
    P = C.P
    mod = P.sb(name, [128, 72, ncol])
    modb, r_modb = load_vecs(P, modb_d, 72, name + "_b")
    r_mod = Res(name)
    for j in range(72):
        wt, wr = C.wsq.next()
        P.op(C.dma_eng(), lambda e, wt=wt, j=j: e.dma_start(out=wt[:], in_=modw_d[j]), writes=[wr], dma=True)
        pt, pr = C.psum.next()
        for c in range(KC):
            P.op("pe", lambda e, pt=pt, wt=wt, c=c: e.matmul(pt[:, 0:ncol], wt[:, c, :], sc_t[:, c, :],
                                                             start=(c == 0), stop=(c == KC - 1)),
                 reads=[wr, sc_r], writes=[pr])
        P.op("dve", lambda e, pt=pt, j=j: e.tensor_scalar(mod[:, j, :], pt[:, 0:ncol], modb[:, j:j + 1], None,
                                                          ALU.add),
             reads=[pr, r_modb], writes=[r_mod])
    return mod, r_mod


def emit_modvecs(C, mod, r_mod, col, normg, r_normg, name):
    P = C.P
    gs = P.sb(name + "_gs", [128, 3, KC])
    hg = P.sb(name + "_hg", [128, 3, KC])
    r = Res(name)
    for k in range(3):
        sc = mod[:, (3 * k + 1) * KC:(3 * k + 2) * KC, col]
        gt = mod[:, (3 * k + 2) * KC:(3 * k + 3) * KC, col]
        P.op("dve", lambda e, k=k, sc=sc: e.scalar_tensor_tensor(gs[:, k, :], sc, 1.0, normg[:, k * KC:(k + 1) * KC],
                                                                 ALU.add, ALU.mult),
             reads=[r_mod, r_normg], writes=[r])
        P.op("dve", lambda e, k=k, gt=gt: e.tensor_scalar(hg[:, k, :], gt, 1.0 if k == 1 else 0.5, None, ALU.mult),
             reads=[r_mod], writes=[r])
    return gs, hg, r


def emit_rstd(C, X, r_X, SQ, r_SQ, R, r_R, nt):
    P = C.P
    pt, pr = C.psum.next()
    for c in range(KC):
        P.op("act" if c % 2 else "dve",
             (lambda e, c=c: e.activation(SQ[:, c, :nt], X[:, c, :nt], AF.Square)) if c % 2 else
             (lambda e, c=c: e.tensor_tensor(SQ[:, c, :nt], X[:, c, :nt], X[:, c, :nt], ALU.mult)),
             reads=[r_X], writes=[r_SQ[c]])
    for c in range(KC):
        P.op("pe", lambda e, c=c: e.matmul(pt[:, :nt], C.ones[:], SQ[:, c, :nt], start=(c == 0), stop=(c == KC - 1)),
             reads=[C.r_ones, r_SQ[c]], writes=[pr])
    P.op("act", lambda e: e.activation(R[:, :nt], pt[:, :nt], AF.Sqrt, bias=C.epsb[:, 0:1], scale=1.0 / D),
         reads=[pr, C.r_epsb], writes=[r_R])
    P.op("dve", lambda e: e.reciprocal(R[:, :nt], R[:, :nt]), reads=[r_R], writes=[r_R])


def emit_pre(C, X, r_X, XN, r_XN, R, r_R, gs, sh, r_vec, nt):
    P = C.P
    for c in range(KC):
        P.op("dve", lambda e, c=c: e.scalar_tensor_tensor(XN[:, c, :nt], X[:, c, :nt], gs[:, c:c + 1], R[:, :nt],
                                                          ALU.mult, ALU.mult),
             reads=[r_X, r_R, r_vec], writes=[r_XN[c]])
        P.op("act", lambda e, c=c: e.activation(XN[:, c, :nt], XN[:, c, :nt], AF.Identity, bias=sh[:, c:c + 1]),
             reads=[r_XN[c], r_vec], writes=[r_XN[c]])


def emit_ffn(C, XN, r_XN, H, r_H, X, r_X, hg, r_vec, wgu_d, wdn_d, nt, wres=()):
    P = C.P
    for fo in range(FC):
        wt, wr = C.wgu.next()
        P.op(C.dma_eng(), lambda e, wt=wt, fo=fo: e.dma_start(out=wt[:], in_=wgu_d[fo]), reads=list(wres), writes=[wr], dma=True)
        pg, prg = C.psum.next()
        pu, pru = C.psum.next()
        for c in range(KC):
            P.op("pe", lambda e, pg=pg, wt=wt, c=c: e.matmul(pg[:, :nt], wt[:, 0, c, :], XN[:, c, :nt],
                                                             start=(c == 0), stop=(c == KC - 1)),
                 reads=[wr, r_XN[c]], writes=[prg])
        for c in range(KC):
            P.op("pe", lambda e, pu=pu, wt=wt, c=c: e.matmul(pu[:, :nt], wt[:, 1, c, :], XN[:, c, :nt],
                                                             start=(c == 0), stop=(c == KC - 1)),
                 reads=[wr, r_XN[c]], writes=[pru])
        P.op("act", lambda e, pg=pg, fo=fo: e.activation(H[:, fo, :nt], pg[:, :nt], AF.Silu),
             reads=[prg], writes=[r_H[fo]])
        P.op("dve", lambda e, pu=pu, fo=fo: e.tensor_tensor(H[:, fo, :nt], H[:, fo, :nt], pu[:, :nt], ALU.mult),
             reads=[pru, r_H[fo]], writes=[r_H[fo]])
    for do in range(KC):
        wt, wr = C.wdn.next()
        P.op(C.dma_eng(), lambda e, wt=wt, do=do: e.dma_start(out=wt[:], in_=wdn_d[do]), reads=list(wres), writes=[wr], dma=True)
        po, pro = C.psum.next()
        for f in range(FC):
            P.op("pe", lambda e, po=po, wt=wt, f=f: e.matmul(po[:, :nt], wt[:, f, :], H[:, f, :nt],
                                                             start=(f == 0), stop=(f == FC - 1)),
                 reads=[wr, r_H[f]], writes=[pro])
        P.op("dve", lambda e, po=po, do=do: e.scalar_tensor_tensor(X[:, do, :nt], po[:, :nt], hg[:, do:do + 1],
                                                                   X[:, do, :nt], ALU.mult, ALU.add),
             reads=[pro, r_vec, r_X], writes=[r_X])


def emit_proj(C, XIN, r_XIN, w_d, n_out, kc, evac, nt, wres=()):
    P = C.P
    for o in range(n_out):
        wt, wr = C.wsq.next()
        P.op(C.dma_eng(), lambda e, wt=wt, o=o: e.dma_start(out=wt[:, :kc, :], in_=w_d[o]), reads=list(wres), writes=[wr], dma=True)
        pt, pr = C.psum.next()
        for c in range(kc):
            P.op("pe", lambda e, pt=pt, wt=wt, c=c: e.matmul(pt[:, :nt], wt[:, c, :], XIN[:, c, :nt],
                                                             start=(c == 0), stop=(c == kc - 1)),
                 reads=[wr, r_XIN[c]], writes=[pr])
        evac(o, pt, pr)


def row_common(P, TB):
    C = RowCtx(P, TB)
    C.epsb = P.sb("epsb", [128, 1])
    C.r_epsb = Res()
    P.op("pool", lambda e: e.memset(C.epsb[:], NORM_EPS), writes=[C.r_epsb])
    C.X = P.sb("X", [128, KC, TB]); C.r_X = Res("X")
    C.XN = P.sb("XN", [128, KC, TB]); C.r_XN = [Res() for _ in range(KC)]
    C.SQ = P.sb("SQ", [128, KC, TB]); C.r_SQ = [Res() for _ in range(KC)]
    C.H = P.sb("H", [128, FC, TB]); C.r_H = [Res() for _ in range(FC)]
    C.R = P.sb("R", [128, TB]); C.r_R = Res("R")
    return C


def emit_silu_c(P, C, cvec_d, ncol):
    t = P.sb("sc", [128, KC, ncol])
    r = Res("sc")
    P.op("sp", lambda e: e.dma_start(out=t[:], in_=cvec_d), writes=[r], dma=True)
    P.op("act", lambda e: e.activation(t[:], t[:], AF.Silu), reads=[r], writes=[r])
    return t, r


def build_l1(NT, NCTX, TB=512):
    nc = bass.Bass("TRN2", target_bir_lowering=False)
    xT = nc.dram_tensor("xT", [D, NT], F32, kind="ExternalInput").ap()
    cxT = nc.dram_tensor("cxT", [D, NCTX], F32, kind="ExternalInput").ap()
    cvec = nc.dram_tensor("cvec", [128, KC, 2], F32, kind="ExternalInput").ap()
    modw = nc.dram_tensor("modw", [72, 128, KC, 128], F32, kind="ExternalInput").ap()
    modb = nc.dram_tensor("modb", [128, 72], F32, kind="ExternalInput").ap()
    normg = nc.dram_tensor("normg", [128, 3 * KC], F32, kind="ExternalInput").ap()
    wgu = nc.dram_tensor("wgu", [FC, 128, 2, KC, 128], F32, kind="ExternalInput").ap()
    wdn = nc.dram_tensor("wdn", [KC, 128, FC, 128], F32, kind="ExternalInput").ap()
    xo = nc.dram_tensor("xo", [D, NT], F32, kind="ExternalOutput").ap()
    ho = nc.dram_tensor("ho", [D, NT], F32, kind="ExternalOutput").ap()
    hco = nc.dram_tensor("hco", [D, NCTX], F32, kind="ExternalOutput").ap()
    modo = nc.dram_tensor("modo", [128, 72, 2], F32, kind="ExternalOutput").ap()

    P = Prog(nc)
    C = row_common(P, TB)
    sc, r_sc = emit_silu_c(P, C, cvec, 2)
    ng, r_ng = load_vecs(P, normg, 3 * KC, "normg")
    mod, r_mod = emit_mod(C, modw, modb, sc, r_sc, 2, "mod")
    P.op("sp", lambda e: e.dma_start(out=modo, in_=mod[:]), reads=[r_mod], dma=True)
    vecs = [emit_modvecs(C, mod, r_mod, col, ng, r_ng, f"mv{col}") for col in range(2)]

    def block(src, t0, nt, col, xdst, hdst):
        gs, hg, r_vec = vecs[col]
        X, r_X = C.X, C.r_X
        xs = src.rearrange("(c p) t -> p c t", p=128)
        P.op("sp", lambda e: e.dma_start(out=X[:, :, :nt], in_=xs[:, :, t0:t0 + nt]), writes=[r_X], dma=True)
        sh = lambda k: mod[:, (3 * k) * KC:(3 * k + 1) * KC, col]
        emit_rstd(C, X, r_X, C.SQ, C.r_SQ, C.R, C.r_R, nt)
        emit_pre(C, X, r_X, C.XN, C.r_XN, C.R, C.r_R, gs[:, 0, :], sh(0), r_vec, nt)
        emit_ffn(C, C.XN, C.r_XN, C.H, C.r_H, X, r_X, hg[:, 0, :], r_vec, wgu, wdn, nt)
        if xdst is not None:
            xd = xdst.rearrange("(c p) t -> p c t", p=128)
            P.op("sp", lambda e: e.dma_start(out=xd[:, :, t0:t0 + nt], in_=X[:, :, :nt]), reads=[r_X], dma=True)
        emit_rstd(C, X, r_X, C.SQ, C.r_SQ, C.R, C.r_R, nt)
        emit_pre(C, X, r_X, C.XN, C.r_XN, C.R, C.r_R, gs[:, 1, :], sh(1), r_vec, nt)
        hd = hdst.rearrange("(c p) t -> p c t", p=128)
        P.op("sp", lambda e: e.dma_start(out=hd[:, :, t0:t0 + nt], in_=C.XN[:, :, :nt]), reads=C.r_XN, dma=True)

    for t0 in range(0, NT, TB):
        block(xT, t0, min(TB, NT - t0), 0, xo, ho)
    for t0 in range(0, NCTX, TB):
        block(cxT, t0, min(TB, NCTX - t0), 1, None, hco)
    P.finish()
    P.emit()
    return nc


def const_masks():
    i = np.arange(128)
    same = (i[:, None] // 64) == (i[None, :] // 64)
    su = same & (i[:, None] < i[None, :])
    sl = same & (i[:, None] > i[None, :])
    iu = same & (i[:, None] <= i[None, :])
    il = same & (i[:, None] >= i[None, :])
    ident = i[:, None] == i[None, :]
    return np.ascontiguousarray(np.stack([su, sl, iu, il, same, ident], axis=1).astype(np.float32))


SU, SL, IU, IL, BO, IDENT = range(6)
HC_ = 4
DEC_C = -math.exp(-0.5)


class EW:
    def __init__(self):
        self.i = 0

    def __call__(self):
        self.i += 1
        return "dve" if self.i % 2 else "pool"


def rwkv_alloc(P, G):
    S = type("S", (), {})()
    S.G = G
    S.cm = P.sb("cm", [128, 6, 128]); S.r_cm = Res("cm")
    S.psum = Ring(P.banks())
    S.ew = EW()
    return S


def build_rwkv_prep(P, S, W, hsrc, nt_total, is_ctx, t_base, dr, T_lat):
    G = S.G
    ew = S.ew
    HALO = 64
    A = S.A
    def group(g0):
        nt = min(G, nt_total - g0)
        nch = nt // 64
        Hin = A["Hin"]
        lo = max(0, g0 - HALO); hi = min(nt_total, g0 + nt + HALO)
        P.op("pool", lambda e: e.memset(Hin[:], 0.0), writes=A["r_Hc"])
        for c in range(KC):
            rc = A["r_Hc"][c]
            for (poff, pn, pap) in hsrc(c, lo, hi - lo):
                d0 = lo - (g0 - HALO) + poff
                P.op("sp" if c % 2 else "pool",
                     lambda e, c=c, d0=d0, pn=pn, pap=pap: e.dma_start(out=Hin[:, c, d0:d0 + pn], in_=pap),
                     writes=[rc], dma=True)
        DF, r_DF = A["DF"], A["r_DF"]
        for c in range(KC):
            rc = A["r_Hc"][c]
            q = c // 2
            if is_ctx:
                off = -1 if q % 2 == 0 else 1
            else:
                off = (-1, 1, -64, 64)[q]
            hv = Hin[:, c, HALO:HALO + nt]
            sv = Hin[:, c, HALO + off:HALO + off + nt]
            P.op(ew(), lambda e, c=c, hv=hv, sv=sv: e.tensor_tensor(DF[:, c, :nt], sv, hv, ALU.subtract),
                 reads=[rc], writes=[r_DF[c]])
            if (not is_ctx) and q < 2:
                col = 0 if q == 0 else 63
                dv = DF[:, c, :nt].rearrange("p (j k) -> p j k", k=64)[:, :, col:col + 1]
                hv3 = hv.rearrange("p (j k) -> p j k", k=64)[:, :, col:col + 1]
                P.op(ew(), lambda e, dv=dv, hv3=hv3: e.tensor_scalar(dv, hv3, -1.0, None, ALU.mult),
                     reads=[rc, r_DF[c]], writes=[r_DF[c]])
        XM, r_XM = A["XM"], A["r_XM"]

        def mixj(j):
            for c in range(KC):
                P.op("dve", lambda e, c=c, j=j: e.scalar_tensor_tensor(XM[:, c, :nt], DF[:, c, :nt],
                                                                      W["mix"][:, j, c:c + 1],
                                                                      Hin[:, c, HALO:HALO + nt], ALU.mult, ALU.add),
                     reads=[r_DF[c], A["r_Hc"][c], W["r"]], writes=[r_XM[c]])

        def proj_fm(wt, n_out, evac, kparts=128, mcols=128):
            for o in range(n_out):
                pt, pr = S.psum.next()
                for c in range(KC):
                    P.op("pe", lambda e, pt=pt, o=o, c=c: e.matmul(pt[:mcols, :nt], wt[:, c, o * mcols:(o + 1) * mcols],
                                                                    XM[:, c, :nt], start=(c == 0), stop=(c == KC - 1)),
                         reads=[W["r"], r_XM[c]], writes=[pr])
                evac(o, pt, pr)

        Rf, Kf, Vf = A["Rf"], A["Kf"], A["Vf"]
        r_Rf, r_Kf, r_Vf = A["r_Rf"], A["r_Kf"], A["r_Vf"]
        mixj(0)
        proj_fm(W["wr"], HC_, lambda o, pt, pr: P.op("act", lambda e: e.activation(Rf[:, o, :nt], pt[:, :nt], AF.Copy),
                                                     reads=[pr], writes=[r_Rf[o]]))
        mixj(1)
        HW, r_HW = A["HW"], A["r_HW"]
        for n in range(2):
            pt, pr = S.psum.next()
            for c in range(KC):
                P.op("pe", lambda e, pt=pt, n=n, c=c: e.matmul(pt[:64, :nt], W["w1"][:, n, c, :], XM[:, c, :nt],
                                                               start=(c == 0), stop=(c == KC - 1)),
                     reads=[W["r"], r_XM[c]], writes=[pr])
            P.op("act", lambda e, pt=pt, n=n: e.activation(HW[:64, n, :nt], pt[:64, :nt], AF.Tanh),
                 reads=[pr], writes=[r_HW[n]])
        mixj(2)
        proj_fm(W["wk"], HC_, lambda o, pt, pr: P.op("act", lambda e: e.activation(Kf[:, o, :nt], pt[:, :nt], AF.Copy),
                                                     reads=[pr], writes=[r_Kf[o]]))
        mixj(3)
        proj_fm(W["wv"], HC_, lambda o, pt, pr: P.op("act", lambda e: e.activation(Vf[:, o, :nt], pt[:, :nt], AF.Copy),
                                                     reads=[pr], writes=[r_Vf[o]]))
        VT, r_VT = A["VT"], A["r_VT"]
        for tt in range(nt // 128):
            pt, pr = S.psum.next()
            for c in range(KC):
                P.op("pe", lambda e, pt=pt, tt=tt, c=c: e.matmul(pt[:, :], XM[:, c, tt * 128:(tt + 1) * 128],
                                                                 W["wv"][:, c, :], start=(c == 0), stop=(c == KC - 1)),
                     reads=[W["r"], r_XM[c]], writes=[pr])
            P.op("dve", lambda e, pt=pt, tt=tt: e.tensor_copy(VT[:, tt, :], pt[:, :]), reads=[pr], writes=[r_VT])
        vtd = dr["VT"]
        P.op("sp", lambda e, g0=g0, nt=nt: e.dma_start(
            out=vtd[t_base + g0:t_base + g0 + nt, :].rearrange("(j p) c -> p j c", p=128), in_=VT[:, :nt // 128, :]),
             reads=[r_VT], dma=True)
        mixj(4)
        HA, r_HA = A["HA"], A["r_HA"]
        for n in range(2):
            pt, pr = S.psum.next()
            for c in range(KC):
                P.op("pe", lambda e, pt=pt, n=n, c=c: e.matmul(pt[:64, :nt], W["a1"][:, n, c, :], XM[:, c, :nt],
                                                               start=(c == 0), stop=(c == KC - 1)),
                     reads=[W["r"], r_XM[c]], writes=[pr])
            P.op("act", lambda e, pt=pt, n=n: e.activation(HA[:64, n, :nt], pt[:64, :nt], AF.Copy),
                 reads=[pr], writes=[r_HA[n]])
        if not is_ctx:
            mixj(5)
            HG, r_HG = A["HG"], A["r_HG"]
            for part, (m0, m1) in enumerate(((0, 128), (128, 160))):
                pt, pr = S.psum.next()
                for c in range(KC):
                    P.op("pe", lambda e, pt=pt, c=c, m0=m0, m1=m1: e.matmul(pt[:m1 - m0, :nt], W["g1"][:, c, m0:m1],
                                                                             XM[:, c, :nt], start=(c == 0),
                                                                             stop=(c == KC - 1)),
                         reads=[W["r"], r_XM[c]], writes=[pr])
                P.op("act", lambda e, pt=pt, part=part, m0=m0, m1=m1: e.activation(HG[:m1 - m0, part, :nt],
                                                                                  pt[:m1 - m0, :nt], AF.Sigmoid),
                     reads=[pr], writes=[r_HG[part]])
            GG, r_GG = A["GG"], A["r_GG"]
            for o in range(HC_):
                pt, pr = S.psum.next()
                P.op("pe", lambda e, pt=pt, o=o: e.matmul(pt[:, :nt], W["g2a"][:, o * 128:(o + 1) * 128], HG[:, 0, :nt],
                                                          start=True, stop=False),
                     reads=[W["r"], r_HG[0]], writes=[pr])
                P.op("pe", lambda e, pt=pt, o=o: e.matmul(pt[:, :nt], W["g2b"][:32, o * 128:(o + 1) * 128],
                                                          HG[:32, 1, :nt], start=False, stop=True),
                     reads=[W["r"], r_HG[1]], writes=[pr])
                P.op("act", lambda e, pt=pt, o=o: e.activation(GG[:, o, :nt], pt[:, :nt], AF.Copy),
                     reads=[pr], writes=[r_GG])
            P.op("sp", lambda e, g0=g0, nt=nt: e.dma_start(out=dr["GG"][:, :, g0:g0 + nt], in_=GG[:, :, :nt]),
                 reads=[r_GG], dma=True)
        KK, r_KK = A["KK"], A["r_KK"]
        T1, r_T1 = A["T1"], A["r_T1"]
        for o in range(HC_):
            P.op(ew(), lambda e, o=o: e.tensor_scalar(KK[:, o, :nt], Kf[:, o, :nt], W["kk_"][:, o:o + 1], None, ALU.mult),
                 reads=[r_Kf[o], W["r"]], writes=[r_KK[o]])
            P.op(ew(), lambda e, o=o: e.tensor_tensor(T1[:, o, :nt], KK[:, o, :nt], KK[:, o, :nt], ALU.mult),
                 reads=[r_KK[o]], writes=[r_T1[o]])
            pt, pr = S.psum.next()
            P.op("pe", lambda e, pt=pt, o=o: e.matmul(pt[:, :nt], S.cm[:, BO, :], T1[:, o, :nt], start=True, stop=True),
                 reads=[S.r_cm, r_T1[o]], writes=[pr])
            P.op("act", lambda e, pt=pt, o=o: e.activation(T1[:, o, :nt], pt[:, :nt], AF.Sqrt),
                 reads=[pr, r_T1[o]], writes=[r_T1[o]])
            P.op("dve", lambda e, o=o: e.tensor_scalar(T1[:, o, :nt], T1[:, o, :nt], 1e-12, None, ALU.max),
                 reads=[r_T1[o]], writes=[r_T1[o]])
            P.op("dve", lambda e, o=o: e.reciprocal(T1[:, o, :nt], T1[:, o, :nt]), reads=[r_T1[o]], writes=[r_T1[o]])
            P.op(ew(), lambda e, o=o: e.tensor_tensor(KK[:, o, :nt], KK[:, o, :nt], T1[:, o, :nt], ALU.mult),
                 reads=[r_T1[o], r_KK[o]], writes=[r_KK[o]])
        AN, r_AN = A["AN"], A["r_AN"]
        KD, r_KD = A["KD"], A["r_KD"]
        LW, r_LW = A["LW"], A["r_LW"]
        def dirpart(n):
            for tt in range(nt // 128):
                pt, pr = S.psum.next()
                P.op("pe", lambda e, pt=pt, n=n, tt=tt: e.matmul(pt[:, :], HW[:64, n, tt * 128:(tt + 1) * 128],
                                                                 W["w2"][:64, n, :], start=True, stop=False),
                     reads=[W["r"], r_HW[n]], writes=[pr])
                P.op("pe", lambda e, pt=pt, n=n: e.matmul(pt[:, :], W["ones1"][0:1, :], W["w0"][0:1, n, :],
                                                          start=False, stop=True),
                     reads=[W["r"]], writes=[pr])
                P.op("act", lambda e, pt=pt, tt=tt: e.activation(LW[:, tt, :], pt[:, :], AF.Sigmoid),
                     reads=[pr], writes=[r_LW[tt]])
                P.op("dve", lambda e, tt=tt: e.tensor_scalar(LW[:, tt, :], LW[:, tt, :], DEC_C, None, ALU.mult),
                     reads=[r_LW[tt]], writes=[r_LW[tt]])
            tin, tex = (IU, SU) if n == 0 else (IL, SL)
            def chunkpart(o):
                pa, pra = S.psum.next()
                P.op("pe", lambda e, pa=pa, n=n, o=o: e.matmul(pa[:, :nt], W["a2"][:64, n, o * 128:(o + 1) * 128],
                                                               HA[:64, n, :nt], start=True, stop=True),
                     reads=[W["r"], r_HA[n]], writes=[pra])
                P.op("act", lambda e, pa=pa, n=n, o=o: e.activation(AN[:, o, :nt], pa[:, :nt], AF.Sigmoid,
                                                                    bias=W["a0"][:, n, o:o + 1]),
                     reads=[pra, W["r"]], writes=[r_AN[o]])
                pi, pri = S.psum.next()
                px, prx = S.psum.next()
                for tt in range(nt // 128):
                    P.op("pe", lambda e, pi=pi, tt=tt, o=o: e.matmul(pi[:, tt * 128:(tt + 1) * 128],
                                                                     LW[:, tt, o * 128:(o + 1) * 128], S.cm[:, tin, :],
                                                                     start=True, stop=True),
                         reads=[S.r_cm, r_LW[tt]], writes=[pri])
                    P.op("pe", lambda e, px=px, tt=tt, o=o: e.matmul(px[:, tt * 128:(tt + 1) * 128],
                                                                     LW[:, tt, o * 128:(o + 1) * 128], S.cm[:, tex, :],
                                                                     start=True, stop=True),
                         reads=[S.r_cm, r_LW[tt]], writes=[prx])
                (EI, EX, EV), r_E = A["Ering"].next()
                P.op("act", lambda e, pi=pi: e.activation(EI[:, :nt], pi[:, :nt], AF.Exp), reads=[pri], writes=[r_E[0]])
                P.op("act", lambda e, px=px: e.activation(EX[:, :nt], px[:, :nt], AF.Exp), reads=[prx], writes=[r_E[1]])
                P.op("act", lambda e, pi=pi: e.activation(EV[:, :nt], pi[:, :nt], AF.Exp, scale=-1.0),
                     reads=[pri], writes=[r_E[2]])
                O4, r_O4 = A["O4ring"].next()
                P.op(ew(), lambda e, o=o: e.tensor_tensor(O4[:, 0, :nt], KK[:, o, :nt], EX[:, :nt], ALU.mult),
                     reads=[r_KK[o], r_E[1]], writes=[r_O4[0]])
                P.op(ew(), lambda e, o=o: e.tensor_tensor(O4[:, 1, :nt], KK[:, o, :nt], AN[:, o, :nt], ALU.mult),
                     reads=[r_KK[o], r_AN[o]], writes=[r_O4[1]])
                P.op("dve", lambda e: e.scalar_tensor_tensor(O4[:, 1, :nt], O4[:, 1, :nt], -1.0, EV[:, :nt],
                                                            ALU.mult, ALU.mult),
                     reads=[r_O4[1], r_E[2]], writes=[r_O4[1]])
                P.op(ew(), lambda e, o=o, n=n: e.tensor_scalar(KD[:, n, o, :nt], AN[:, o, :nt], W["ka_"][:, o:o + 1],
                                                               W["oka_"][:, o:o + 1], ALU.mult, ALU.add),
                     reads=[r_AN[o], W["r"]], writes=[r_KD[n][o]])
                P.op(ew(), lambda e, o=o, n=n: e.tensor_tensor(KD[:, n, o, :nt], KD[:, n, o, :nt], Kf[:, o, :nt], ALU.mult),
                     reads=[r_KD[n][o], r_Kf[o]], writes=[r_KD[n][o]])
                P.op(ew(), lambda e, o=o, n=n: e.tensor_tensor(O4[:, 2, :nt], KD[:, n, o, :nt], EV[:, :nt], ALU.mult),
                     reads=[r_KD[n][o], r_E[2]], writes=[r_O4[2]])
                P.op(ew(), lambda e, o=o: e.tensor_tensor(O4[:, 3, :nt], Rf[:, o, :nt], EI[:, :nt], ALU.mult),
                     reads=[r_Rf[o], r_E[0]], writes=[r_O4[3]])
                WCt, r_WC = A["WC"], A["r_WC"]
                ecol = 63 if n == 0 else 0
                ev3 = EI[:, :nt].rearrange("p (j k) -> p j k", k=64)[:, :, ecol:ecol + 1]
                P.op(ew(), lambda e, ev3=ev3: e.tensor_copy(WCt[:, :nch].rearrange("p (j k) -> p j k", k=1), ev3),
                     reads=[r_E[0]], writes=[r_WC])
                c0 = (t_base + g0) // 64
                P.op("sp", lambda e, n=n, o=o, c0=c0, nch=nch: e.dma_start(out=dr["WC"][n, o, :, c0:c0 + nch],
                                                                           in_=WCt[:, :nch]),
                     reads=[r_WC], dma=True)
                for q in range(4):
                    P.op("sp" if q % 2 else "pool",
                         lambda e, n=n, o=o, q=q, g0=g0, nt=nt: e.dma_start(
                             out=dr["OPS"][n, q, o, :, t_base + g0:t_base + g0 + nt], in_=O4[:, q, :nt]),
                         reads=[r_O4[q]], dma=True)
            for o in range(HC_):
                chunkpart(o)
        for n in range(2):
            dirpart(n)
        if not is_ctx:
            BV, r_BV = A["BV"], A["r_BV"]
            for o in range(HC_):
                P.op(ew(), lambda e, o=o: e.tensor_tensor(T1[:, o, :nt], KD[:, 0, o, :nt], KD[:, 1, o, :nt], ALU.add),
                     reads=[r_KD[0][o], r_KD[1][o], r_T1[o]], writes=[r_T1[o]])
                P.op("dve", lambda e, o=o: e.scalar_tensor_tensor(T1[:, o, :nt], T1[:, o, :nt], W["hrk_"][:, o:o + 1],
                                                                 Rf[:, o, :nt], ALU.mult, ALU.mult),
                     reads=[r_T1[o], r_Rf[o], W["r"]], writes=[r_T1[o]])
                pt, pr = S.psum.next()
                P.op("pe", lambda e, pt=pt, o=o: e.matmul(pt[:, :nt], S.cm[:, BO, :], T1[:, o, :nt], start=True, stop=True),
                     reads=[S.r_cm, r_T1[o]], writes=[pr])
                P.op("dve", lambda e, pt=pt, o=o: e.tensor_tensor(BV[:, o, :nt], pt[:, :nt], Vf[:, o, :nt], ALU.mult),
                     reads=[pr, r_Vf[o]], writes=[r_BV])
            P.op("sp", lambda e, g0=g0, nt=nt: e.dma_start(out=dr["BV"][:, :, g0:g0 + nt], in_=BV[:, :, :nt]),
                 reads=[r_BV], dma=True)

    for g0 in range(0, nt_total, G):
        group(g0)


def barrier(P):
    targets = {}
    for e in ENGS:
        n = P.cnt[e]
        if n > 0:
            ep, idx = divmod(n - 1, EPOCH)
            targets[(e, ep)] = idx + 1
    for _k, (k, v) in P.last_dma.items():
        targets[k] = max(targets.get(k, 0), v)
    for e in ENGS:
        waits = []
        for k, v in targets.items():
            if e == "pe" and k[0] == "pe":
                continue
            if P.known[e].get(k, 0) < v:
                waits.append((k, v))
                P.known[e][k] = v
        P.streams[e].append((waits, None, None, False))


def mk_ring(tiles, nres):
    r = Ring(tiles)
    r.res = [[Res() for _ in range(nres)] for _ in tiles]
    return r


def rwkv_prep_alloc(P, S, G):
    A = {}
    A["Hin"] = P.sb("Hin", [128, KC, G + 128]); A["r_Hc"] = [Res() for _ in range(KC)]
    A["DF"] = P.sb("DF", [128, KC, G]); A["r_DF"] = [Res() for _ in range(KC)]
    A["XM"] = P.sb("XM", [128, KC, G]); A["r_XM"] = [Res() for _ in range(KC)]
    for nm in ("Rf", "Kf", "Vf", "KK", "T1", "AN"):
        A[nm] = P.sb(nm, [128, HC_, G]); A["r_" + nm] = [Res() for _ in range(HC_)]
    A["VT"] = P.sb("VT", [128, G // 128, 512]); A["r_VT"] = Res()
    A["HW"] = P.sb("HW", [64, 2, G]); A["r_HW"] = [Res(), Res()]
    A["HA"] = P.sb("HA", [64, 2, G]); A["r_HA"] = [Res(), Res()]
    A["HG"] = P.sb("HG", [128, 2, G]); A["r_HG"] = [Res(), Res()]
    A["GG"] = P.sb("GG", [128, HC_, G]); A["r_GG"] = Res()
    A["BV"] = P.sb("BV", [128, HC_, G]); A["r_BV"] = Res()
    A["KD"] = P.sb("KD", [128, 2, HC_, G]); A["r_KD"] = [[Res() for _ in range(HC_)] for _ in range(2)]
    A["LW"] = P.sb("LW", [128, G // 128, 512]); A["r_LW"] = [Res() for _ in range(G // 128)]
    A["Ering"] = mk_ring([tuple(P.sb(f"E{i}{j}", [128, G]) for j in range(3)) for i in range(2)], 3)
    A["O4ring"] = mk_ring([P.sb(f"O4{i}", [128, 4, G]) for i in range(2)], 4)
    A["WC"] = P.sb("WCt", [128, G // 64]); A["r_WC"] = Res()
    S.A = A


def rwkv_load_weights(P, S, wd):
    W = {"r": Res("W")}
    shapes = dict(mix=[128, 6, KC], wr=[128, KC, 512], wk=[128, KC, 512], wv=[128, KC, 512],
                  w1=[128, 2, KC, 64], a1=[128, 2, KC, 64], w2=[64, 2, 512], a2=[64, 2, 512],
                  w0=[1, 2, 512], a0=[128, 2, HC_], g1=[128, KC, 160], g2a=[128, 512], g2b=[32, 512],
                  kk_=[128, HC_], ka_=[128, HC_], rk=[128, HC_], lng=[128, HC_], lnb=[128, HC_])
    i = 0
    order = ["lng", "lnb"] + [k for k in shapes if k not in ("lng", "lnb")]
    for nm in order:
        shp = shapes[nm]
        t = P.sb("w_" + nm, shp)
        W[nm] = t
        i += 1
        P.op("sp" if i % 2 else "pool", lambda e, t=t, nm=nm: e.dma_start(out=t[:], in_=wd[nm]), writes=[Res()],
             dma=True)
        if nm == "lnb":
            S.m_persist = P.mark()
    P.op("sp", lambda e: e.dma_start(out=S.cm[:], in_=wd["cm"]), writes=[S.r_cm], dma=True)
    barrier(P)
    W["oka_"] = P.sb("w_oka", [128, HC_]); W["hrk_"] = P.sb("w_hrk", [128, HC_]); W["ones1"] = P.sb("w_ones1", [1, 128])
    P.op("dve", lambda e: e.tensor_scalar(W["oka_"][:], W["ka_"][:], -1.0, 1.0, ALU.mult, ALU.add), writes=[W["r"]])
    P.op("dve", lambda e: e.tensor_scalar(W["hrk_"][:], W["rk"][:], 0.5, None, ALU.mult), writes=[W["r"]])
    P.op("pool", lambda e: e.memset(W["ones1"][:], 1.0), writes=[W["r"]])
    return W


def build_rwkv_scan(P, S, dr, NC_CTX, NC_LAT):
    NCT = NC_CTX + NC_LAT
    CG = 2
    chains = [(n, o) for n in range(2) for o in range(HC_)]
    order = [list(range(NCT)),
             list(range(NC_CTX - 1, -1, -1)) + list(range(NCT - 1, NC_CTX - 1, -1))]
    cm = S.cm
    ps = Ring(P.banks())

    def ps_next():
        t, r = ps.next()
        return t[:, 0:128], r

    OPB = {}
    for ch in chains:
        tiles = [P.sb(f"opb{ch[0]}{ch[1]}{i}", [128, 4, CG, 128]) for i in range(2)]
        OPB[ch] = mk_ring(tiles, 1)
        for t in tiles:
            P.op("pool", lambda e, t=t: e.memset(t[:], 0.0), writes=[])
    VB = {}
    for ch in chains:
        tiles = [P.sb(f"vb{ch[0]}{ch[1]}{i}", [128, CG, 128]) for i in range(2)]
        VB[ch] = mk_ring(tiles, 1)
        for t in tiles:
            P.op("pool", lambda e, t=t: e.memset(t[:], 0.0), writes=[])
    WCB = {ch: mk_ring([P.sb(f"wcb{ch[0]}{ch[1]}{i}", [128, CG]) for i in range(2)], 1) for ch in chains}
    tmp = {}
    rtmp = {}
    for ch in chains:
        for nm in ["M0", "M1", "M2", "M3", "M4", "M5", "Na", "Nb", "BT", "PT", "NQT", "KHT", "NKAHT", "U", "ST"]:
            tmp[ch, nm] = P.sb(f"t{nm}{ch[0]}{ch[1]}", [128, 128])
            rtmp[ch, nm] = Res()
        P.op("pool", lambda e, t=tmp[ch, "ST"]: e.memset(t[:], 0.0), writes=[rtmp[ch, "ST"]])
    YT = {ch: mk_ring([P.sb(f"yt{ch[0]}{ch[1]}{i}", [128, 64]) for i in range(2)], 1) for ch in chains}
    barrier(P)
    cur = {}
    curv = {}
    dq = [0]

    def dma_q():
        dq[0] += 1
        return "sp" if dq[0] % 2 else "pool"

    def load_v(ch, lo, n):
        o = ch[1]
        vt, vr = VB[ch].next()
        vr = vr[0]
        for h in range(2):
            src = dr["VT"][lo * 64:(lo + n) * 64, o * 128 + h * 64:o * 128 + h * 64 + 64].rearrange(
                "(j s) v -> s j v", s=64)
            P.op(dma_q(), lambda e, vt=vt, h=h, src=src, n=n: e.dma_start(out=vt[h * 64:(h + 1) * 64, :n, h * 64:(h + 1) * 64],
                                                                        in_=src), writes=[vr], dma=True)
        return vt, vr

    for i in range(NCT):
        units = []
        for ch in chains:
            n, o = ch
            chunk = order[n][i]
            is_ctx = chunk < NC_CTX
            if is_ctx:
                base = 0; lim = NC_CTX
            else:
                base = NC_CTX; lim = NCT
            lo = base + ((chunk - base) // CG) * CG
            cnt = min(CG, lim - lo)
            if ch not in cur or cur[ch][0] != lo:
                ot, orr = OPB[ch].next(); orr = orr[0]
                wt, wr = WCB[ch].next(); wr = wr[0]
                for q in range(4):
                    for h in range(2):
                        src = dr["OPS"][n, q, o, h * 64:(h + 1) * 64, lo * 64:(lo + cnt) * 64].rearrange(
                            "p (j t) -> p j t", t=64)
                        P.op(dma_q(), lambda e, ot=ot, q=q, h=h, src=src, cnt=cnt: e.dma_start(
                            out=ot[h * 64:(h + 1) * 64, q, :cnt, h * 64:(h + 1) * 64], in_=src), writes=[orr], dma=True)
                P.op(dma_q(), lambda e, wt=wt, n=n, o=o, lo=lo, cnt=cnt: e.dma_start(out=wt[:, :cnt],
                                                                                 in_=dr["WC"][n, o, :, lo:lo + cnt]),
                     writes=[wr], dma=True)
                vt, vr = load_v(ch, lo, cnt)
                cur[ch] = (lo, ot, orr, wt, wr, vt, vr)
            lo, ot, orr, wt, wr, vt, vr = cur[ch]
            j = chunk - lo
            u = dict(ch=ch, n=n, o=o, chunk=chunk, is_ctx=is_ctx, j=j, orr=orr, vr=vr, wr=wr,
                     KKT=ot[:, 0, j, :], NKAH=ot[:, 1, j, :], KH=ot[:, 2, j, :], RT=ot[:, 3, j, :],
                     V=vt[:, j, :], WC=wt[:, j:j + 1])
            units.append(u)

        def T(u, nm):
            return tmp[u["ch"], nm]

        def R(u, nm):
            return rtmp[u["ch"], nm]

        def mm_evac(lhs_nm, rhs_nm, out_nm, mask=None, cond=lambda u: True):
            for u in units:
                if not cond(u):
                    continue
                pt, pr = ps_next()

                def opnd(nm):
                    if nm in ("KKT", "NKAH", "KH", "RT"):
                        return u[nm], u["orr"]
                    if nm == "I":
                        return cm[:, IDENT, :], S.r_cm
                    return T(u, nm)[:], R(u, nm)
                la, lr = opnd(lhs_nm)
                ra, rr = opnd(rhs_nm)
                P.op("pe", lambda e, pt=pt, la=la, ra=ra: e.matmul(pt, la, ra, start=True, stop=True),
                     reads=[lr, rr], writes=[pr])
                out = T(u, out_nm)
                if mask is None:
                    P.op("act", lambda e, pt=pt, out=out: e.activation(out[:], pt, AF.Copy),
                         reads=[pr], writes=[R(u, out_nm)])
                else:
                    mk = mask(u)
                    P.op("dve", lambda e, pt=pt, out=out, mk=mk: e.tensor_tensor(out[:], pt, cm[:, mk, :], ALU.mult),
                         reads=[pr, S.r_cm], writes=[R(u, out_nm)])

        ms = lambda u: SU if u["n"] == 0 else SL
        mst = lambda u: SL if u["n"] == 0 else SU
        mi = lambda u: IU if u["n"] == 0 else IL
        lat = lambda u: not u["is_ctx"]
        mm_evac("NKAH", "KKT", "M0", ms)
        mm_evac("KKT", "NKAH", "Na", mst)
        mm_evac("KH", "KKT", "BT", ms)
        mm_evac("KH", "RT", "PT", mi, lat)
        mm_evac("NKAH", "RT", "NQT", mi, lat)
        mm_evac("KH", "I", "KHT")
        mm_evac("NKAH", "I", "NKAHT")
        ncur, nnxt = "Na", "Nb"
        for lvl in range(5):
            mm_evac(ncur, f"M{lvl}", f"M{lvl + 1}")
            if lvl < 4:
                mm_evac(f"M{lvl}", ncur, nnxt)
                ncur, nnxt = nnxt, ncur
        xs = []
        for u in units:
            pt, pr = ps_next()
            P.op("pe", lambda e, pt=pt, u=u: e.matmul(pt, u["KKT"], T(u, "ST")[:], start=True, stop=False),
                 reads=[u["orr"], R(u, "ST")], writes=[pr])
            P.op("pe", lambda e, pt=pt, u=u: e.matmul(pt, T(u, "BT")[:], u["V"], start=False, stop=True),
                 reads=[R(u, "BT"), u["vr"]], writes=[pr])
            P.op("act", lambda e, pt=pt, u=u: e.activation(T(u, "U")[:], pt, AF.Copy), reads=[pr], writes=[R(u, "U")])
        for lvl in range(6):
            for u in units:
                pt, pr = ps_next()
                P.op("pe", lambda e, pt=pt, u=u, lvl=lvl: e.matmul(pt, T(u, f"M{lvl}")[:], T(u, "U")[:], start=True, stop=True),
                     reads=[R(u, f"M{lvl}"), R(u, "U")], writes=[pr])
                P.op("dve", lambda e, pt=pt, u=u: e.tensor_tensor(T(u, "U")[:], T(u, "U")[:], pt, ALU.add),
                     reads=[pr, R(u, "U")], writes=[R(u, "U")])
        for u in units:
            if u["is_ctx"]:
                continue
            pt, pr = ps_next()
            P.op("pe", lambda e, pt=pt, u=u: e.matmul(pt, T(u, "ST")[:], u["RT"], start=True, stop=False),
                 reads=[u["orr"], R(u, "ST")], writes=[pr])
            P.op("pe", lambda e, pt=pt, u=u: e.matmul(pt, u["V"], T(u, "PT")[:], start=False, stop=False),
                 reads=[u["vr"], R(u, "PT")], writes=[pr])
            P.op("pe", lambda e, pt=pt, u=u: e.matmul(pt, T(u, "U")[:], T(u, "NQT")[:], start=False, stop=True),
                 reads=[R(u, "U"), R(u, "NQT")], writes=[pr])
            yt, yr = YT[u["ch"]].next(); yr = yr[0]
            for h in range(2):
                P.op("act", lambda e, pt=pt, yt=yt, h=h: e.activation(yt[h * 64:(h + 1) * 64, :],
                                                                     pt[h * 64:(h + 1) * 64, h * 64:(h + 1) * 64], AF.Copy),
                     reads=[pr], writes=[yr])
            t0 = (u["chunk"] - NC_CTX) * 64
            P.op(dma_q(), lambda e, yt=yt, u=u, t0=t0: e.dma_start(out=dr["Y"][u["n"], u["o"], :, t0:t0 + 64], in_=yt[:]),
                 reads=[yr], dma=True)
        for u in units:
            pt, pr = ps_next()
            P.op("pe", lambda e, pt=pt, u=u: e.matmul(pt, T(u, "KHT")[:], u["V"], start=True, stop=False),
                 reads=[R(u, "KHT"), u["vr"]], writes=[pr])
            P.op("pe", lambda e, pt=pt, u=u: e.matmul(pt, T(u, "NKAHT")[:], T(u, "U")[:], start=False, stop=True),
                 reads=[R(u, "NKAHT"), R(u, "U")], writes=[pr])
            P.op("dve", lambda e, pt=pt, u=u: e.tensor_tensor(T(u, "ST")[:], T(u, "ST")[:], pt, ALU.add),
                 reads=[pr, R(u, "ST")], writes=[R(u, "ST")])
            P.op("dve", lambda e, u=u: e.tensor_scalar(T(u, "ST")[:], T(u, "ST")[:], u["WC"], None, ALU.mult),
                 reads=[R(u, "ST"), u["wr"]], writes=[R(u, "ST")])


def build_rwkv_post(P, S, W, dr, T_lat, ygdst, G=512):
    bufs = mk_ring([tuple(P.sb(f"pb{i}{j}", [128, G]) for j in range(5)) for i in range(2)], 5)
    epsg = P.sb("epsg", [128, 1]); r_eps = Res()
    P.op("pool", lambda e: e.memset(epsg[:], GN_EPS), writes=[r_eps])
    def one(g0, nt, o):
        if True:
            (Y0, Y1, Bv, Gg, Tm), rr = bufs.next()
            srcs = [dr["Y"][0, o, :, g0:g0 + nt], dr["Y"][1, o, :, g0:g0 + nt], dr["BV"][:, o, g0:g0 + nt],
                    dr["GG"][:, o, g0:g0 + nt]]
            for k, (t, s_) in enumerate(zip((Y0, Y1, Bv, Gg), srcs)):
                P.op("sp" if k % 2 else "pool", lambda e, t=t, s_=s_: e.dma_start(out=t[:, :nt], in_=s_),
                     writes=[rr[k]], dma=True)
            P.op("dve", lambda e: e.tensor_tensor(Y0[:, :nt], Y0[:, :nt], Y1[:, :nt], ALU.add),
                 reads=[rr[0], rr[1]], writes=[rr[0]])
            pm, prm = S.psum.next()
            P.op("pe", lambda e, pm=pm: e.matmul(pm[:, :nt], S.cm[:, BO, :], Y0[:, :nt], start=True, stop=True),
                 reads=[S.r_cm, rr[0]], writes=[prm])
            P.op("dve", lambda e, pm=pm: e.scalar_tensor_tensor(Y0[:, :nt], pm[:, :nt], -1.0 / 64, Y0[:, :nt],
                                                                ALU.mult, ALU.add),
                 reads=[prm, rr[0]], writes=[rr[0]])
            P.op("pool", lambda e: e.tensor_tensor(Tm[:, :nt], Y0[:, :nt], Y0[:, :nt], ALU.mult),
                 reads=[rr[0]], writes=[rr[4]])
            pv, prv = S.psum.next()
            P.op("pe", lambda e, pv=pv: e.matmul(pv[:, :nt], S.cm[:, BO, :], Tm[:, :nt], start=True, stop=True),
                 reads=[S.r_cm, rr[4]], writes=[prv])
            P.op("act", lambda e, pv=pv: e.activation(Tm[:, :nt], pv[:, :nt], AF.Sqrt, bias=epsg[:, 0:1], scale=1.0 / 64),
                 reads=[prv, r_eps, rr[4]], writes=[rr[4]])
            P.op("dve", lambda e: e.reciprocal(Tm[:, :nt], Tm[:, :nt]), reads=[rr[4]], writes=[rr[4]])
            P.op("dve", lambda e, o=o: e.scalar_tensor_tensor(Y0[:, :nt], Y0[:, :nt], W["lng"][:, o:o + 1], Tm[:, :nt],
                                                              ALU.mult, ALU.mult),
                 reads=[rr[0], rr[4], W["r"]], writes=[rr[0]])
            P.op("dve", lambda e, o=o: e.scalar_tensor_tensor(Y0[:, :nt], Y0[:, :nt], W["lnb"][:, o:o + 1], Bv[:, :nt],
                                                              ALU.add, ALU.add),
                 reads=[rr[0], rr[2], W["r"]], writes=[rr[0]])
            P.op("pool", lambda e: e.tensor_tensor(Y0[:, :nt], Y0[:, :nt], Gg[:, :nt], ALU.mult),
                 reads=[rr[0], rr[3]], writes=[rr[0]])
            P.op("sp", lambda e, o=o, g0=g0, nt=nt: e.dma_start(out=ygdst(o, g0, nt), in_=Y0[:, :nt]),
                 reads=[rr[0]], dma=True)

    for g0 in range(0, T_lat, G):
        for o in range(HC_):
            one(g0, min(G, T_lat - g0), o)


def rwkv_host_weights(inp, half):
    c0 = 512 * half
    cs = slice(c0, c0 + 512)
    f = lambda a: np.ascontiguousarray(a.astype(np.float32))
    w_rkv = inp["rw_w_rkv"][0]
    lay = lambda w: f(w.reshape(KC, 128, -1).transpose(1, 0, 2))
    d = {}
    d["mix"] = f(inp["rw_mix"][0].reshape(6, KC, 128).transpose(2, 0, 1))
    d["wr"] = lay(w_rkv[0][:, cs]); d["wk"] = lay(w_rkv[1][:, cs]); d["wv"] = lay(w_rkv[2][:, cs])
    d["w1"] = f(inp["rw_w1"][0].reshape(2, KC, 128, 64).transpose(2, 0, 1, 3))
    d["a1"] = f(inp["rw_a1"][0].reshape(2, KC, 128, 64).transpose(2, 0, 1, 3))
    d["w2"] = f(inp["rw_w2"][0][:, :, cs].transpose(1, 0, 2))
    d["a2"] = f(inp["rw_a2"][0][:, :, cs].transpose(1, 0, 2))
    d["w0"] = f(inp["rw_w0"][0][:, cs][None])
    d["a0"] = f(inp["rw_a0"][0][:, cs].reshape(2, HC_, 128).transpose(2, 0, 1))
    d["g1"] = lay(inp["rw_g1"][0])
    d["g2a"] = f(inp["rw_g2"][0][:128, cs]); d["g2b"] = f(inp["rw_g2"][0][128:, cs])
    pv = lambda v: f(v[cs].reshape(HC_, 128).T)
    d["kk_"] = pv(inp["rw_k_k"][0]); d["ka_"] = pv(inp["rw_k_a"][0]); d["rk"] = pv(inp["rw_r_k"][0].reshape(-1))
    d["lng"] = pv(inp["rw_ln_g"][0]); d["lnb"] = pv(inp["rw_ln_b"][0])
    d["cm"] = const_masks()
    return d


RW_SHAPES = dict(mix=[128, 6, KC], wr=[128, KC, 512], wk=[128, KC, 512], wv=[128, KC, 512],
                 w1=[128, 2, KC, 64], a1=[128, 2, KC, 64], w2=[64, 2, 512], a2=[64, 2, 512],
                 w0=[1, 2, 512], a0=[128, 2, HC_], g1=[128, KC, 160], g2a=[128, 512], g2b=[32, 512],
                 kk_=[128, HC_], ka_=[128, HC_], rk=[128, HC_], lng=[128, HC_], lnb=[128, HC_], cm=[128, 6, 128])


def rwkv_phase(P, nc, wd, hl_src, hc_src, T_lat, T_ctx, ygdst, G=256):
    NTOT = T_ctx + T_lat
    dr = {}
    dr["OPS"] = nc.dram_tensor("rw_ops", [2, 4, HC_, 128, NTOT], F32).ap()
    dr["VT"] = nc.dram_tensor("rw_vt", [NTOT, 512], F32).ap()
    dr["WC"] = nc.dram_tensor("rw_wc", [2, HC_, 128, NTOT // 64], F32).ap()
    dr["GG"] = nc.dram_tensor("rw_gg", [128, HC_, T_lat], F32).ap()
    dr["BV"] = nc.dram_tensor("rw_bv", [128, HC_, T_lat], F32).ap()
    dr["Y"] = nc.dram_tensor("rw_y", [2, HC_, 128, T_lat], F32).ap()
    m0 = P.mark()
    S = rwkv_alloc(P, G)
    W = rwkv_load_weights(P, S, wd)
    m1 = P.mark()
    rwkv_prep_alloc(P, S, G)
    build_rwkv_prep(P, S, W, hc_src, T_ctx, True, 0, dr, T_lat)
    build_rwkv_prep(P, S, W, hl_src, T_lat, False, T_ctx, dr, T_lat)
    barrier(P)
    P.release(S.m_persist)
    build_rwkv_scan(P, S, dr, T_ctx // 64, T_lat // 64)
    barrier(P)
    P.release(S.m_persist)
    build_rwkv_post(P, S, W, dr, T_lat, ygdst)
    barrier(P)
    P.release(m0)


def build_rwkv_test(T_lat, T_ctx):
    nc = bass.Bass("TRN2", target_bir_lowering=False)
    hl = nc.dram_tensor("hl", [D, T_lat], F32, kind="ExternalInput").ap()
    hc = nc.dram_tensor("hc", [D, T_ctx], F32, kind="ExternalInput").ap()
    wd = {nm: nc.dram_tensor("rw_" + nm, shp, F32, kind="ExternalInput").ap() for nm, shp in RW_SHAPES.items()}
    yg = nc.dram_tensor("yg", [HC_, 128, T_lat], F32, kind="ExternalOutput").ap()
    P = Prog(nc)
    hl3 = hl.rearrange("(c p) t -> c p t", p=128)
    hc3 = hc.rearrange("(c p) t -> c p t", p=128)
    rwkv_phase(P, nc, wd, lambda c, t0, n: [(0, n, hl3[c, :, t0:t0 + n])], lambda c, t0, n: [(0, n, hc3[c, :, t0:t0 + n])],
               T_lat, T_ctx, lambda o, g0, nt: yg[o, :, g0:g0 + nt])
    P.finish()
    P.emit()
    return nc


LSEQ = 8192
NFFT = 2 * LSEQ
HY_BANDS = 16


def hyena_consts():
    f64 = np.float64
    n1 = np.arange(64, dtype=f64)[:, None]
    k = np.arange(128, dtype=f64)[None, :]
    n = np.arange(128, dtype=f64)[:, None]
    d = {}
    a = 2 * np.pi * n1 * k / 128
    d["FA"] = np.concatenate([np.cos(a), -np.sin(a)], axis=1)
    tw = 2 * np.pi * n * k / NFFT
    d["TWC"] = np.stack([np.cos(tw), np.cos(tw)], axis=1)
    d["TWS"] = np.stack([np.sin(tw), np.sin(tw)], axis=1)
    b = 2 * np.pi * n * k / 128
    d["FBC"] = np.cos(b); d["FBS"] = np.sin(b); d["FBNS"] = -np.sin(b)
    d["INV1"] = np.concatenate([np.cos(b), np.sin(b)], axis=1)
    d["INV2"] = np.concatenate([-np.sin(b), np.cos(b)], axis=1)
    c = 2 * np.pi * np.arange(128, dtype=f64)[:, None] * np.arange(64, dtype=f64)[None, :] / 128
    d["CNN"] = np.cos(c) / NFFT; d["NSNN"] = -np.sin(c) / NFFT
    f32 = np.float32
    L = LSEQ
    t = np.linspace(0.0, 1.0, L, dtype=f32)[:, None]
    ang = f32(2 * math.pi / L) * np.arange(L, dtype=f32)[:, None] * np.linspace(1e-4, HY_BANDS - 1, HY_BANDS, dtype=f32)[None]
    zf = np.concatenate([t, np.cos(ang), -np.sin(ang)], axis=-1).astype(f32)
    d["ZFT"] = zf.T
    d["TPOS"] = t.T
    return {k_: np.ascontiguousarray(v.astype(np.float32)) for k_, v in d.items()}


HYC_SHAPES = dict(FA=[64, 256], TWC=[128, 2, 128], TWS=[128, 2, 128], FBC=[128, 128], FBS=[128, 128],
                  FBNS=[128, 128], INV1=[128, 256], INV2=[128, 256], CNN=[128, 64], NSNN=[128, 64],
                  ZFT=[33, LSEQ], TPOS=[1, LSEQ])
HYW_SHAPES = dict(win=[128, KC, 1536], convw=[128, 3, 12], convb=[128, 12], fw1=[33, 64], fb1=[64, 1], ffreq=[64, 2],
                  fw2=[64, 64], fb2=[64, 1], fw3=[64, 1024], deltas=[128, 8], bias=[1, 1024])


def hyena_host_weights(inp, half):
    c0 = 512 * half
    f = lambda a: np.ascontiguousarray(np.asarray(a, dtype=np.float32))
    cols = np.concatenate([np.arange(c0, c0 + 512) + 1024 * i for i in range(3)])
    d = {}
    w_in = inp["hy_w_in"][0][:, cols]
    d["win"] = f(w_in.reshape(KC, 128, 1536).transpose(1, 0, 2))
    d["convw"] = f(inp["hy_conv_w"][0][:, cols].reshape(3, 12, 128).transpose(2, 0, 1))
    d["convb"] = f(inp["hy_conv_b"][0][cols].reshape(12, 128).T)
    d["fw1"] = f(inp["hy_f_w1"][0]); d["fb1"] = f(inp["hy_f_b1"][0][:, None])
    d["ffreq"] = f(inp["hy_f_freq"][0].T)
    d["fw2"] = f(inp["hy_f_w2"][0]); d["fb2"] = f(inp["hy_f_b2"][0][:, None])
    fcols = np.concatenate([np.arange(c0, c0 + 512) + 1024 * i for i in range(2)])
    d["fw3"] = f(inp["hy_f_w3"][0][:, fcols])
    d["deltas"] = f(inp["hy_deltas"][0][:, c0:c0 + 512].reshape(8, 128).T)
    d["bias"] = f(inp["hy_bias"][0][:, c0:c0 + 512].reshape(1, 1024))
    return d


def hyena_phase(P, nc, wd, cd, h_src, zdst):
    L = LSEQ
    m0 = P.mark()
    ew = EW()
    banks = Ring(P.banks())
    Ud = nc.dram_tensor("hy_ud", [12, 128, L + 2], F32).ap()
    UC = nc.dram_tensor("hy_uc", [12 * 128, L], F32).ap()
    FD = nc.dram_tensor("hy_fd", [8 * 128, L], F32).ap()
    Wt = {}
    rW = Res("hyW")
    i = 0
    for nm, shp in list(HYW_SHAPES.items()) + list(HYC_SHAPES.items()):
        if nm in ("ZFT", "TPOS", "bias", "win"):
            continue
        t = P.sb("hy_" + nm, shp)
        Wt[nm] = t
        src = wd[nm] if nm in wd else cd[nm]
        i += 1
        P.op("sp" if i % 2 else "pool", lambda e, t=t, src=src: e.dma_start(out=t[:], in_=src), writes=[Res()], dma=True)
    biasr = P.sb("hy_biasr", [64, 1024])
    P.op("sp", lambda e: e.dma_start(out=biasr[:], in_=wd["bias"].partition_broadcast(64)), writes=[Res()], dma=True)
    negpi = P.sb("hy_negpi", [128, 1])
    P.op("pool", lambda e: e.memset(negpi[:], -math.pi), writes=[Res()])
    barrier(P)
    fbs = P.sb("hy_fbs", [64, 2])
    nabsd = P.sb("hy_nabsd", [128, 8])
    P.op("dve", lambda e: e.tensor_tensor(fbs[:, 0:1], Wt["ffreq"][:, 0:1], Wt["fb1"][:, 0:1], ALU.mult), writes=[rW])
    P.op("dve", lambda e: e.tensor_tensor(fbs[:, 1:2], Wt["ffreq"][:, 1:2], Wt["fb2"][:, 0:1], ALU.mult), writes=[rW])
    P.op("dve", lambda e: e.tensor_scalar(nabsd[:], Wt["deltas"][:], -1.0, None, ALU.mult), writes=[rW])
    P.op("dve", lambda e: e.tensor_tensor(nabsd[:], nabsd[:], Wt["deltas"][:], ALU.max), reads=[rW], writes=[rW])
    P.op("dve", lambda e: e.tensor_scalar(nabsd[:], nabsd[:], -1.0, None, ALU.mult), reads=[rW], writes=[rW])
    m1 = P.mark()
    TBK = 512
    Wt["win"] = P.sb("hy_win", HYW_SHAPES["win"])
    P.op("sp", lambda e: e.dma_start(out=Wt["win"][:], in_=wd["win"]), writes=[rW], dma=True)
    XIN = P.sb("hy_xin", [128, KC, TBK]); r_XIN = [Res() for _ in range(KC)]
    UB = mk_ring([P.sb(f"hy_ub{i}", [128, 12, TBK]) for i in range(2)], 1)
    zt = P.sb("hy_zero", [128, 12, 1])
    P.op("pool", lambda e: e.memset(zt[:], 0.0), writes=[rW])
    P.op("sp", lambda e: e.dma_start(out=Ud[:, :, 0:1].rearrange("q p o -> p q o"), in_=zt[:], allow_slow_non_contiguous=True), reads=[rW], dma=True)
    P.op("sp", lambda e: e.dma_start(out=Ud[:, :, L + 1:L + 2].rearrange("q p o -> p q o"), in_=zt[:], allow_slow_non_contiguous=True), reads=[rW], dma=True)

    def d1_block(t0):
        for c in range(KC):
            P.op("sp" if c % 2 else "pool", lambda e, c=c: e.dma_start(out=XIN[:, c, :], in_=h_src(c, t0, TBK)),
                 writes=[r_XIN[c]], dma=True)
        ub, rub = UB.next(); rub = rub[0]
        for q in range(12):
            pt, pr = banks.next()
            for c in range(KC):
                P.op("pe", lambda e, pt=pt, q=q, c=c: e.matmul(pt[:, :], Wt["win"][:, c, q * 128:(q + 1) * 128], XIN[:, c, :],
                                                                start=(c == 0), stop=(c == KC - 1)),
                     reads=[r_XIN[c], rW], writes=[pr])
            P.op("act" if q % 2 else "dve",
                 (lambda e, pt=pt, q=q: e.activation(ub[:, q, :], pt[:, :], AF.Copy)) if q % 2 else
                 (lambda e, pt=pt, q=q: e.tensor_copy(ub[:, q, :], pt[:, :])), reads=[pr], writes=[rub])
        P.op("sp", lambda e: e.dma_start(out=Ud[:, :, 1 + t0:1 + t0 + TBK].rearrange("q p t -> p q t"), in_=ub[:]),
             reads=[rub], dma=True)
    for t0 in range(0, L, TBK):
        d1_block(t0)
    barrier(P)
    P.release(m1)
    CB = mk_ring([(P.sb(f"hy_cu{i}", [128, L + 2]), P.sb(f"hy_co{i}", [128, L])) for i in range(2)], 2)

    def conv_chunk(q):
        (cu, co), rr = CB.next()
        P.op("sp", lambda e: e.dma_start(out=cu[:], in_=Ud[q]), writes=[rr[0]], dma=True)
        cw = Wt["convw"]
        P.op("dve", lambda e: e.tensor_scalar(co[:], cu[:, 0:L], cw[:, 0, q:q + 1], Wt["convb"][:, q:q + 1], ALU.mult, ALU.add),
             reads=[rr[0], rW], writes=[rr[1]])
        P.op("dve", lambda e: e.scalar_tensor_tensor(co[:], cu[:, 1:L + 1], cw[:, 1, q:q + 1], co[:], ALU.mult, ALU.add),
             reads=[rr[0], rr[1]], writes=[rr[1]])
        P.op("dve", lambda e: e.scalar_tensor_tensor(co[:], cu[:, 2:L + 2], cw[:, 2, q:q + 1], co[:], ALU.mult, ALU.add),
             reads=[rr[0], rr[1]], writes=[rr[1]])
        P.op("pool", lambda e: e.dma_start(out=UC[q * 128:(q + 1) * 128, :], in_=co[:]), reads=[rr[1]], dma=True)
    for q in range(12):
        conv_chunk(q)
    barrier(P)
    P.release(m1)
    zft = P.sb("hy_zft", [33, TBK]); r_zft = Res()
    tps = P.sb("hy_tps", [128, TBK]); r_tps = Res()
    h1 = P.sb("hy_h1", [64, TBK]); r_h1 = Res()
    h2 = P.sb("hy_h2", [64, TBK]); r_h2 = Res()
    FB_ = mk_ring([(P.sb(f"hy_fw{i}", [128, TBK]), P.sb(f"hy_ff{i}", [128, TBK])) for i in range(2)], 2)
    TWO_PI = 2 * math.pi
    MAGIC = 12582912.0
    PI_LO = 3.1415925
    rr_ = P.sb("hy_rr", [64, TBK]); r_rr = Res()

    def sin_layer(pt, pr, dst, r_dst, k):
        P.op("dve", lambda e: e.tensor_scalar(dst[:], pt[:64, :], Wt["ffreq"][:, k:k + 1], fbs[:, k:k + 1], ALU.mult, ALU.add),
             reads=[pr, rW], writes=[r_dst])
        P.op("dve", lambda e: e.tensor_scalar(rr_[:], dst[:], 1.0 / TWO_PI, MAGIC, ALU.mult, ALU.add), reads=[r_dst], writes=[r_rr])
        P.op("dve", lambda e: e.tensor_scalar(rr_[:], rr_[:], MAGIC, None, ALU.subtract), reads=[r_rr], writes=[r_rr])
        P.op("dve", lambda e: e.scalar_tensor_tensor(dst[:], rr_[:], -TWO_PI, dst[:], ALU.mult, ALU.add), reads=[r_rr, r_dst], writes=[r_dst])
        P.op("dve", lambda e: e.tensor_scalar(dst[:], dst[:], PI_LO, -PI_LO, ALU.min, ALU.max), reads=[r_dst], writes=[r_dst])
        P.op("act", lambda e: e.activation(dst[:], dst[:], AF.Sin), reads=[r_dst], writes=[r_dst])

    def filt_block(t0):
        P.op("sp", lambda e: e.dma_start(out=zft[:], in_=cd["ZFT"][:, t0:t0 + TBK]), writes=[r_zft], dma=True)
        P.op("pool", lambda e: e.dma_start(out=tps[:], in_=cd["TPOS"][:, t0:t0 + TBK].partition_broadcast(128)),
             writes=[r_tps], dma=True)
        pt, pr = banks.next()
        P.op("pe", lambda e: e.matmul(pt[:64, :], Wt["fw1"][:, :], zft[:, :], start=True, stop=True), reads=[r_zft, rW], writes=[pr])
        sin_layer(pt, pr, h1, r_h1, 0)
        pt2, pr2 = banks.next()
        P.op("pe", lambda e: e.matmul(pt2[:64, :], Wt["fw2"][:, :], h1[:, :], start=True, stop=True), reads=[r_h1, rW], writes=[pr2])
        sin_layer(pt2, pr2, h2, r_h2, 1)
        for cc in range(8):
            (fw, ff), rr = FB_.next()
            pt3, pr3 = banks.next()
            P.op("pe", lambda e, cc=cc, pt3=pt3: e.matmul(pt3[:, :], Wt["fw3"][:, cc * 128:(cc + 1) * 128], h2[:, :], start=True, stop=True),
                 reads=[r_h2, rW], writes=[pr3])
            P.op("act", lambda e, cc=cc, fw=fw: e.activation(fw[:], tps[:], AF.Exp, scale=nabsd[:, cc:cc + 1]),
                 reads=[r_tps, rW], writes=[rr[0]])
            P.op("dve", lambda e, pt3=pt3, fw=fw, ff=ff: e.tensor_tensor(ff[:], pt3[:, :], fw[:], ALU.mult),
                 reads=[pr3, rr[0]], writes=[rr[1]])
            P.op("sp" if cc % 2 else "pool", lambda e, cc=cc, ff=ff: e.dma_start(out=FD[cc * 128:(cc + 1) * 128, t0:t0 + TBK], in_=ff[:]),
                 reads=[rr[1]], dma=True)
    for t0 in range(0, L, TBK):
        filt_block(t0)
    barrier(P)
    P.release(m1)
    CGR = 8
    NSET = 4
    IN_ = mk_ring([tuple(P.sb(f"hy_in{i}{j}", [64, CGR, 128]) for j in range(5)) for i in range(2)], 5)
    ZO = mk_ring([P.sb(f"hy_zo{i}", [64, CGR, 128]) for i in range(2)], 1)

    class BSet:
        def __init__(self, k):
            mk = lambda nm, shp: P.sb(f"hy_{nm}{k}", shp)
            self.PA = mk("pa", [128, 2, 2, 128]); self.r_PA = Res()
            self.Y1 = mk("y1", [128, 2, 2, 128]); self.r_Y1 = Res()
            self.TA = mk("ta", [128, 2, 128]); self.TB = mk("tb", [128, 2, 128]); self.r_T = [Res(), Res()]
            self.HT = [mk(f"ht{f}", [128, 2, 2, 128]) for f in range(2)]; self.r_HT = [Res(), Res()]
            self.XT = mk("xt", [128, 2, 2, 128]); self.r_XT = Res()
            self.ZZ = mk("zz", [128, 2, 2, 128]); self.r_ZZ = Res()
            self.PC = mk("pc", [128, 2, 2, 128]); self.r_PC = Res()
            self.TT = mk("tt", [128, 2, 2, 128]); self.r_TT = Res()
            self.Z1 = mk("z1", [64, 2, 128]); self.r_Z1 = Res()
            self.TE = mk("te", [64, 2, 128]); self.r_TE = Res()
    SETS = [BSet(k) for k in range(NSET)]

    def fwd_fft(B, src, r_src, dst, r_dst):
        PA, Y1, TA, TBb, r_PA, r_Y1, r_T = B.PA, B.Y1, B.TA, B.TB, B.r_PA, B.r_Y1, B.r_T
        pt, pr = banks.next()
        for s_ in range(2):
            P.op("pe", lambda e, s_=s_: e.matmul(pt[:, s_ * 256:(s_ + 1) * 256], src(s_), Wt["FA"][:, :], start=True, stop=True),
                 reads=[r_src, rW], writes=[pr])
        yield
        P.op("act", lambda e: e.activation(PA[:].rearrange("p s c k -> p (s c k)"), pt[:, :], AF.Copy), reads=[pr], writes=[r_PA])
        yield
        re = PA[:, :, 0, :]; im = PA[:, :, 1, :]
        c_, s2 = Wt["TWC"][:], Wt["TWS"][:]
        P.op(ew(), lambda e: e.tensor_tensor(TA[:], re, c_, ALU.mult), reads=[r_PA, rW], writes=[r_T[0]])
        P.op(ew(), lambda e: e.tensor_tensor(TBb[:], im, s2, ALU.mult), reads=[r_PA, rW], writes=[r_T[1]])
        yield
        P.op(ew(), lambda e: e.tensor_tensor(Y1[:, 0, :, :], TA[:], TBb[:], ALU.add), reads=r_T, writes=[r_Y1])
        yield
        P.op(ew(), lambda e: e.tensor_tensor(TA[:], im, c_, ALU.mult), reads=[r_PA, rW, r_Y1], writes=[r_T[0]])
        P.op(ew(), lambda e: e.tensor_tensor(TBb[:], re, s2, ALU.mult), reads=[r_PA, rW, r_Y1], writes=[r_T[1]])
        yield
        P.op(ew(), lambda e: e.tensor_tensor(Y1[:, 1, :, :], TA[:], TBb[:], ALU.subtract), reads=r_T, writes=[r_Y1])
        yield
        y_re = Y1[:, 0, :, :].rearrange("p s k -> p (s k)"); y_im = Y1[:, 1, :, :].rearrange("p s k -> p (s k)")
        pb, prb = banks.next()
        P.op("pe", lambda e: e.matmul(pb[:, 0:256], Wt["FBC"][:, :], y_re, start=True, stop=False), reads=[r_Y1, rW], writes=[prb])
        P.op("pe", lambda e: e.matmul(pb[:, 0:256], Wt["FBS"][:, :], y_im, start=False, stop=True), reads=[r_Y1, rW], writes=[prb])
        P.op("pe", lambda e: e.matmul(pb[:, 256:512], Wt["FBC"][:, :], y_im, start=True, stop=False), reads=[r_Y1, rW], writes=[prb])
        P.op("pe", lambda e: e.matmul(pb[:, 256:512], Wt["FBNS"][:, :], y_re, start=False, stop=True), reads=[r_Y1, rW], writes=[prb])
        yield
        P.op("act", lambda e: e.activation(dst[:].rearrange("p c s k -> p (c s k)"), pb[:, :], AF.Copy), reads=[prb], writes=[r_dst])
        yield

    def conv(B, src, r_src, f, out_psum_cb):
        XT, ZZ, PC, TT, TA, TBb = B.XT, B.ZZ, B.PC, B.TT, B.TA, B.TB
        r_XT, r_ZZ, r_PC, r_TT, r_T = B.r_XT, B.r_ZZ, B.r_PC, B.r_TT, B.r_T
        yield from fwd_fft(B, src, r_src, XT, r_XT)
        H = B.HT[f]; r_H = B.r_HT[f]
        xr, xi = XT[:, 0, :, :], XT[:, 1, :, :]
        hr, hi = H[:, 0, :, :], H[:, 1, :, :]
        P.op(ew(), lambda e: e.tensor_tensor(TA[:], xr, hr, ALU.mult), reads=[r_XT, r_H], writes=[r_T[0]])
        P.op(ew(), lambda e: e.tensor_tensor(TBb[:], xi, hi, ALU.mult), reads=[r_XT, r_H], writes=[r_T[1]])
        yield
        P.op(ew(), lambda e: e.tensor_tensor(ZZ[:, 0, :, :], TA[:], TBb[:], ALU.subtract), reads=r_T, writes=[r_ZZ])
        yield
        P.op(ew(), lambda e: e.tensor_tensor(TA[:], xr, hi, ALU.mult), reads=[r_XT, r_H, r_ZZ], writes=[r_T[0]])
        P.op(ew(), lambda e: e.tensor_tensor(TBb[:], xi, hr, ALU.mult), reads=[r_XT, r_H, r_ZZ], writes=[r_T[1]])
        yield
        P.op(ew(), lambda e: e.tensor_tensor(ZZ[:, 1, :, :], TA[:], TBb[:], ALU.add), reads=r_T, writes=[r_ZZ])
        yield
        pc, prc = banks.next()
        for s_ in range(2):
            P.op("pe", lambda e, s_=s_: e.matmul(pc[:, s_ * 256:(s_ + 1) * 256], ZZ[:, 0, s_, :], Wt["INV1"][:, :], start=True, stop=False),
                 reads=[r_ZZ, rW], writes=[prc])
            P.op("pe", lambda e, s_=s_: e.matmul(pc[:, s_ * 256:(s_ + 1) * 256], ZZ[:, 1, s_, :], Wt["INV2"][:, :], start=False, stop=True),
                 reads=[r_ZZ, rW], writes=[prc])
        yield
        P.op("act", lambda e: e.activation(PC[:].rearrange("p s c k -> p (s c k)"), pc[:, :], AF.Copy), reads=[prc], writes=[r_PC])
        yield
        re = PC[:, :, 0, :]; im = PC[:, :, 1, :]
        c_, s2 = Wt["TWC"][:], Wt["TWS"][:]
        P.op(ew(), lambda e: e.tensor_tensor(TA[:], re, c_, ALU.mult), reads=[r_PC, rW], writes=[r_T[0]])
        P.op(ew(), lambda e: e.tensor_tensor(TBb[:], im, s2, ALU.mult), reads=[r_PC, rW], writes=[r_T[1]])
        yield
        P.op(ew(), lambda e: e.tensor_tensor(TT[:, 0, :, :], TA[:], TBb[:], ALU.subtract), reads=r_T, writes=[r_TT])
        yield
        P.op(ew(), lambda e: e.tensor_tensor(TA[:], re, s2, ALU.mult), reads=[r_PC, rW, r_TT], writes=[r_T[0]])
        P.op(ew(), lambda e: e.tensor_tensor(TBb[:], im, c_, ALU.mult), reads=[r_PC, rW, r_TT], writes=[r_T[1]])
        yield
        P.op(ew(), lambda e: e.tensor_tensor(TT[:, 1, :, :], TA[:], TBb[:], ALU.add), reads=r_T, writes=[r_TT])
        yield
        pd, prd = banks.next()
        P.op("pe", lambda e: e.matmul(pd[:64, 0:256], Wt["CNN"][:, :], TT[:, 0, :, :].rearrange("p s k -> p (s k)"), start=True, stop=False),
             reads=[r_TT, rW], writes=[prd])
        P.op("pe", lambda e: e.matmul(pd[:64, 0:256], Wt["NSNN"][:, :], TT[:, 1, :, :].rearrange("p s k -> p (s k)"), start=False, stop=True),
             reads=[r_TT, rW], writes=[prd])
        yield
        out_psum_cb(pd, prd)
        yield

    def group(g):
        ch0 = g * CGR
        (F0, F1, X1, X2, Vv), rin = IN_.next()
        srcs = [FD[ch0:ch0 + CGR, :], FD[512 + ch0:512 + ch0 + CGR, :], UC[ch0:ch0 + CGR, :], UC[512 + ch0:512 + ch0 + CGR, :],
                UC[1024 + ch0:1024 + ch0 + CGR, :]]
        for k, (t, s_) in enumerate(zip((F0, F1, X1, X2, Vv), srcs)):
            P.op("sp" if k % 2 else "pool", lambda e, t=t, s_=s_: e.dma_start(out=t[:], in_=s_.rearrange("c (a b) -> a c b", b=128)),
                 writes=[rin[k]], dma=True)
        zo, rzo = ZO.next(); rzo = rzo[0]

        def pair(pi):
            B = SETS[pi % NSET]
            c_ = pi * 2
            Z1, TE, r_Z1, r_TE = B.Z1, B.TE, B.r_Z1, B.r_TE
            yield from fwd_fft(B, lambda s_: F0[:, c_ + s_, :], rin[0], B.HT[0], B.r_HT[0])
            yield from fwd_fft(B, lambda s_: F1[:, c_ + s_, :], rin[1], B.HT[1], B.r_HT[1])

            def ep1(pd, prd):
                for s_ in range(2):
                    col = ch0 + c_ + s_
                    P.op("dve", lambda e, s_=s_, col=col: e.scalar_tensor_tensor(TE[:, s_, :], Vv[:, c_ + s_, :], biasr[:, col:col + 1],
                                                                                 pd[:64, s_ * 128:(s_ + 1) * 128], ALU.mult, ALU.add),
                         reads=[rin[4], prd], writes=[r_TE])
                P.op("pool", lambda e: e.tensor_tensor(Z1[:], TE[:], X1[:, c_:c_ + 2, :], ALU.mult), reads=[r_TE, rin[2]], writes=[r_Z1])
            yield from conv(B, lambda s_: Vv[:, c_ + s_, :], rin[4], 0, ep1)

            def ep2(pd, prd):
                for s_ in range(2):
                    col = 512 + ch0 + c_ + s_
                    P.op("dve", lambda e, s_=s_, col=col: e.scalar_tensor_tensor(TE[:, s_, :], Z1[:, s_, :], biasr[:, col:col + 1],
                                                                                 pd[:64, s_ * 128:(s_ + 1) * 128], ALU.mult, ALU.add),
                         reads=[r_Z1, prd], writes=[r_TE])
                P.op("pool", lambda e: e.tensor_tensor(zo[:, c_:c_ + 2, :], TE[:], X2[:, c_:c_ + 2, :], ALU.mult),
                     reads=[r_TE, rin[3]], writes=[rzo])
            yield from conv(B, lambda s_: Z1[:, s_, :], r_Z1, 1, ep2)
        gens = [pair(pi) for pi in range(CGR // 2)]
        while gens:
            for g_ in list(gens):
                try:
                    next(g_)
                except StopIteration:
                    gens.remove(g_)
        P.op("sp", lambda e: e.dma_start(out=zdst[ch0:ch0 + CGR, :].rearrange("c (a b) -> a c b", b=128), in_=zo[:]),
             reads=[rzo], dma=True)
    for g in range(512 // CGR):
        group(g)
    barrier(P)
    P.release(m0)


def build_hyena_test():
    nc = bass.Bass("TRN2", target_bir_lowering=False)
    hl = nc.dram_tensor("hl", [D, LSEQ], F32, kind="ExternalInput").ap()
    wd = {nm: nc.dram_tensor("hy_" + nm, shp, F32, kind="ExternalInput").ap() for nm, shp in HYW_SHAPES.items()}
    cd = {nm: nc.dram_tensor("hc_" + nm, shp, F32, kind="ExternalInput").ap() for nm, shp in HYC_SHAPES.items()}
    zo = nc.dram_tensor("zo", [512, LSEQ], F32, kind="ExternalOutput").ap()
    P = Prog(nc)
    hl3 = hl.rearrange("(c p) t -> c p t", p=128)
    hyena_phase(P, nc, wd, cd, lambda c, t0, n: hl3[c, :, t0:t0 + n], zo)
    P.finish()
    P.emit()
    return nc


NT_CORE = 4096
NCTX_CORE = 128
T_LAT = 8192
T_CTX = 256
PAIRS = [[0, 1], [2, 3], [4, 5], [6, 7]]
SHARED = dict(modw0=(72, 128, KC, 128), modw1=(72, 128, KC, 128),
              wgu00=(FC, 128, 2, KC, 128), wgu01=(FC, 128, 2, KC, 128), wgu10=(FC, 128, 2, KC, 128), wgu11=(FC, 128, 2, KC, 128),
              wdn00=(KC, 128, FC, 128), wdn01=(KC, 128, FC, 128), wdn10=(KC, 128, FC, 128), wdn11=(KC, 128, FC, 128),
              wo=(KC, 128, KC, 128), wout=(KC, 128, KC, 128))
_LET = "abcdefgh"


def gather_shared(P, nc):
    out = {}
    for nm, shp in SHARED.items():
        n = int(np.prod(shp))
        m = n // 8 // 128
        src = nc.dram_tensor("sh_" + nm, [128, m], F32, kind="ExternalInput").ap()
        bnc = nc.dram_tensor("shb_" + nm, [128, m], F32)
        full = nc.dram_tensor("shg_" + nm, [8 * 128, m], F32)
        r1, r2 = Res(), Res()
        P.op("sp", lambda e, bnc=bnc, src=src: e.dma_start(out=bnc.ap(), in_=src), writes=[r1], dma=True)
        P.cc(lambda e, bnc=bnc, full=full: e.collective_compute("AllGather", ALU.bypass, replica_groups=[list(range(8))],
                                                               ins=[bnc.ap().opt()], outs=[full.ap().opt()]),
             reads=[r1], writes=[r2])
        dims = " ".join(_LET[i] for i in range(len(shp)))
        kw = {_LET[i]: shp[i] for i in range(1, len(shp))}
        view = full.ap().rearrange("x y -> (x y)").rearrange(f"({dims}) -> {dims}", **kw)
        out[nm] = (view, r2)
    return out


def pair_gather(P, nc, name, loc, shape):
    full = nc.dram_tensor(name, [2 * shape[0]] + list(shape[1:]), F32)
    r = Res()
    P.cc(lambda e: e.collective_compute("AllGather", ALU.bypass, replica_groups=PAIRS,
                                        ins=[loc.ap().opt()], outs=[full.ap().opt()]), writes=[r])
    return full, r


def with_res(P, res_list):
    return list(res_list)


def build_full(stop=None):
    nc = bass.Bass("TRN2", target_bir_lowering=False)
    TB = 512
    ext = lambda nm, shp: nc.dram_tensor(nm, list(shp), F32, kind="ExternalInput").ap()
    xT = ext("xT", [D, NT_CORE]); cxT = ext("cxT", [D, NCTX_CORE]); cvec = ext("cvec", [128, KC, 2])
    seld = ext("sel", [128, 2])
    modb = [ext("modb0", [128, 72]), ext("modb1", [128, 72])]
    normg = [ext("normg0", [128, 3 * KC]), ext("normg1", [128, 3 * KC])]
    finalg = ext("finalg", [128, KC])
    rwd = {nm: ext("rw_" + nm, shp) for nm, shp in RW_SHAPES.items()}
    hyd = {nm: ext("hy_" + nm, shp) for nm, shp in HYW_SHAPES.items()}
    hcd = {nm: ext("hc_" + nm, shp) for nm, shp in HYC_SHAPES.items()}
    outT = nc.dram_tensor("outT", [D, NT_CORE], F32, kind="ExternalOutput").ap()
    XLd = nc.dram_tensor("i_xl", [D, NT_CORE], F32).ap()
    HLloc = nc.dram_tensor("i_hl", [D, NT_CORE], F32)
    HCloc = nc.dram_tensor("i_hc", [D, NCTX_CORE], F32)
    YGloc = nc.dram_tensor("i_yg", [512, T_LAT], F32)
    HL1loc = nc.dram_tensor("i_hl1", [D, NT_CORE], F32)
    ZDloc = nc.dram_tensor("i_zd", [512, T_LAT], F32)

    P = Prog(nc)
    SH = gather_shared(P, nc)
    barrier(P)
    if stop == "W":
        for k_, nm in enumerate(("wout", "wo", "modw1", "modw0")):
            v_, r_ = SH[nm]
            P.op("sp", lambda e, v_=v_, k_=k_: e.dma_start(out=outT[:, k_ * 1024:(k_ + 1) * 1024].rearrange("(a p) (c k) -> a p c k", p=128, k=128),
                                                        in_=v_[0:8]), reads=[r_], dma=True)
        P.finish(); P.emit()
        return nc

    def row_setup(layer, ncol):
        C = row_common(P, TB)
        sc, r_sc = emit_silu_c(P, C, cvec, 2)
        ng, r_ng = load_vecs(P, normg[layer], 3 * KC, f"normg{layer}")
        mw, r_mw = SH[f"modw{layer}"]
        C.P_extra = [r_mw]
        mod, r_mod = emit_mod_g(C, mw, r_mw, modb[layer], sc, r_sc, 2, f"mod{layer}")
        vecs = [emit_modvecs(C, mod, r_mod, col, ng, r_ng, f"mv{layer}{col}") for col in range(ncol)]
        return C, mod, vecs

    def ffn(C, gs_k, sh_k, hg_k, r_vec, wname, nt):
        wg, r_wg = SH["wgu" + wname]
        wd_, r_wd = SH["wdn" + wname]
        emit_rstd(C, C.X, C.r_X, C.SQ, C.r_SQ, C.R, C.r_R, nt)
        emit_pre(C, C.X, C.r_X, C.XN, C.r_XN, C.R, C.r_R, gs_k, sh_k, r_vec, nt)
        emit_ffn(C, C.XN, C.r_XN, C.H, C.r_H, C.X, C.r_X, hg_k, r_vec, wg, wd_, nt, wres=[r_wg, r_wd])

    def shv(mod, k, col):
        return mod[:, (3 * k) * KC:(3 * k + 1) * KC, col]

    mA = P.mark()
    C, mod0, vecs0 = row_setup(0, 2)
    r_hl, r_hc, r_xl = Res(), Res(), Res()

    def blockA(src, t0, nt, col, xdst, hdst, r_hdst):
        gs, hg, r_vec = vecs0[col]
        xs = src.rearrange("(c p) t -> p c t", p=128)
        P.op("sp", lambda e: e.dma_start(out=C.X[:, :, :nt], in_=xs[:, :, t0:t0 + nt]), writes=[C.r_X], dma=True)
        ffn(C, gs[:, 0, :], shv(mod0, 0, col), hg[:, 0, :], r_vec, "00", nt)
        if xdst is not None:
            xd = xdst.rearrange("(c p) t -> p c t", p=128)
            P.op("sp", lambda e: e.dma_start(out=xd[:, :, t0:t0 + nt], in_=C.X[:, :, :nt]), reads=[C.r_X], writes=[r_xl], dma=True)
        emit_rstd(C, C.X, C.r_X, C.SQ, C.r_SQ, C.R, C.r_R, nt)
        emit_pre(C, C.X, C.r_X, C.XN, C.r_XN, C.R, C.r_R, gs[:, 1, :], shv(mod0, 1, col), r_vec, nt)
        hd = hdst.rearrange("(c p) t -> p c t", p=128)
        P.op("sp", lambda e: e.dma_start(out=hd[:, :, t0:t0 + nt], in_=C.XN[:, :, :nt]), reads=C.r_XN, writes=[r_hdst], dma=True)
    for t0 in range(0, NT_CORE, TB):
        blockA(xT, t0, TB, 0, XLd, HLloc.ap(), r_hl)
    blockA(cxT, 0, NCTX_CORE, 1, None, HCloc.ap(), r_hc)
    barrier(P)
    P.release(mA)
    HLg, r_HLg = pair_gather(P, nc, "g_hl", HLloc, [D, NT_CORE])
    HCg, r_HCg = pair_gather(P, nc, "g_hc", HCloc, [D, NCTX_CORE])
    barrier(P)
    if stop == "A":
        hw_ = NT_CORE // 2
        P.op("sp", lambda e: e.dma_start(out=outT[:, 0:hw_], in_=HLg.ap()[0:D, 0:hw_]), dma=True)
        P.op("sp", lambda e: e.dma_start(out=outT[:, hw_:2 * hw_], in_=HLg.ap()[D:2 * D, 0:hw_]), dma=True)
        P.finish(); P.emit()
        return nc
    hl4 = HLg.ap().rearrange("(h c p) t -> h c p t", h=2, p=128)
    hc4 = HCg.ap().rearrange("(h c p) t -> h c p t", h=2, p=128)

    def pieces(v4, hs):
        def f(c, t0, n):
            out = []
            t = t0
            while t < t0 + n:
                h = t // hs
                e_ = min(t0 + n, (h + 1) * hs)
                out.append((t - t0, e_ - t, v4[h, c, :, t - h * hs:e_ - h * hs]))
                t = e_
            return out
        return f
    ygl = YGloc.ap().rearrange("(o p) t -> o p t", p=128)
    rwkv_phase(P, nc, rwd, pieces(hl4, NT_CORE), pieces(hc4, NCTX_CORE), T_LAT, T_CTX,
               lambda o, g0, nt: ygl[o, :, g0:g0 + nt])
    YGg, r_YGg = pair_gather(P, nc, "g_yg", YGloc, [512, T_LAT])
    barrier(P)
    mC = P.mark()
    C, mod0, vecs0 = row_setup(0, 1)
    selt, r_sel = load_vecs(P, seld, 2, "sel")
    ng1, r_ng1 = load_vecs(P, normg[1], 3 * KC, "normg1b")
    sc1, r_sc1 = emit_silu_c(P, C, cvec, 2)
    mw1, r_mw1 = SH["modw1"]
    mod1, r_mod1 = emit_mod_g(C, mw1, r_mw1, modb[1], sc1, r_sc1, 2, "mod1c")
    vecs1 = [emit_modvecs(C, mod1, r_mod1, 0, ng1, r_ng1, "mv1c")]
    BL = mk_ring([(P.sb(f"bl{i}a", [128, TB]), P.sb(f"bl{i}b", [128, TB])) for i in range(2)], 2)

    def blend_in(G_ap, t0, nt):
        g3 = G_ap.rearrange("(c p) t -> c p t", p=128)
        for c in range(KC):
            (a0, a1), rr = BL.next()
            P.op("sp", lambda e, c=c, a0=a0: e.dma_start(out=a0[:, :nt], in_=g3[c, :, t0:t0 + nt]), writes=[rr[0]], dma=True)
            P.op("pool", lambda e, c=c, a1=a1: e.dma_start(out=a1[:, :nt], in_=g3[c, :, NT_CORE + t0:NT_CORE + t0 + nt]),
                 writes=[rr[1]], dma=True)
            P.op("pool", lambda e, a1=a1: e.tensor_scalar(a1[:, :nt], a1[:, :nt], selt[:, 1:2], None, ALU.mult),
                 reads=[rr[1], r_sel], writes=[rr[1]])
            P.op("dve", lambda e, c=c, a0=a0, a1=a1: e.scalar_tensor_tensor(C.XN[:, c, :nt], a0[:, :nt], selt[:, 0:1], a1[:, :nt],
                                                                            ALU.mult, ALU.add),
                 reads=[rr[0], rr[1], r_sel], writes=[C.r_XN[c]])

    def mixer_out(wname, gate, r_vec, nt):
        wv_, r_wv = SH[wname]

        def evac(o, pt, pr):
            P.op("dve", lambda e: e.scalar_tensor_tensor(C.X[:, o, :nt], pt[:, :nt], gate[:, o:o + 1], C.X[:, o, :nt], ALU.mult, ALU.add),
                 reads=[pr, r_vec, C.r_X], writes=[C.r_X])
        emit_proj(C, C.XN, C.r_XN, wv_, KC, KC, evac, nt, wres=[r_wv])
    xld3 = XLd.rearrange("(c p) t -> p c t", p=128)
    hl1d = HL1loc.ap().rearrange("(c p) t -> p c t", p=128)
    r_hl1 = Res()

    def blockC(t0):
        nt = TB
        gs0, hg0, rv0 = vecs0[0]
        gs1, hg1, rv1 = vecs1[0]
        P.op("sp", lambda e: e.dma_start(out=C.X[:, :, :nt], in_=xld3[:, :, t0:t0 + nt]), reads=[r_xl], writes=[C.r_X], dma=True)
        blend_in(YGg.ap(), t0, nt)
        mixer_out("wo", hg0[:, 1, :], rv0, nt)
        ffn(C, gs0[:, 2, :], shv(mod0, 2, 0), hg0[:, 2, :], rv0, "01", nt)
        ffn(C, gs1[:, 0, :], shv(mod1, 0, 0), hg1[:, 0, :], rv1, "10", nt)
        P.op("sp", lambda e: e.dma_start(out=xld3[:, :, t0:t0 + nt], in_=C.X[:, :, :nt]), reads=[C.r_X], writes=[r_xl], dma=True)
        emit_rstd(C, C.X, C.r_X, C.SQ, C.r_SQ, C.R, C.r_R, nt)
        emit_pre(C, C.X, C.r_X, C.XN, C.r_XN, C.R, C.r_R, gs1[:, 1, :], shv(mod1, 1, 0), rv1, nt)
        P.op("sp", lambda e: e.dma_start(out=hl1d[:, :, t0:t0 + nt], in_=C.XN[:, :, :nt]), reads=C.r_XN, writes=[r_hl1], dma=True)
    for t0 in range(0, NT_CORE, TB):
        blockC(t0)
    barrier(P)
    P.release(mC)
    HL1g, r_HL1g = pair_gather(P, nc, "g_hl1", HL1loc, [D, NT_CORE])
    barrier(P)
    h14 = HL1g.ap().rearrange("(h c p) t -> h c p t", h=2, p=128)
    hyena_phase(P, nc, hyd, hcd, lambda c, t0, n: h14[t0 // NT_CORE, c, :, t0 % NT_CORE:t0 % NT_CORE + n], ZDloc.ap())
    ZDg, r_ZDg = pair_gather(P, nc, "g_zd", ZDloc, [512, T_LAT])
    barrier(P)
    mE = P.mark()
    C = row_common(P, TB)
    selt, r_sel = load_vecs(P, seld, 2, "sel2")
    ng1, r_ng1 = load_vecs(P, normg[1], 3 * KC, "normg1e")
    fg, r_fg = load_vecs(P, finalg, KC, "finalg")
    sc1, r_sc1 = emit_silu_c(P, C, cvec, 2)
    mod1, r_mod1 = emit_mod_g(C, mw1, r_mw1, modb[1], sc1, r_sc1, 2, "mod1e")
    vecs1 = [emit_modvecs(C, mod1, r_mod1, 0, ng1, r_ng1, "mv1e")]
    BL = mk_ring([(P.sb(f"bm{i}a", [128, TB]), P.sb(f"bm{i}b", [128, TB])) for i in range(2)], 2)
    od3 = outT.rearrange("(c p) t -> p c t", p=128)

    def blockE(t0):
        nt = TB
        gs1, hg1, rv1 = vecs1[0]
        P.op("sp", lambda e: e.dma_start(out=C.X[:, :, :nt], in_=xld3[:, :, t0:t0 + nt]), reads=[r_xl], writes=[C.r_X], dma=True)
        blend_in(ZDg.ap(), t0, nt)
        mixer_out("wout", hg1[:, 1, :], rv1, nt)
        ffn(C, gs1[:, 2, :], shv(mod1, 2, 0), hg1[:, 2, :], rv1, "11", nt)
        emit_rstd(C, C.X, C.r_X, C.SQ, C.r_SQ, C.R, C.r_R, nt)
        for c in range(KC):
            P.op("dve", lambda e, c=c: e.scalar_tensor_tensor(C.XN[:, c, :nt], C.X[:, c, :nt], fg[:, c:c + 1], C.R[:, :nt], ALU.mult, ALU.mult),
                 reads=[C.r_X, C.r_R, r_fg], writes=[C.r_XN[c]])
        P.op("sp", lambda e: e.dma_start(out=od3[:, :, t0:t0 + nt], in_=C.XN[:, :, :nt]), reads=C.r_XN, dma=True)
    for t0 in range(0, NT_CORE, TB):
        blockE(t0)
    P.finish()
    P.emit()
    return nc


def emit_mod_g(C, modw_v, r_modw, modb_d, sc_t, sc_r, ncol, name):
    P = C.P
    mod = P.sb(name, [128, 72, ncol])
    modb, r_modb = load_vecs(P, modb_d, 72, name + "_b")
    r_mod = Res(name)
    for j in range(72):
        wt, wr = C.wsq.next()
        P.op(C.dma_eng(), lambda e, wt=wt, j=j: e.dma_start(out=wt[:], in_=modw_v[j]), reads=[r_modw], writes=[wr], dma=True)
        pt, pr = C.psum.next()
        for c in range(KC):
            P.op("pe", lambda e, pt=pt, wt=wt, c=c: e.matmul(pt[:, 0:ncol], wt[:, c, :], sc_t[:, c, :],
                                                             start=(c == 0), stop=(c == KC - 1)),
                 reads=[wr, sc_r], writes=[pr])
        P.op("dve", lambda e, pt=pt, j=j: e.tensor_scalar(mod[:, j, :], pt[:, 0:ncol], modb[:, j:j + 1], None, ALU.add),
             reads=[pr, r_modb], writes=[r_mod])
    return mod, r_mod


_NC_CACHE = {}


def kernel_fused(**inp):
    inp = {k: np.asarray(v) for k, v in inp.items()}
    f = lambda a: np.ascontiguousarray(np.asarray(a, dtype=np.float32))
    shared = {}
    for l in range(2):
        shared[f"modw{l}"] = wlay(inp["mod_w"][l])
        for i in range(2):
            shared[f"wgu{l}{i}"] = wgulay(inp["ffn_w_gu"][l, i])
            shared[f"wdn{l}{i}"] = wlay(inp["ffn_w_down"][l, i])
    shared["wo"] = wlay(inp["rw_w_o"][0])
    shared["wout"] = wlay(inp["hy_w_out"][0])
    shards = {nm: a.reshape(8, 128, -1) for nm, a in shared.items()}
    hyc = hyena_consts()
    rww = [rwkv_host_weights(inp, j) for j in range(2)]
    hyw = [hyena_host_weights(inp, j) for j in range(2)]
    in_maps = []
    for c in range(NCORES):
        b, j = c // 2, c % 2
        m = {}
        m["xT"] = fm(inp["x"][b, j * NT_CORE:(j + 1) * NT_CORE])
        m["cxT"] = fm(inp["ctx"][b, j * NCTX_CORE:(j + 1) * NCTX_CORE])
        m["cvec"] = f(np.stack([pvec(inp["c"][b]), pvec(inp["c_ctx"])], axis=-1))
        m["sel"] = f(np.tile(np.array([[1.0 - j, float(j)]], np.float32), (128, 1)))
        for l in range(2):
            m[f"modb{l}"] = pvec(f(inp["mod_b"][l]))
            m[f"normg{l}"] = pvec(f(inp["norm_g"][l]).reshape(-1))
        m["finalg"] = pvec(f(inp["final_g"]))
        for nm, a in shards.items():
            m["sh_" + nm] = f(a[c])
        for nm, a in rww[j].items():
            m["rw_" + nm] = a
        for nm, a in hyw[j].items():
            m["hy_" + nm] = a
        for nm, a in hyc.items():
            m["hc_" + nm] = a
        in_maps.append(m)
    if "nc" not in _NC_CACHE:
        import os
        _NC_CACHE["nc"] = build_full(os.environ.get("KSTOP"))
    res = run_bass_kernel_spmd(_NC_CACHE["nc"], in_maps, core_ids=list(range(NCORES)))
    out = np.empty((4, T_LAT, D), np.float32)
    for c in range(NCORES):
        b, j = c // 2, c % 2
        out[b, j * NT_CORE:(j + 1) * NT_CORE, :] = res.results[c]["outT"].T
    return out


def _ffn_ext(C, gs_k, sh_k, hg_k, r_vec, wg, wd_, nt):
    emit_rstd(C, C.X, C.r_X, C.SQ, C.r_SQ, C.R, C.r_R, nt)
    emit_pre(C, C.X, C.r_X, C.XN, C.r_XN, C.R, C.r_R, gs_k, sh_k, r_vec, nt)
    emit_ffn(C, C.XN, C.r_XN, C.H, C.r_H, C.X, C.r_X, hg_k, r_vec, wg, wd_, nt)


def _shv(mod, k, col):
    return mod[:, (3 * k) * KC:(3 * k + 1) * KC, col]


def _mixer_out(C, P, wv_, gate, r_vec, nt):
    def evac(o, pt, pr):
        P.op("dve", lambda e: e.scalar_tensor_tensor(C.X[:, o, :nt], pt[:, :nt], gate[:, o:o + 1], C.X[:, o, :nt], ALU.mult, ALU.add),
             reads=[pr, r_vec, C.r_X], writes=[C.r_X])
    emit_proj(C, C.XN, C.r_XN, wv_, KC, KC, evac, nt)


def build_lC(NT, TB=512):
    nc = bass.Bass("TRN2", target_bir_lowering=False)
    ext = lambda nm, shp: nc.dram_tensor(nm, list(shp), F32, kind="ExternalInput").ap()
    xT = ext("xT", [D, NT]); ygT = ext("ygT", [D, NT]); cvec = ext("cvec", [128, KC, 2])
    mod0d = ext("mod0", [128, 72, 2]); modw1 = ext("modw1", [72, 128, KC, 128]); modb1 = ext("modb1", [128, 72])
    normg0 = ext("normg0", [128, 3 * KC]); normg1 = ext("normg1", [128, 3 * KC])
    wo = ext("wo", [KC, 128, KC, 128])
    wgu01 = ext("wgu01", [FC, 128, 2, KC, 128]); wdn01 = ext("wdn01", [KC, 128, FC, 128])
    wgu10 = ext("wgu10", [FC, 128, 2, KC, 128]); wdn10 = ext("wdn10", [KC, 128, FC, 128])
    xo = nc.dram_tensor("xo", [D, NT], F32, kind="ExternalOutput").ap()
    h1o = nc.dram_tensor("h1o", [D, NT], F32, kind="ExternalOutput").ap()
    mod1o = nc.dram_tensor("mod1o", [128, 72, 2], F32, kind="ExternalOutput").ap()
    P = Prog(nc)
    C = row_common(P, TB)
    mod0 = P.sb("mod0t", [128, 72, 2]); r_mod0 = Res()
    P.op("sp", lambda e: e.dma_start(out=mod0[:], in_=mod0d), writes=[r_mod0], dma=True)
    ng0, r_ng0 = load_vecs(P, normg0, 3 * KC, "ng0")
    ng1, r_ng1 = load_vecs(P, normg1, 3 * KC, "ng1")
    sc, r_sc = emit_silu_c(P, C, cvec, 2)
    mod1, r_mod1 = emit_mod(C, modw1, modb1, sc, r_sc, 2, "mod1")
    P.op("sp", lambda e: e.dma_start(out=mod1o, in_=mod1[:]), reads=[r_mod1], dma=True)
    gs0, hg0, rv0 = emit_modvecs(C, mod0, r_mod0, 0, ng0, r_ng0, "mv0")
    gs1, hg1, rv1 = emit_modvecs(C, mod1, r_mod1, 0, ng1, r_ng1, "mv1")
    xs = xT.rearrange("(c p) t -> p c t", p=128); ys = ygT.rearrange("(c p) t -> p c t", p=128)
    xd = xo.rearrange("(c p) t -> p c t", p=128); hd = h1o.rearrange("(c p) t -> p c t", p=128)

    def block(t0, nt):
        P.op("sp", lambda e: e.dma_start(out=C.X[:, :, :nt], in_=xs[:, :, t0:t0 + nt]), writes=[C.r_X], dma=True)
        P.op("sp", lambda e: e.dma_start(out=C.XN[:, :, :nt], in_=ys[:, :, t0:t0 + nt]), writes=C.r_XN, dma=True)
        _mixer_out(C, P, wo, hg0[:, 1, :], rv0, nt)
        _ffn_ext(C, gs0[:, 2, :], _shv(mod0, 2, 0), hg0[:, 2, :], rv0, wgu01, wdn01, nt)
        _ffn_ext(C, gs1[:, 0, :], _shv(mod1, 0, 0), hg1[:, 0, :], rv1, wgu10, wdn10, nt)
        P.op("sp", lambda e: e.dma_start(out=xd[:, :, t0:t0 + nt], in_=C.X[:, :, :nt]), reads=[C.r_X], dma=True)
        emit_rstd(C, C.X, C.r_X, C.SQ, C.r_SQ, C.R, C.r_R, nt)
        emit_pre(C, C.X, C.r_X, C.XN, C.r_XN, C.R, C.r_R, gs1[:, 1, :], _shv(mod1, 1, 0), rv1, nt)
        P.op("sp", lambda e: e.dma_start(out=hd[:, :, t0:t0 + nt], in_=C.XN[:, :, :nt]), reads=C.r_XN, dma=True)
    for t0 in range(0, NT, TB):
        block(t0, min(TB, NT - t0))
    P.finish()
    P.emit()
    return nc


def build_lE(NT, TB=512):
    nc = bass.Bass("TRN2", target_bir_lowering=False)
    ext = lambda nm, shp: nc.dram_tensor(nm, list(shp), F32, kind="ExternalInput").ap()
    xT = ext("xT", [D, NT]); zT = ext("zT", [D, NT])
    mod1d = ext("mod1", [128, 72, 2]); normg1 = ext("normg1", [128, 3 * KC]); finalg = ext("finalg", [128, KC])
    wout = ext("wout", [KC, 128, KC, 128])
    wgu11 = ext("wgu11", [FC, 128, 2, KC, 128]); wdn11 = ext("wdn11", [KC, 128, FC, 128])
    outT = nc.dram_tensor("outT", [D, NT], F32, kind="ExternalOutput").ap()
    P = Prog(nc)
    C = row_common(P, TB)
    mod1 = P.sb("mod1t", [128, 72, 2]); r_mod1 = Res()
    P.op("sp", lambda e: e.dma_start(out=mod1[:], in_=mod1d), writes=[r_mod1], dma=True)
    ng1, r_ng1 = load_vecs(P, normg1, 3 * KC, "ng1")
    fg, r_fg = load_vecs(P, finalg, KC, "fg")
    gs1, hg1, rv1 = emit_modvecs(C, mod1, r_mod1, 0, ng1, r_ng1, "mv1")
    xs = xT.rearrange("(c p) t -> p c t", p=128); zs = zT.rearrange("(c p) t -> p c t", p=128)
    od = outT.rearrange("(c p) t -> p c t", p=128)

    def block(t0, nt):
        P.op("sp", lambda e: e.dma_start(out=C.X[:, :, :nt], in_=xs[:, :, t0:t0 + nt]), writes=[C.r_X], dma=True)
        P.op("sp", lambda e: e.dma_start(out=C.XN[:, :, :nt], in_=zs[:, :, t0:t0 + nt]), writes=C.r_XN, dma=True)
        _mixer_out(C, P, wout, hg1[:, 1, :], rv1, nt)
        _ffn_ext(C, gs1[:, 2, :], _shv(mod1, 2, 0), hg1[:, 2, :], rv1, wgu11, wdn11, nt)
        emit_rstd(C, C.X, C.r_X, C.SQ, C.r_SQ, C.R, C.r_R, nt)
        for c in range(KC):
            P.op("dve", lambda e, c=c: e.scalar_tensor_tensor(C.XN[:, c, :nt], C.X[:, c, :nt], fg[:, c:c + 1], C.R[:, :nt], ALU.mult, ALU.mult),
                 reads=[C.r_X, C.r_R, r_fg], writes=[C.r_XN[c]])
        P.op("sp", lambda e: e.dma_start(out=od[:, :, t0:t0 + nt], in_=C.XN[:, :, :nt]), reads=C.r_XN, dma=True)
    for t0 in range(0, NT, TB):
        block(t0, min(TB, NT - t0))
    P.finish()
    P.emit()
    return nc


def kernel_multi(**inp):
    inp = {k: np.asarray(v) for k, v in inp.items()}
    f = lambda a: np.ascontiguousarray(np.asarray(a, dtype=np.float32))
    cores = list(range(NCORES))
    run = lambda nc, ims: run_bass_kernel_spmd(nc, ims, core_ids=cores).results
    cv = [f(np.stack([pvec(inp["c"][c // 2]), pvec(inp["c_ctx"])], axis=-1)) for c in cores]
    ng = [pvec(f(inp["norm_g"][l]).reshape(-1)) for l in range(2)]
    wA = dict(modw=wlay(inp["mod_w"][0]), modb=pvec(f(inp["mod_b"][0])), normg=ng[0],
              wgu=wgulay(inp["ffn_w_gu"][0, 0]), wdn=wlay(inp["ffn_w_down"][0, 0]))
    ims = []
    for c in cores:
        b, j = c // 2, c % 2
        m = dict(wA)
        m["xT"] = fm(inp["x"][b, j * NT_CORE:(j + 1) * NT_CORE]); m["cxT"] = fm(inp["ctx"][b, j * NCTX_CORE:(j + 1) * NCTX_CORE])
        m["cvec"] = cv[c]
        ims.append(m)
    rA = run(build_l1(NT_CORE, NCTX_CORE), ims)
    del wA, ims
    rww = [rwkv_host_weights(inp, j) for j in range(2)]
    ims = []
    for c in cores:
        b, j = c // 2, c % 2
        m = {"rw_" + k: v for k, v in rww[j].items()}
        m["hl"] = np.ascontiguousarray(np.concatenate([rA[2 * b]["ho"], rA[2 * b + 1]["ho"]], axis=1))
        m["hc"] = np.ascontiguousarray(np.concatenate([rA[2 * b]["hco"], rA[2 * b + 1]["hco"]], axis=1))
        ims.append(m)
    rB = run(build_rwkv_test(T_LAT, T_CTX), ims)
    del ims
    wC = dict(modw1=wlay(inp["mod_w"][1]), modb1=pvec(f(inp["mod_b"][1])), normg0=ng[0], normg1=ng[1],
              wo=wlay(inp["rw_w_o"][0]), wgu01=wgulay(inp["ffn_w_gu"][0, 1]), wdn01=wlay(inp["ffn_w_down"][0, 1]),
              wgu10=wgulay(inp["ffn_w_gu"][1, 0]), wdn10=wlay(inp["ffn_w_down"][1, 0]))
    ims = []
    for c in cores:
        b, j = c // 2, c % 2
        m = dict(wC)
        m["xT"] = rA[c]["xo"]
        yg = np.concatenate([rB[2 * b]["yg"].reshape(512, T_LAT), rB[2 * b + 1]["yg"].reshape(512, T_LAT)], axis=0)
        m["ygT"] = np.ascontiguousarray(yg[:, j * NT_CORE:(j + 1) * NT_CORE])
        m["cvec"] = cv[c]; m["mod0"] = rA[c]["modo"]
        ims.append(m)
    del rB
    rC = run(build_lC(NT_CORE), ims)
    del wC, ims, rA
    hyc = {"hc_" + k: v for k, v in hyena_consts().items()}
    hyw = [hyena_host_weights(inp, j) for j in range(2)]
    ims = []
    for c in cores:
        b, j = c // 2, c % 2
        m = {"hy_" + k: v for k, v in hyw[j].items()}
        m.update(hyc)
        m["hl"] = np.ascontiguousarray(np.concatenate([rC[2 * b]["h1o"], rC[2 * b + 1]["h1o"]], axis=1))
        ims.append(m)
    rD = run(build_hyena_test(), ims)
    del ims
    wE = dict(normg1=ng[1], finalg=pvec(f(inp["final_g"])), wout=wlay(inp["hy_w_out"][0]),
              wgu11=wgulay(inp["ffn_w_gu"][1, 1]), wdn11=wlay(inp["ffn_w_down"][1, 1]))
    ims = []
    for c in cores:
        b, j = c // 2, c % 2
        m = dict(wE)
        m["xT"] = rC[c]["xo"]; m["mod1"] = rC[c]["mod1o"]
        z = np.concatenate([rD[2 * b]["zo"], rD[2 * b + 1]["zo"]], axis=0)
        m["zT"] = np.ascontiguousarray(z[:, j * NT_CORE:(j + 1) * NT_CORE])
        ims.append(m)
    del rD
    rE = run(build_lE(NT_CORE), ims)
    out = np.empty((4, T_LAT, D), np.float32)
    for c in cores:
        b, j = c // 2, c % 2
        out[b, j * NT_CORE:(j + 1) * NT_CORE, :] = rE[c]["outT"].T
    return out


def kernel(**inp):
    return kernel_multi(**inp)
```

```python
import math
from contextlib import ExitStack

import numpy as np
import concourse.bass as bass
import concourse.mybir as mybir
from concourse.bass_utils import run_bass_kernel_spmd

F32 = mybir.dt.float32
AF = mybir.ActivationFunctionType
ALU = mybir.AluOpType
AX = mybir.AxisListType

D = 1024
KC = D // 128
FF = 2816
FC = FF // 128
NCORES = 8
NORM_EPS = 1e-6
GN_EPS = 64e-5

ENGS = ("pe", "act", "dve", "pool", "sp")
EPOCH = 30000
DMA_EPOCH = 1800
ARENA_KB = 204
NO_POOL = True


class Res:
    __slots__ = ("w", "rs", "name")

    def __init__(self, name=""):
        self.w = None
        self.rs = {}
        self.name = name


class Prog:
    def __init__(self, nc, n_dma=8):
        self.nc = nc
        self.streams = {e: [] for e in ENGS}
        self.cnt = {e: 0 for e in ENGS}
        self.known = {e: {} for e in ENGS}
        self.n_dma = n_dma
        self.dma_rr = {}
        self.dma_cnt = {}
        self.semkeys = {}
        self.stack = ExitStack()
        self.last_dma = {}
        self.sb_off = 0
        self.sb_id = 0
        self.arena = None
        self.n_cc = 0
        self._banks = None

    def sb(self, name, shape, dtype=F32):
        if self.arena is None:
            self.arena = self.stack.enter_context(self.nc.sbuf_tensor("arena", [128, ARENA_KB * 256], F32))
        n = 1
        for d in shape[1:]:
            n *= d
        n = (n + 7) // 8 * 8
        off = self.sb_off
        assert off + n <= ARENA_KB * 256, (name, off, n)
        self.sb_off = off + n
        v = self.arena[0:shape[0], off:off + n]
        dims = " ".join(f"d{i}" for i in range(1, len(shape)))
        if len(shape) > 2:
            kw = {f"d{i}": shape[i] for i in range(2, len(shape))}
            v = v[:, 0:int(np.prod(shape[1:]))].rearrange(f"p ({dims}) -> p {dims}", **kw)
        else:
            v = v[:, 0:shape[1]]
        return v

    def mark(self):
        return self.sb_off

    def release(self, m):
        self.sb_off = m

    def banks(self):
        if self._banks is None:
            self._banks = [self.stack.enter_context(self.nc.psum_tensor(f"pp_bank{i}", [128, 512], F32))
                           for i in range(8)]
        return self._banks

    def _key(self, key):
        if key not in self.semkeys:
            self.semkeys[key] = None
        return key

    def op(self, eng, fn, reads=(), writes=(), dma=False):
        if NO_POOL:
            if dma:
                eng = "sp"
            elif eng == "pool":
                eng = "dve"
        waits = {}
        known = self.known[eng]

        def need(dep):
            if dep is None:
                return
            key, val = dep
            if eng == "pe" and key[0] == "pe":
                return
            if known.get(key, 0) >= val:
                return
            if waits.get(key, 0) < val:
                waits[key] = val

        for r in reads:
            need(r.w)
        for w in writes:
            need(w.w)
            for k, v in w.rs.items():
                need((k, v))
        if dma:
            i = self.dma_rr.get(eng, 0)
            self.dma_rr[eng] = (i + 1) % self.n_dma
            n = self.dma_cnt.get((eng, i), 0)
            ep, idx = divmod(n, DMA_EPOCH)
            key = self._key(("dma", eng, i, ep))
            if idx > 0:
                need((key, idx * 16))
            elif ep > 0:
                need((("dma", eng, i, ep - 1), DMA_EPOCH * 16))
            self.dma_cnt[(eng, i)] = n + 1
            done = (key, (idx + 1) * 16)
            self.last_dma[(eng, i, ep)] = done
        else:
            n = self.cnt[eng]
            ep, idx = divmod(n, EPOCH)
            key = self._key((eng, ep))
            self.cnt[eng] = n + 1
            done = (key, idx + 1)
        for k, v in waits.items():
            known[k] = v
        self.streams[eng].append((list(waits.items()), fn, done, dma))
        for r in reads:
            r.rs[done[0]] = done[1]
        for w in writes:
            w.w = done
            w.rs = {}
        return done

    def cc(self, fn, reads=(), writes=()):
        eng = "pool"
        waits = {}
        known = self.known[eng]

        def need(dep):
            if dep is None:
                return
            key, val = dep
            if known.get(key, 0) >= val:
                return
            if waits.get(key, 0) < val:
                waits[key] = val
        for r in reads:
            need(r.w)
        for w in writes:
            need(w.w)
            for k, v in w.rs.items():
                need((k, v))
        if self.n_cc > 0:
            need((("cc", self.n_cc), 1))
        self.n_cc += 1
        key = self._key(("cc", self.n_cc))
        done = (key, 1)
        self.last_dma[("cc", self.n_cc)] = done
        for k, v in waits.items():
            known[k] = v
        self.streams[eng].append((list(waits.items()), fn, done, "cc"))
        for r in reads:
            r.rs[done[0]] = done[1]
        for w in writes:
            w.w = done
            w.rs = {}
        return done

    def finish(self):
        waits = [(k, v) for k, v in self.last_dma.values()]
        self.streams["sp"].append((waits, None, None, False))

    def emit(self):
        nc = self.nc
        sems = {}
        for key in self.semkeys:
            nm = "s_" + "_".join(str(x) for x in key)
            sems[key] = self.stack.enter_context(nc.semaphore(nm))
        streams = self.streams

        def run(name, e):
            for waits, fn, done, dma in streams[name]:
                for k, v in waits:
                    e.wait_ge(sems[k], v)
                if fn is None:
                    continue
                ins = fn(e)
                if dma == "cc":
                    ins.then_inc(sems[done[0]])
                else:
                    ins.then_inc(sems[done[0]], 16 if dma else 1)

        with nc.Block() as block:
            @block.tensor
            def _(e):
                run("pe", e)

            @block.scalar
            def _(e):
                run("act", e)

            @block.vector
            def _(e):
                run("dve", e)

            @block.gpsimd
            def _(e):
                run("pool", e)

            @block.sync
            def _(e):
                run("sp", e)
        self.stack.close()


class Ring:
    def __init__(self, tiles):
        self.tiles = tiles
        self.res = [Res() for _ in tiles]
        self.i = 0

    def next(self):
        i = self.i
        self.i = (i + 1) % len(self.tiles)
        return self.tiles[i], self.res[i]


def fm(a):
    return np.ascontiguousarray(a.T)


def wlay(w):
    K, N = w.shape
    kc, oc = K // 128, N // 128
    return np.ascontiguousarray(w.reshape(kc, 128, oc, 128).transpose(2, 1, 0, 3))


def wgulay(w):
    a = wlay(w)
    return np.ascontiguousarray(np.stack([a[:FC], a[FC:]], axis=2))


def pvec(v):
    return np.ascontiguousarray(v.reshape(-1, 128).T)


class RowCtx:
    def __init__(self, P, TB):
        self.P = P
        self.TB = TB
        nc = P.nc
        self.ones = P.sb("ones", [128, 128])
        self.r_ones = Res()
        P.op("pool", lambda e: e.memset(self.ones[:], 1.0), writes=[self.r_ones])
        self.psum = Ring(P.banks())
        self.wgu = Ring([P.sb(f"wgu{i}", [128, 2, KC, 128]) for i in range(3)])
        self.wdn = Ring([P.sb(f"wdn{i}", [128, FC, 128]) for i in range(2)])
        self.wsq = Ring([P.sb(f"wsq{i}", [128, KC, 128]) for i in range(3)])
        self.dq = 0

    def dma_eng(self):
        self.dq += 1
        return "sp" if self.dq % 2 else "pool"


def load_vecs(P, dram_ap, n, name):
    t = P.sb(name, [128, n])
    r = Res(name)
    P.op("sp", lambda e: e.dma_start(out=t[:], in_=dram_ap), writes=[r], dma=True)
    return t, r


def emit_mod(C, modw_d, modb_d, sc_t, sc_r, ncol, name):
    P = C.P
    mod = P.sb(name, [128, 72, ncol])
    modb, r_modb = load_vecs(P, modb_d, 72, name + "_b")
    r_mod = Res(name)
    for j in range(72):
        wt, wr = C.wsq.next()
        P.op(C.dma_eng(), lambda e, wt=wt, j=j: e.dma_start(out=wt[:], in_=modw_d[j]), writes=[wr], dma=True)
        pt, pr = C.psum.next()
        for c in range(KC):
            P.op("pe", lambda e, pt=pt, wt=wt, c=c: e.matmul(pt[:, 0:ncol], wt[:, c, :], sc_t[:, c, :],
                                                             start=(c == 0), stop=(c == KC - 1)),
                 reads=[wr, sc_r], writes=[pr])
        P.op("dve", lambda e, pt=pt, j=j: e.tensor_scalar(mod[:, j, :], pt[:, 0:ncol], modb[:, j:j + 1], None,
                                                          ALU.add),
             reads=[pr, r_modb], writes=[r_mod])
    return mod, r_mod


def emit_modvecs(C, mod, r_mod, col, normg, r_normg, name):
    P = C.P
    gs = P.sb(name + "_gs", [128, 3, KC])
    hg = P.sb(name + "_hg", [128, 3, KC])
    r = Res(name)
    for k in range(3):
        sc = mod[:, (3 * k + 1) * KC:(3 * k + 2) * KC, col]
        gt = mod[:, (3 * k + 2) * KC:(3 * k + 3) * KC, col]
        P.op("dve", lambda e, k=k, sc=sc: e.scalar_tensor_tensor(gs[:, k, :], sc, 1.0, normg[:, k * KC:(k + 1) * KC],
                                                                 ALU.add, ALU.mult),
             reads=[r_mod, r_normg], writes=[r])
        P.op("dve", lambda e, k=k, gt=gt: e.tensor_scalar(hg[:, k, :], gt, 1.0 if k == 1 else 0.5, None, ALU.mult),
             reads=[r_mod], writes=[r])
    return gs, hg, r


def emit_rstd(C, X, r_X, SQ, r_SQ, R, r_R, nt):
    P = C.P
    pt, pr = C.psum.next()
    for c in range(KC):
        P.op("act" if c % 2 else "dve",
             (lambda e, c=c: e.activation(SQ[:, c, :nt], X[:, c, :nt], AF.Square)) if c % 2 else
             (lambda e, c=c: e.tensor_tensor(SQ[:, c, :nt], X[:, c, :nt], X[:, c, :nt], ALU.mult)),
             reads=[r_X], writes=[r_SQ[c]])
    for c in range(KC):
        P.op("pe", lambda e, c=c: e.matmul(pt[:, :nt], C.ones[:], SQ[:, c, :nt], start=(c == 0), stop=(c == KC - 1)),
             reads=[C.r_ones, r_SQ[c]], writes=[pr])
    P.op("act", lambda e: e.activation(R[:, :nt], pt[:, :nt], AF.Sqrt, bias=C.epsb[:, 0:1], scale=1.0 / D),
         reads=[pr, C.r_epsb], writes=[r_R])
    P.op("dve", lambda e: e.reciprocal(R[:, :nt], R[:, :nt]), reads=[r_R], writes=[r_R])


def emit_pre(C, X, r_X, XN, r_XN, R, r_R, gs, sh, r_vec, nt):
    P = C.P
    for c in range(KC):
        P.op("dve", lambda e, c=c: e.scalar_tensor_tensor(XN[:, c, :nt], X[:, c, :nt], gs[:, c:c + 1], R[:, :nt],
                                                          ALU.mult, ALU.mult),
             reads=[r_X, r_R, r_vec], writes=[r_XN[c]])
        P.op("act", lambda e, c=c: e.activation(XN[:, c, :nt], XN[:, c, :nt], AF.Identity, bias=sh[:, c:c + 1]),
             reads=[r_XN[c], r_vec], writes=[r_XN[c]])


def emit_ffn(C, XN, r_XN, H, r_H, X, r_X, hg, r_vec, wgu_d, wdn_d, nt, wres=()):
    P = C.P
    for fo in range(FC):
        wt, wr = C.wgu.next()
        P.op(C.dma_eng(), lambda e, wt=wt, fo=fo: e.dma_start(out=wt[:], in_=wgu_d[fo]), reads=list(wres), writes=[wr], dma=True)
        pg, prg = C.psum.next()
        pu, pru = C.psum.next()
        for c in range(KC):
            P.op("pe", lambda e, pg=pg, wt=wt, c=c: e.matmul(pg[:, :nt], wt[:, 0, c, :], XN[:, c, :nt],
                                                             start=(c == 0), stop=(c == KC - 1)),
                 reads=[wr, r_XN[c]], writes=[prg])
        for c in range(KC):
            P.op("pe", lambda e, pu=pu, wt=wt, c=c: e.matmul(pu[:, :nt], wt[:, 1, c, :], XN[:, c, :nt],
                                                             start=(c == 0), stop=(c == KC - 1)),
                 reads=[wr, r_XN[c]], writes=[pru])
        P.op("act", lambda e, pg=pg, fo=fo: e.activation(H[:, fo, :nt], pg[:, :nt], AF.Silu),
             reads=[prg], writes=[r_H[fo]])
        P.op("dve", lambda e, pu=pu, fo=fo: e.tensor_tensor(H[:, fo, :nt], H[:, fo, :nt], pu[:, :nt], ALU.mult),
             reads=[pru, r_H[fo]], writes=[r_H[fo]])
    for do in range(KC):
        wt, wr = C.wdn.next()
        P.op(C.dma_eng(), lambda e, wt=wt, do=do: e.dma_start(out=wt[:], in_=wdn_d[do]), reads=list(wres), writes=[wr], dma=True)
        po, pro = C.psum.next()
        for f in range(FC):
            P.op("pe", lambda e, po=po, wt=wt, f=f: e.matmul(po[:, :nt], wt[:, f, :], H[:, f, :nt],
                                                             start=(f == 0), stop=(f == FC - 1)),
                 reads=[wr, r_H[f]], writes=[pro])
        P.op("dve", lambda e, po=po, do=do: e.scalar_tensor_tensor(X[:, do, :nt], po[:, :nt], hg[:, do:do + 1],
                                                                   X[:, do, :nt], ALU.mult, ALU.add),
             reads=[pro, r_vec, r_X], writes=[r_X])


def emit_proj(C, XIN, r_XIN, w_d, n_out, kc, evac, nt, wres=()):
    P = C.P
    for o in range(n_out):
        wt, wr = C.wsq.next()
        P.op(C.dma_eng(), lambda e, wt=wt, o=o: e.dma_start(out=wt[:, :kc, :], in_=w_d[o]), reads=list(wres), writes=[wr], dma=True)
        pt, pr = C.psum.next()
        for c in range(kc):
            P.op("pe", lambda e, pt=pt, wt=wt, c=c: e.matmul(pt[:, :nt], wt[:, c, :], XIN[:, c, :nt],
                                                             start=(c == 0), stop=(c == kc - 1)),
                 reads=[wr, r_XIN[c]], writes=[pr])
        evac(o, pt, pr)


def row_common(P, TB):
    C = RowCtx(P, TB)
    C.epsb = P.sb("epsb", [128, 1])
    C.r_epsb = Res()
    P.op("pool", lambda e: e.memset(C.epsb[:], NORM_EPS), writes=[C.r_epsb])
    C.X = P.sb("X", [128, KC, TB]); C.r_X = Res("X")
    C.XN = P.sb("XN", [128, KC, TB]); C.r_XN = [Res() for _ in range(KC)]
    C.SQ = P.sb("SQ", [128, KC, TB]); C.r_SQ = [Res() for _ in range(KC)]
    C.H = P.sb("H", [128, FC, TB]); C.r_H = [Res() for _ in range(FC)]
    C.R = P.sb("R", [128, TB]); C.r_R = Res("R")
    return C


def emit_silu_c(P, C, cvec_d, ncol):
    t = P.sb("sc", [128, KC, ncol])
    r = Res("sc")
    P.op("sp", lambda e: e.dma_start(out=t[:], in_=cvec_d), writes=[r], dma=True)
    P.op("act", lambda e: e.activation(t[:], t[:], AF.Silu), reads=[r], writes=[r])
    return t, r


def build_l1(NT, NCTX, TB=512):
    nc = bass.Bass("TRN2", target_bir_lowering=False)
    xT = nc.dram_tensor("xT", [D, NT], F32, kind="ExternalInput").ap()
    cxT = nc.dram_tensor("cxT", [D, NCTX], F32, kind="ExternalInput").ap()
    cvec = nc.dram_tensor("cvec", [128, KC, 2], F32, kind="ExternalInput").ap()
    modw = nc.dram_tensor("modw", [72, 128, KC, 128], F32, kind="ExternalInput").ap()
    modb = nc.dram_tensor("modb", [128, 72], F32, kind="ExternalInput").ap()
    normg = nc.dram_tensor("normg", [128, 3 * KC], F32, kind="ExternalInput").ap()
    wgu = nc.dram_tensor("wgu", [FC, 128, 2, KC, 128], F32, kind="ExternalInput").ap()
    wdn = nc.dram_tensor("wdn", [KC, 128, FC, 128], F32, kind="ExternalInput").ap()
    xo = nc.dram_tensor("xo", [D, NT], F32, kind="ExternalOutput").ap()
    ho = nc.dram_tensor("ho", [D, NT], F32, kind="ExternalOutput").ap()
    hco = nc.dram_tensor("hco", [D, NCTX], F32, kind="ExternalOutput").ap()
    modo = nc.dram_tensor("modo", [128, 72, 2], F32, kind="ExternalOutput").ap()

    P = Prog(nc)
    C = row_common(P, TB)
    sc, r_sc = emit_silu_c(P, C, cvec, 2)
    ng, r_ng = load_vecs(P, normg, 3 * KC, "normg")
    mod, r_mod = emit_mod(C, modw, modb, sc, r_sc, 2, "mod")
    P.op("sp", lambda e: e.dma_start(out=modo, in_=mod[:]), reads=[r_mod], dma=True)
    vecs = [emit_modvecs(C, mod, r_mod, col, ng, r_ng, f"mv{col}") for col in range(2)]

    def block(src, t0, nt, col, xdst, hdst):
        gs, hg, r_vec = vecs[col]
        X, r_X = C.X, C.r_X
        xs = src.rearrange("(c p) t -> p c t", p=128)
        P.op("sp", lambda e: e.dma_start(out=X[:, :, :nt], in_=xs[:, :, t0:t0 + nt]), writes=[r_X], dma=True)
        sh = lambda k: mod[:, (3 * k) * KC:(3 * k + 1) * KC, col]
        emit_rstd(C, X, r_X, C.SQ, C.r_SQ, C.R, C.r_R, nt)
        emit_pre(C, X, r_X, C.XN, C.r_XN, C.R, C.r_R, gs[:, 0, :], sh(0), r_vec, nt)
        emit_ffn(C, C.XN, C.r_XN, C.H, C.r_H, X, r_X, hg[:, 0, :], r_vec, wgu, wdn, nt)
        if xdst is not None:
            xd = xdst.rearrange("(c p) t -> p c t", p=128)
            P.op("sp", lambda e: e.dma_start(out=xd[:, :, t0:t0 + nt], in_=X[:, :, :nt]), reads=[r_X], dma=True)
        emit_rstd(C, X, r_X, C.SQ, C.r_SQ, C.R, C.r_R, nt)
        emit_pre(C, X, r_X, C.XN, C.r_XN, C.R, C.r_R, gs[:, 1, :], sh(1), r_vec, nt)
        hd = hdst.rearrange("(c p) t -> p c t", p=128)
        P.op("sp", lambda e: e.dma_start(out=hd[:, :, t0:t0 + nt], in_=C.XN[:, :, :nt]), reads=C.r_XN, dma=True)

    for t0 in range(0, NT, TB):
        block(xT, t0, min(TB, NT - t0), 0, xo, ho)
    for t0 in range(0, NCTX, TB):
        block(cxT, t0, min(TB, NCTX - t0), 1, None, hco)
    P.finish()
    P.emit()
    return nc


def const_masks():
    i = np.arange(128)
    same = (i[:, None] // 64) == (i[None, :] // 64)
    su = same & (i[:, None] < i[None, :])
    sl = same & (i[:, None] > i[None, :])
    iu = same & (i[:, None] <= i[None, :])
    il = same & (i[:, None] >= i[None, :])
    ident = i[:, None] == i[None, :]
    return np.ascontiguousarray(np.stack([su, sl, iu, il, same, ident], axis=1).astype(np.float32))


SU, SL, IU, IL, BO, IDENT = range(6)
HC_ = 4
DEC_C = -math.exp(-0.5)


class EW:
    def __init__(self):
        self.i = 0

    def __call__(self):
        self.i += 1
        return "dve" if self.i % 2 else "pool"


def rwkv_alloc(P, G):
    S = type("S", (), {})()
    S.G = G
    S.cm = P.sb("cm", [128, 6, 128]); S.r_cm = Res("cm")
    S.psum = Ring(P.banks())
    S.ew = EW()
    return S


def build_rwkv_prep(P, S, W, hsrc, nt_total, is_ctx, t_base, dr, T_lat):
    G = S.G
    ew = S.ew
    HALO = 64
    A = S.A
    def group(g0):
        nt = min(G, nt_total - g0)
        nch = nt // 64
        Hin = A["Hin"]
        lo = max(0, g0 - HALO); hi = min(nt_total, g0 + nt + HALO)
        P.op("pool", lambda e: e.memset(Hin[:], 0.0), writes=A["r_Hc"])
        for c in range(KC):
            rc = A["r_Hc"][c]
            for (poff, pn, pap) in hsrc(c, lo, hi - lo):
                d0 = lo - (g0 - HALO) + poff
                P.op("sp" if c % 2 else "pool",
                     lambda e, c=c, d0=d0, pn=pn, pap=pap: e.dma_start(out=Hin[:, c, d0:d0 + pn], in_=pap),
                     writes=[rc], dma=True)
        DF, r_DF = A["DF"], A["r_DF"]
        for c in range(KC):
            rc = A["r_Hc"][c]
            q = c // 2
            if is_ctx:
                off = -1 if q % 2 == 0 else 1
            else:
                off = (-1, 1, -64, 64)[q]
            hv = Hin[:, c, HALO:HALO + nt]
            sv = Hin[:, c, HALO + off:HALO + off + nt]
            P.op(ew(), lambda e, c=c, hv=hv, sv=sv: e.tensor_tensor(DF[:, c, :nt], sv, hv, ALU.subtract),
                 reads=[rc], writes=[r_DF[c]])
            if (not is_ctx) and q < 2:
                col = 0 if q == 0 else 63
                dv = DF[:, c, :nt].rearrange("p (j k) -> p j k", k=64)[:, :, col:col + 1]
                hv3 = hv.rearrange("p (j k) -> p j k", k=64)[:, :, col:col + 1]
                P.op(ew(), lambda e, dv=dv, hv3=hv3: e.tensor_scalar(dv, hv3, -1.0, None, ALU.mult),
                     reads=[rc, r_DF[c]], writes=[r_DF[c]])
        XM, r_XM = A["XM"], A["r_XM"]

        def mixj(j):
            for c in range(KC):
                P.op("dve", lambda e, c=c, j=j: e.scalar_tensor_tensor(XM[:, c, :nt], DF[:, c, :nt],
                                                                      W["mix"][:, j, c:c + 1],
                                                                      Hin[:, c, HALO:HALO + nt], ALU.mult, ALU.add),
                     reads=[r_DF[c], A["r_Hc"][c], W["r"]], writes=[r_XM[c]])

        def proj_fm(wt, n_out, evac, kparts=128, mcols=128):
            for o in range(n_out):
                pt, pr = S.psum.next()
                for c in range(KC):
                    P.op("pe", lambda e, pt=pt, o=o, c=c: e.matmul(pt[:mcols, :nt], wt[:, c, o * mcols:(o + 1) * mcols],
                                                                    XM[:, c, :nt], start=(c == 0), stop=(c == KC - 1)),
                         reads=[W["r"], r_XM[c]], writes=[pr])
                evac(o, pt, pr)

        Rf, Kf, Vf = A["Rf"], A["Kf"], A["Vf"]
        r_Rf, r_Kf, r_Vf = A["r_Rf"], A["r_Kf"], A["r_Vf"]
        mixj(0)
        proj_fm(W["wr"], HC_, lambda o, pt, pr: P.op("act", lambda e: e.activation(Rf[:, o, :nt], pt[:, :nt], AF.Copy),
                                                     reads=[pr], writes=[r_Rf[o]]))
        mixj(1)
        HW, r_HW = A["HW"], A["r_HW"]
        for n in range(2):
            pt, pr = S.psum.next()
            for c in range(KC):
                P.op("pe", lambda e, pt=pt, n=n, c=c: e.matmul(pt[:64, :nt], W["w1"][:, n, c, :], XM[:, c, :nt],
                                                               start=(c == 0), stop=(c == KC - 1)),
                     reads=[W["r"], r_XM[c]], writes=[pr])
            P.op("act", lambda e, pt=pt, n=n: e.activation(HW[:64, n, :nt], pt[:64, :nt], AF.Tanh),
                 reads=[pr], writes=[r_HW[n]])
        mixj(2)
        proj_fm(W["wk"], HC_, lambda o, pt, pr: P.op("act", lambda e: e.activation(Kf[:, o, :nt], pt[:, :nt], AF.Copy),
                                                     reads=[pr], writes=[r_Kf[o]]))
        mixj(3)
        proj_fm(W["wv"], HC_, lambda o, pt, pr: P.op("act", lambda e: e.activation(Vf[:, o, :nt], pt[:, :nt], AF.Copy),
                                                     reads=[pr], writes=[r_Vf[o]]))
        VT, r_VT = A["VT"], A["r_VT"]
        for tt in range(nt // 128):
            pt, pr = S.psum.next()
            for c in range(KC):
                P.op("pe", lambda e, pt=pt, tt=tt, c=c: e.matmul(pt[:, :], XM[:, c, tt * 128:(tt + 1) * 128],
                                                                 W["wv"][:, c, :], start=(c == 0), stop=(c == KC - 1)),
                     reads=[W["r"], r_XM[c]], writes=[pr])
            P.op("dve", lambda e, pt=pt, tt=tt: e.tensor_copy(VT[:, tt, :], pt[:, :]), reads=[pr], writes=[r_VT])
        vtd = dr["VT"]
        P.op("sp", lambda e, g0=g0, nt=nt: e.dma_start(
            out=vtd[t_base + g0:t_base + g0 + nt, :].rearrange("(j p) c -> p j c", p=128), in_=VT[:, :nt // 128, :]),
             reads=[r_VT], dma=True)
        mixj(4)
        HA, r_HA = A["HA"], A["r_HA"]
        for n in range(2):
            pt, pr = S.psum.next()
            for c in range(KC):
                P.op("pe", lambda e, pt=pt, n=n, c=c: e.matmul(pt[:64, :nt], W["a1"][:, n, c, :], XM[:, c, :nt],
                                                               start=(c == 0), stop=(c == KC - 1)),
                     reads=[W["r"], r_XM[c]], writes=[pr])
            P.op("act", lambda e, pt=pt, n=n: e.activation(HA[:64, n, :nt], pt[:64, :nt], AF.Copy),
                 reads=[pr], writes=[r_HA[n]])
        if not is_ctx:
            mixj(5)
            HG, r_HG = A["HG"], A["r_HG"]
            for part, (m0, m1) in enumerate(((0, 128), (128, 160))):
                pt, pr = S.psum.next()
                for c in range(KC):
                    P.op("pe", lambda e, pt=pt, c=c, m0=m0, m1=m1: e.matmul(pt[:m1 - m0, :nt], W["g1"][:, c, m0:m1],
                                                                             XM[:, c, :nt], start=(c == 0),
                                                                             stop=(c == KC - 1)),
                         reads=[W["r"], r_XM[c]], writes=[pr])
                P.op("act", lambda e, pt=pt, part=part, m0=m0, m1=m1: e.activation(HG[:m1 - m0, part, :nt],
                                                                                  pt[:m1 - m0, :nt], AF.Sigmoid),
                     reads=[pr], writes=[r_HG[part]])
            GG, r_GG = A["GG"], A["r_GG"]
            for o in range(HC_):
                pt, pr = S.psum.next()
                P.op("pe", lambda e, pt=pt, o=o: e.matmul(pt[:, :nt], W["g2a"][:, o * 128:(o + 1) * 128], HG[:, 0, :nt],
                                                          start=True, stop=False),
                     reads=[W["r"], r_HG[0]], writes=[pr])
                P.op("pe", lambda e, pt=pt, o=o: e.matmul(pt[:, :nt], W["g2b"][:32, o * 128:(o + 1) * 128],
                                                          HG[:32, 1, :nt], start=False, stop=True),
                     reads=[W["r"], r_HG[1]], writes=[pr])
                P.op("act", lambda e, pt=pt, o=o: e.activation(GG[:, o, :nt], pt[:, :nt], AF.Copy),
                     reads=[pr], writes=[r_GG])
            P.op("sp", lambda e, g0=g0, nt=nt: e.dma_start(out=dr["GG"][:, :, g0:g0 + nt], in_=GG[:, :, :nt]),
                 reads=[r_GG], dma=True)
        KK, r_KK = A["KK"], A["r_KK"]
        T1, r_T1 = A["T1"], A["r_T1"]
        for o in range(HC_):
            P.op(ew(), lambda e, o=o: e.tensor_scalar(KK[:, o, :nt], Kf[:, o, :nt], W["kk_"][:, o:o + 1], None, ALU.mult),
                 reads=[r_Kf[o], W["r"]], writes=[r_KK[o]])
            P.op(ew(), lambda e, o=o: e.tensor_tensor(T1[:, o, :nt], KK[:, o, :nt], KK[:, o, :nt], ALU.mult),
                 reads=[r_KK[o]], writes=[r_T1[o]])
            pt, pr = S.psum.next()
            P.op("pe", lambda e, pt=pt, o=o: e.matmul(pt[:, :nt], S.cm[:, BO, :], T1[:, o, :nt], start=True, stop=True),
                 reads=[S.r_cm, r_T1[o]], writes=[pr])
            P.op("act", lambda e, pt=pt, o=o: e.activation(T1[:, o, :nt], pt[:, :nt], AF.Sqrt),
                 reads=[pr, r_T1[o]], writes=[r_T1[o]])
            P.op("dve", lambda e, o=o: e.tensor_scalar(T1[:, o, :nt], T1[:, o, :nt], 1e-12, None, ALU.max),
                 reads=[r_T1[o]], writes=[r_T1[o]])
            P.op("dve", lambda e, o=o: e.reciprocal(T1[:, o, :nt], T1[:, o, :nt]), reads=[r_T1[o]], writes=[r_T1[o]])
            P.op(ew(), lambda e, o=o: e.tensor_tensor(KK[:, o, :nt], KK[:, o, :nt], T1[:, o, :nt], ALU.mult),
                 reads=[r_T1[o], r_KK[o]], writes=[r_KK[o]])
        AN, r_AN = A["AN"], A["r_AN"]
        KD, r_KD = A["KD"], A["r_KD"]
        LW, r_LW = A["LW"], A["r_LW"]
        def dirpart(n):
            for tt in range(nt // 128):
                pt, pr = S.psum.next()
                P.op("pe", lambda e, pt=pt, n=n, tt=tt: e.matmul(pt[:, :], HW[:64, n, tt * 128:(tt + 1) * 128],
                                                                 W["w2"][:64, n, :], start=True, stop=False),
                     reads=[W["r"], r_HW[n]], writes=[pr])
                P.op("pe", lambda e, pt=pt, n=n: e.matmul(pt[:, :], W["ones1"][0:1, :], W["w0"][0:1, n, :],
                                                          start=False, stop=True),
                     reads=[W["r"]], writes=[pr])
                P.op("act", lambda e, pt=pt, tt=tt: e.activation(LW[:, tt, :], pt[:, :], AF.Sigmoid),
                     reads=[pr], writes=[r_LW[tt]])
                P.op("dve", lambda e, tt=tt: e.tensor_scalar(LW[:, tt, :], LW[:, tt, :], DEC_C, None, ALU.mult),
                     reads=[r_LW[tt]], writes=[r_LW[tt]])
            tin, tex = (IU, SU) if n == 0 else (IL, SL)
            def chunkpart(o):
                pa, pra = S.psum.next()
                P.op("pe", lambda e, pa=pa, n=n, o=o: e.matmul(pa[:, :nt], W["a2"][:64, n, o * 128:(o + 1) * 128],
                                                               HA[:64, n, :nt], start=True, stop=True),
                     reads=[W["r"], r_HA[n]], writes=[pra])
                P.op("act", lambda e, pa=pa, n=n, o=o: e.activation(AN[:, o, :nt], pa[:, :nt], AF.Sigmoid,
                                                                    bias=W["a0"][:, n, o:o + 1]),
                     reads=[pra, W["r"]], writes=[r_AN[o]])
                pi, pri = S.psum.next()
                px, prx = S.psum.next()
                for tt in range(nt // 128):
                    P.op("pe", lambda e, pi=pi, tt=tt, o=o: e.matmul(pi[:, tt * 128:(tt + 1) * 128],
                                                                     LW[:, tt, o * 128:(o + 1) * 128], S.cm[:, tin, :],
                                                                     start=True, stop=True),
                         reads=[S.r_cm, r_LW[tt]], writes=[pri])
                    P.op("pe", lambda e, px=px, tt=tt, o=o: e.matmul(px[:, tt * 128:(tt + 1) * 128],
                                                                     LW[:, tt, o * 128:(o + 1) * 128], S.cm[:, tex, :],
                                                                     start=True, stop=True),
                         reads=[S.r_cm, r_LW[tt]], writes=[prx])
                (EI, EX, EV), r_E = A["Ering"].next()
                P.op("act", lambda e, pi=pi: e.activation(EI[:, :nt], pi[:, :nt], AF.Exp), reads=[pri], writes=[r_E[0]])
                P.op("act", lambda e, px=px: e.activation(EX[:, :nt], px[:, :nt], AF.Exp), reads=[prx], writes=[r_E[1]])
                P.op("act", lambda e, pi=pi: e.activation(EV[:, :nt], pi[:, :nt], AF.Exp, scale=-1.0),
                     reads=[pri], writes=[r_E[2]])
                O4, r_O4 = A["O4ring"].next()
                P.op(ew(), lambda e, o=o: e.tensor_tensor(O4[:, 0, :nt], KK[:, o, :nt], EX[:, :nt], ALU.mult),
                     reads=[r_KK[o], r_E[1]], writes=[r_O4[0]])
                P.op(ew(), lambda e, o=o: e.tensor_tensor(O4[:, 1, :nt], KK[:, o, :nt], AN[:, o, :nt], ALU.mult),
                     reads=[r_KK[o], r_AN[o]], writes=[r_O4[1]])
                P.op("dve", lambda e: e.scalar_tensor_tensor(O4[:, 1, :nt], O4[:, 1, :nt], -1.0, EV[:, :nt],
                                                            ALU.mult, ALU.mult),
                     reads=[r_O4[1], r_E[2]], writes=[r_O4[1]])
                P.op(ew(), lambda e, o=o, n=n: e.tensor_scalar(KD[:, n, o, :nt], AN[:, o, :nt], W["ka_"][:, o:o + 1],
                                                               W["oka_"][:, o:o + 1], ALU.mult, ALU.add),
                     reads=[r_AN[o], W["r"]], writes=[r_KD[n][o]])
                P.op(ew(), lambda e, o=o, n=n: e.tensor_tensor(KD[:, n, o, :nt], KD[:, n, o, :nt], Kf[:, o, :nt], ALU.mult),
                     reads=[r_KD[n][o], r_Kf[o]], writes=[r_KD[n][o]])
                P.op(ew(), lambda e, o=o, n=n: e.tensor_tensor(O4[:, 2, :nt], KD[:, n, o, :nt], EV[:, :nt], ALU.mult),
                     reads=[r_KD[n][o], r_E[2]], writes=[r_O4[2]])
                P.op(ew(), lambda e, o=o: e.tensor_tensor(O4[:, 3, :nt], Rf[:, o, :nt], EI[:, :nt], ALU.mult),
                     reads=[r_Rf[o], r_E[0]], writes=[r_O4[3]])
                WCt, r_WC = A["WC"], A["r_WC"]
                ecol = 63 if n == 0 else 0
                ev3 = EI[:, :nt].rearrange("p (j k) -> p j k", k=64)[:, :, ecol:ecol + 1]
                P.op(ew(), lambda e, ev3=ev3: e.tensor_copy(WCt[:, :nch].rearrange("p (j k) -> p j k", k=1), ev3),
                     reads=[r_E[0]], writes=[r_WC])
                c0 = (t_base + g0) // 64
                P.op("sp", lambda e, n=n, o=o, c0=c0, nch=nch: e.dma_start(out=dr["WC"][n, o, :, c0:c0 + nch],
                                                                           in_=WCt[:, :nch]),
                     reads=[r_WC], dma=True)
                for q in range(4):
                    P.op("sp" if q % 2 else "pool",
                         lambda e, n=n, o=o, q=q, g0=g0, nt=nt: e.dma_start(
                             out=dr["OPS"][n, o, :, (t_base + g0) // 64:(t_base + g0 + nt) // 64, q, :],
                             in_=O4[:, q, :nt].rearrange("p (j t) -> p j t", t=64)),
                         reads=[r_O4[q]], dma=True)
            for o in range(HC_):
                chunkpart(o)
        for n in range(2):
            dirpart(n)
        if not is_ctx:
            BV, r_BV = A["BV"], A["r_BV"]
            for o in range(HC_):
                P.op(ew(), lambda e, o=o: e.tensor_tensor(T1[:, o, :nt], KD[:, 0, o, :nt], KD[:, 1, o, :nt], ALU.add),
                     reads=[r_KD[0][o], r_KD[1][o], r_T1[o]], writes=[r_T1[o]])
                P.op("dve", lambda e, o=o: e.scalar_tensor_tensor(T1[:, o, :nt], T1[:, o, :nt], W["hrk_"][:, o:o + 1],
                                                                 Rf[:, o, :nt], ALU.mult, ALU.mult),
                     reads=[r_T1[o], r_Rf[o], W["r"]], writes=[r_T1[o]])
                pt, pr = S.psum.next()
                P.op("pe", lambda e, pt=pt, o=o: e.matmul(pt[:, :nt], S.cm[:, BO, :], T1[:, o, :nt], start=True, stop=True),
                     reads=[S.r_cm, r_T1[o]], writes=[pr])
                P.op("dve", lambda e, pt=pt, o=o: e.tensor_tensor(BV[:, o, :nt], pt[:, :nt], Vf[:, o, :nt], ALU.mult),
                     reads=[pr, r_Vf[o]], writes=[r_BV])
            P.op("sp", lambda e, g0=g0, nt=nt: e.dma_start(out=dr["BV"][:, :, g0:g0 + nt], in_=BV[:, :, :nt]),
                 reads=[r_BV], dma=True)

    for g0 in range(0, nt_total, G):
        group(g0)


def barrier(P):
    targets = {}
    for e in ENGS:
        n = P.cnt[e]
        if n > 0:
            ep, idx = divmod(n - 1, EPOCH)
            targets[(e, ep)] = idx + 1
    for _k, (k, v) in P.last_dma.items():
        targets[k] = max(targets.get(k, 0), v)
    for e in ENGS:
        waits = []
        for k, v in targets.items():
            if e == "pe" and k[0] == "pe":
                continue
            if P.known[e].get(k, 0) < v:
                waits.append((k, v))
                P.known[e][k] = v
        P.streams[e].append((waits, None, None, False))


def mk_ring(tiles, nres):
    r = Ring(tiles)
    r.res = [[Res() for _ in range(nres)] for _ in tiles]
    return r


def rwkv_prep_alloc(P, S, G):
    A = {}
    A["Hin"] = P.sb("Hin", [128, KC, G + 128]); A["r_Hc"] = [Res() for _ in range(KC)]
    A["DF"] = P.sb("DF", [128, KC, G]); A["r_DF"] = [Res() for _ in range(KC)]
    A["XM"] = P.sb("XM", [128, KC, G]); A["r_XM"] = [Res() for _ in range(KC)]
    for nm in ("Rf", "Kf", "Vf", "KK", "T1", "AN"):
        A[nm] = P.sb(nm, [128, HC_, G]); A["r_" + nm] = [Res() for _ in range(HC_)]
    A["VT"] = P.sb("VT", [128, G // 128, 512]); A["r_VT"] = Res()
    A["HW"] = P.sb("HW", [64, 2, G]); A["r_HW"] = [Res(), Res()]
    A["HA"] = P.sb("HA", [64, 2, G]); A["r_HA"] = [Res(), Res()]
    A["HG"] = P.sb("HG", [128, 2, G]); A["r_HG"] = [Res(), Res()]
    A["GG"] = P.sb("GG", [128, HC_, G]); A["r_GG"] = Res()
    A["BV"] = P.sb("BV", [128, HC_, G]); A["r_BV"] = Res()
    A["KD"] = P.sb("KD", [128, 2, HC_, G]); A["r_KD"] = [[Res() for _ in range(HC_)] for _ in range(2)]
    A["LW"] = P.sb("LW", [128, G // 128, 512]); A["r_LW"] = [Res() for _ in range(G // 128)]
    A["Ering"] = mk_ring([tuple(P.sb(f"E{i}{j}", [128, G]) for j in range(3)) for i in range(2)], 3)
    A["O4ring"] = mk_ring([P.sb(f"O4{i}", [128, 4, G]) for i in range(2)], 4)
    A["WC"] = P.sb("WCt", [128, G // 64]); A["r_WC"] = Res()
    S.A = A


def rwkv_load_weights(P, S, wd):
    W = {"r": Res("W")}
    shapes = dict(mix=[128, 6, KC], wr=[128, KC, 512], wk=[128, KC, 512], wv=[128, KC, 512],
                  w1=[128, 2, KC, 64], a1=[128, 2, KC, 64], w2=[64, 2, 512], a2=[64, 2, 512],
                  w0=[1, 2, 512], a0=[128, 2, HC_], g1=[128, KC, 160], g2a=[128, 512], g2b=[32, 512],
                  kk_=[128, HC_], ka_=[128, HC_], rk=[128, HC_], lng=[128, HC_], lnb=[128, HC_])
    i = 0
    order = ["lng", "lnb"] + [k for k in shapes if k not in ("lng", "lnb")]
    for nm in order:
        shp = shapes[nm]
        t = P.sb("w_" + nm, shp)
        W[nm] = t
        i += 1
        P.op("sp" if i % 2 else "pool", lambda e, t=t, nm=nm: e.dma_start(out=t[:], in_=wd[nm]), writes=[Res()],
             dma=True)
        if nm == "lnb":
            S.m_persist = P.mark()
    P.op("sp", lambda e: e.dma_start(out=S.cm[:], in_=wd["cm"]), writes=[S.r_cm], dma=True)
    barrier(P)
    W["oka_"] = P.sb("w_oka", [128, HC_]); W["hrk_"] = P.sb("w_hrk", [128, HC_]); W["ones1"] = P.sb("w_ones1", [1, 128])
    P.op("dve", lambda e: e.tensor_scalar(W["oka_"][:], W["ka_"][:], -1.0, 1.0, ALU.mult, ALU.add), writes=[W["r"]])
    P.op("dve", lambda e: e.tensor_scalar(W["hrk_"][:], W["rk"][:], 0.5, None, ALU.mult), writes=[W["r"]])
    P.op("pool", lambda e: e.memset(W["ones1"][:], 1.0), writes=[W["r"]])
    return W


def build_rwkv_scan(P, S, dr, NC_CTX, NC_LAT):
    NCT = NC_CTX + NC_LAT
    CG = 2
    chains = [(n, o) for n in range(2) for o in range(HC_)]
    order = [list(range(NCT)),
             list(range(NC_CTX - 1, -1, -1)) + list(range(NCT - 1, NC_CTX - 1, -1))]
    cm = S.cm
    ps = Ring(P.banks())

    def ps_next():
        t, r = ps.next()
        return t[:, 0:128], r

    OPB = {}
    for ch in chains:
        tiles = [P.sb(f"opb{ch[0]}{ch[1]}{i}", [128, CG, 4, 128]) for i in range(2)]
        OPB[ch] = mk_ring(tiles, 1)
        for t in tiles:
            P.op("pool", lambda e, t=t: e.memset(t[:], 0.0), writes=[])
    VB = {}
    for ch in chains:
        tiles = [P.sb(f"vb{ch[0]}{ch[1]}{i}", [128, CG, 128]) for i in range(2)]
        VB[ch] = mk_ring(tiles, 1)
        for t in tiles:
            P.op("pool", lambda e, t=t: e.memset(t[:], 0.0), writes=[])
    WCB = {ch: mk_ring([P.sb(f"wcb{ch[0]}{ch[1]}{i}", [128, CG]) for i in range(2)], 1) for ch in chains}
    tmp = {}
    rtmp = {}
    for ch in chains:
        for nm in ["M0", "M1", "M2", "M3", "M4", "M5", "Na", "Nb", "BT", "PT", "NQT", "KHT", "NKAHT", "U", "ST"]:
            tmp[ch, nm] = P.sb(f"t{nm}{ch[0]}{ch[1]}", [128, 128])
            rtmp[ch, nm] = Res()
        P.op("pool", lambda e, t=tmp[ch, "ST"]: e.memset(t[:], 0.0), writes=[rtmp[ch, "ST"]])
    YT = {ch: mk_ring([P.sb(f"yt{ch[0]}{ch[1]}{i}", [128, 64]) for i in range(2)], 1) for ch in chains}
    barrier(P)
    cur = {}
    curv = {}
    dq = [0]

    def dma_q():
        dq[0] += 1
        return "sp" if dq[0] % 2 else "pool"

    def load_v(ch, lo, n):
        o = ch[1]
        vt, vr = VB[ch].next()
        vr = vr[0]
        for h in range(2):
            src = dr["VT"][lo * 64:(lo + n) * 64, o * 128 + h * 64:o * 128 + h * 64 + 64].rearrange(
                "(j s) v -> s j v", s=64)
            P.op(dma_q(), lambda e, vt=vt, h=h, src=src, n=n: e.dma_start(out=vt[h * 64:(h + 1) * 64, :n, h * 64:(h + 1) * 64],
                                                                        in_=src), writes=[vr], dma=True)
        return vt, vr

    for i in range(NCT):
        units = []
        for ch in chains:
            n, o = ch
            chunk = order[n][i]
            is_ctx = chunk < NC_CTX
            if is_ctx:
                base = 0; lim = NC_CTX
            else:
                base = NC_CTX; lim = NCT
            lo = base + ((chunk - base) // CG) * CG
            cnt = min(CG, lim - lo)
            if ch not in cur or cur[ch][0] != lo:
                ot, orr = OPB[ch].next(); orr = orr[0]
                wt, wr = WCB[ch].next(); wr = wr[0]
                for h in range(2):
                    src = dr["OPS"][n, o, h * 64:(h + 1) * 64, lo:lo + cnt, :, :]
                    P.op(dma_q(), lambda e, ot=ot, h=h, src=src, cnt=cnt: e.dma_start(
                        out=ot[h * 64:(h + 1) * 64, :cnt, :, h * 64:(h + 1) * 64], in_=src), writes=[orr], dma=True)
                P.op(dma_q(), lambda e, wt=wt, n=n, o=o, lo=lo, cnt=cnt: e.dma_start(out=wt[:, :cnt],
                                                                                 in_=dr["WC"][n, o, :, lo:lo + cnt]),
                     writes=[wr], dma=True)
                vt, vr = load_v(ch, lo, cnt)
                cur[ch] = (lo, ot, orr, wt, wr, vt, vr)
            lo, ot, orr, wt, wr, vt, vr = cur[ch]
            j = chunk - lo
            u = dict(ch=ch, n=n, o=o, chunk=chunk, is_ctx=is_ctx, j=j, orr=orr, vr=vr, wr=wr,
                     KKT=ot[:, j, 0, :], NKAH=ot[:, j, 1, :], KH=ot[:, j, 2, :], RT=ot[:, j, 3, :],
                     V=vt[:, j, :], WC=wt[:, j:j + 1])
            units.append(u)

        def T(u, nm):
            return tmp[u["ch"], nm]

        def R(u, nm):
            return rtmp[u["ch"], nm]

        def mm_evac(lhs_nm, rhs_nm, out_nm, mask=None, cond=lambda u: True):
            for u in units:
                if not cond(u):
                    continue
                pt, pr = ps_next()

                def opnd(nm):
                    if nm in ("KKT", "NKAH", "KH", "RT"):
                        return u[nm], u["orr"]
                    if nm == "I":
                        return cm[:, IDENT, :], S.r_cm
                    return T(u, nm)[:], R(u, nm)
                la, lr = opnd(lhs_nm)
                ra, rr = opnd(rhs_nm)
                P.op("pe", lambda e, pt=pt, la=la, ra=ra: e.matmul(pt, la, ra, start=True, stop=True),
                     reads=[lr, rr], writes=[pr])
                out = T(u, out_nm)
                if mask is None:
                    P.op("act", lambda e, pt=pt, out=out: e.activation(out[:], pt, AF.Copy),
                         reads=[pr], writes=[R(u, out_nm)])
                else:
                    mk = mask(u)
                    P.op("dve", lambda e, pt=pt, out=out, mk=mk: e.tensor_tensor(out[:], pt, cm[:, mk, :], ALU.mult),
                         reads=[pr, S.r_cm], writes=[R(u, out_nm)])

        ms = lambda u: SU if u["n"] == 0 else SL
        mst = lambda u: SL if u["n"] == 0 else SU
        mi = lambda u: IU if u["n"] == 0 else IL
        lat = lambda u: not u["is_ctx"]
        mm_evac("NKAH", "KKT", "M0", ms)
        mm_evac("KKT", "NKAH", "Na", mst)
        mm_evac("KH", "KKT", "BT", ms)
        mm_evac("KH", "RT", "PT", mi, lat)
        mm_evac("NKAH", "RT", "NQT", mi, lat)
        mm_evac("KH", "I", "KHT")
        mm_evac("NKAH", "I", "NKAHT")
        ncur, nnxt = "Na", "Nb"
        for lvl in range(5):
            mm_evac(ncur, f"M{lvl}", f"M{lvl + 1}")
            if lvl < 4:
                mm_evac(f"M{lvl}", ncur, nnxt)
                ncur, nnxt = nnxt, ncur
        xs = []
        for u in units:
            pt, pr = ps_next()
            P.op("pe", lambda e, pt=pt, u=u: e.matmul(pt, u["KKT"], T(u, "ST")[:], start=True, stop=False),
                 reads=[u["orr"], R(u, "ST")], writes=[pr])
            P.op("pe", lambda e, pt=pt, u=u: e.matmul(pt, T(u, "BT")[:], u["V"], start=False, stop=True),
                 reads=[R(u, "BT"), u["vr"]], writes=[pr])
            P.op("act", lambda e, pt=pt, u=u: e.activation(T(u, "U")[:], pt, AF.Copy), reads=[pr], writes=[R(u, "U")])
        for lvl in range(6):
            for u in units:
                pt, pr = ps_next()
                P.op("pe", lambda e, pt=pt, u=u, lvl=lvl: e.matmul(pt, T(u, f"M{lvl}")[:], T(u, "U")[:], start=True, stop=True),
                     reads=[R(u, f"M{lvl}"), R(u, "U")], writes=[pr])
                P.op("dve", lambda e, pt=pt, u=u: e.tensor_tensor(T(u, "U")[:], T(u, "U")[:], pt, ALU.add),
                     reads=[pr, R(u, "U")], writes=[R(u, "U")])
        for u in units:
            if u["is_ctx"]:
                continue
            pt, pr = ps_next()
            P.op("pe", lambda e, pt=pt, u=u: e.matmul(pt, T(u, "ST")[:], u["RT"], start=True, stop=False),
                 reads=[u["orr"], R(u, "ST")], writes=[pr])
            P.op("pe", lambda e, pt=pt, u=u: e.matmul(pt, u["V"], T(u, "PT")[:], start=False, stop=False),
                 reads=[u["vr"], R(u, "PT")], writes=[pr])
            P.op("pe", lambda e, pt=pt, u=u: e.matmul(pt, T(u, "U")[:], T(u, "NQT")[:], start=False, stop=True),
                 reads=[R(u, "U"), R(u, "NQT")], writes=[pr])
            yt, yr = YT[u["ch"]].next(); yr = yr[0]
            for h in range(2):
                P.op("act", lambda e, pt=pt, yt=yt, h=h: e.activation(yt[h * 64:(h + 1) * 64, :],
                                                                     pt[h * 64:(h + 1) * 64, h * 64:(h + 1) * 64], AF.Copy),
                     reads=[pr], writes=[yr])
            t0 = (u["chunk"] - NC_CTX) * 64
            P.op(dma_q(), lambda e, yt=yt, u=u, t0=t0: e.dma_start(out=dr["Y"][u["n"], u["o"], :, t0:t0 + 64], in_=yt[:]),
                 reads=[yr], dma=True)
        for u in units:
            pt, pr = ps_next()
            P.op("pe", lambda e, pt=pt, u=u: e.matmul(pt, T(u, "KHT")[:], u["V"], start=True, stop=False),
                 reads=[R(u, "KHT"), u["vr"]], writes=[pr])
            P.op("pe", lambda e, pt=pt, u=u: e.matmul(pt, T(u, "NKAHT")[:], T(u, "U")[:], start=False, stop=True),
                 reads=[R(u, "NKAHT"), R(u, "U")], writes=[pr])
            P.op("dve", lambda e, pt=pt, u=u: e.tensor_tensor(T(u, "ST")[:], T(u, "ST")[:], pt, ALU.add),
                 reads=[pr, R(u, "ST")], writes=[R(u, "ST")])
            P.op("dve", lambda e, u=u: e.tensor_scalar(T(u, "ST")[:], T(u, "ST")[:], u["WC"], None, ALU.mult),
                 reads=[R(u, "ST"), u["wr"]], writes=[R(u, "ST")])


def build_rwkv_post(P, S, W, dr, T_lat, ygdst, G=512):
    bufs = mk_ring([tuple(P.sb(f"pb{i}{j}", [128, G]) for j in range(5)) for i in range(2)], 5)
    epsg = P.sb("epsg", [128, 1]); r_eps = Res()
    P.op("pool", lambda e: e.memset(epsg[:], GN_EPS), writes=[r_eps])
    def one(g0, nt, o):
        if True:
            (Y0, Y1, Bv, Gg, Tm), rr = bufs.next()
            srcs = [dr["Y"][0, o, :, g0:g0 + nt], dr["Y"][1, o, :, g0:g0 + nt], dr["BV"][:, o, g0:g0 + nt],
                    dr["GG"][:, o, g0:g0 + nt]]
            for k, (t, s_) in enumerate(zip((Y0, Y1, Bv, Gg), srcs)):
                P.op("sp" if k % 2 else "pool", lambda e, t=t, s_=s_: e.dma_start(out=t[:, :nt], in_=s_),
                     writes=[rr[k]], dma=True)
            P.op("dve", lambda e: e.tensor_tensor(Y0[:, :nt], Y0[:, :nt], Y1[:, :nt], ALU.add),
                 reads=[rr[0], rr[1]], writes=[rr[0]])
            pm, prm = S.psum.next()
            P.op("pe", lambda e, pm=pm: e.matmul(pm[:, :nt], S.cm[:, BO, :], Y0[:, :nt], start=True, stop=True),
                 reads=[S.r_cm, rr[0]], writes=[prm])
            P.op("dve", lambda e, pm=pm: e.scalar_tensor_tensor(Y0[:, :nt], pm[:, :nt], -1.0 / 64, Y0[:, :nt],
                                                                ALU.mult, ALU.add),
                 reads=[prm, rr[0]], writes=[rr[0]])
            P.op("pool", lambda e: e.tensor_tensor(Tm[:, :nt], Y0[:, :nt], Y0[:, :nt], ALU.mult),
                 reads=[rr[0]], writes=[rr[4]])
            pv, prv = S.psum.next()
            P.op("pe", lambda e, pv=pv: e.matmul(pv[:, :nt], S.cm[:, BO, :], Tm[:, :nt], start=True, stop=True),
                 reads=[S.r_cm, rr[4]], writes=[prv])
            P.op("act", lambda e, pv=pv: e.activation(Tm[:, :nt], pv[:, :nt], AF.Sqrt, bias=epsg[:, 0:1], scale=1.0 / 64),
                 reads=[prv, r_eps, rr[4]], writes=[rr[4]])
            P.op("dve", lambda e: e.reciprocal(Tm[:, :nt], Tm[:, :nt]), reads=[rr[4]], writes=[rr[4]])
            P.op("dve", lambda e, o=o: e.scalar_tensor_tensor(Y0[:, :nt], Y0[:, :nt], W["lng"][:, o:o + 1], Tm[:, :nt],
                                                              ALU.mult, ALU.mult),
                 reads=[rr[0], rr[4], W["r"]], writes=[rr[0]])
            P.op("dve", lambda e, o=o: e.scalar_tensor_tensor(Y0[:, :nt], Y0[:, :nt], W["lnb"][:, o:o + 1], Bv[:, :nt],
                                                              ALU.add, ALU.add),
                 reads=[rr[0], rr[2], W["r"]], writes=[rr[0]])
            P.op("pool", lambda e: e.tensor_tensor(Y0[:, :nt], Y0[:, :nt], Gg[:, :nt], ALU.mult),
                 reads=[rr[0], rr[3]], writes=[rr[0]])
            P.op("sp", lambda e, o=o, g0=g0, nt=nt: e.dma_start(out=ygdst(o, g0, nt), in_=Y0[:, :nt]),
                 reads=[rr[0]], dma=True)

    for g0 in range(0, T_lat, G):
        for o in range(HC_):
            one(g0, min(G, T_lat - g0), o)


def rwkv_host_weights(inp, half):
    c0 = 512 * half
    cs = slice(c0, c0 + 512)
    f = lambda a: np.ascontiguousarray(a.astype(np.float32))
    w_rkv = inp["rw_w_rkv"][0]
    lay = lambda w: f(w.reshape(KC, 128, -1).transpose(1, 0, 2))
    d = {}
    d["mix"] = f(inp["rw_mix"][0].reshape(6, KC, 128).transpose(2, 0, 1))
    d["wr"] = lay(w_rkv[0][:, cs]); d["wk"] = lay(w_rkv[1][:, cs]); d["wv"] = lay(w_rkv[2][:, cs])
    d["w1"] = f(inp["rw_w1"][0].reshape(2, KC, 128, 64).transpose(2, 0, 1, 3))
    d["a1"] = f(inp["rw_a1"][0].reshape(2, KC, 128, 64).transpose(2, 0, 1, 3))
    d["w2"] = f(inp["rw_w2"][0][:, :, cs].transpose(1, 0, 2))
    d["a2"] = f(inp["rw_a2"][0][:, :, cs].transpose(1, 0, 2))
    d["w0"] = f(inp["rw_w0"][0][:, cs][None])
    d["a0"] = f(inp["rw_a0"][0][:, cs].reshape(2, HC_, 128).transpose(2, 0, 1))
    d["g1"] = lay(inp["rw_g1"][0])
    d["g2a"] = f(inp["rw_g2"][0][:128, cs]); d["g2b"] = f(inp["rw_g2"][0][128:, cs])
    pv = lambda v: f(v[cs].reshape(HC_, 128).T)
    d["kk_"] = pv(inp["rw_k_k"][0]); d["ka_"] = pv(inp["rw_k_a"][0]); d["rk"] = pv(inp["rw_r_k"][0].reshape(-1))
    d["lng"] = pv(inp["rw_ln_g"][0]); d["lnb"] = pv(inp["rw_ln_b"][0])
    d["cm"] = const_masks()
    return d


RW_SHAPES = dict(mix=[128, 6, KC], wr=[128, KC, 512], wk=[128, KC, 512], wv=[128, KC, 512],
                 w1=[128, 2, KC, 64], a1=[128, 2, KC, 64], w2=[64, 2, 512], a2=[64, 2, 512],
                 w0=[1, 2, 512], a0=[128, 2, HC_], g1=[128, KC, 160], g2a=[128, 512], g2b=[32, 512],
                 kk_=[128, HC_], ka_=[128, HC_], rk=[128, HC_], lng=[128, HC_], lnb=[128, HC_], cm=[128, 6, 128])


def rwkv_phase(P, nc, wd, hl_src, hc_src, T_lat, T_ctx, ygdst, G=256):
    NTOT = T_ctx + T_lat
    dr = {}
    dr["OPS"] = nc.dram_tensor("rw_ops", [2, HC_, 128, NTOT // 64, 4, 64], F32).ap()
    dr["VT"] = nc.dram_tensor("rw_vt", [NTOT, 512], F32).ap()
    dr["WC"] = nc.dram_tensor("rw_wc", [2, HC_, 128, NTOT // 64], F32).ap()
    dr["GG"] = nc.dram_tensor("rw_gg", [128, HC_, T_lat], F32).ap()
    dr["BV"] = nc.dram_tensor("rw_bv", [128, HC_, T_lat], F32).ap()
    dr["Y"] = nc.dram_tensor("rw_y", [2, HC_, 128, T_lat], F32).ap()
    m0 = P.mark()
    S = rwkv_alloc(P, G)
    W = rwkv_load_weights(P, S, wd)
    m1 = P.mark()
    rwkv_prep_alloc(P, S, G)
    build_rwkv_prep(P, S, W, hc_src, T_ctx, True, 0, dr, T_lat)
    build_rwkv_prep(P, S, W, hl_src, T_lat, False, T_ctx, dr, T_lat)
    barrier(P)
    P.release(S.m_persist)
    build_rwkv_scan(P, S, dr, T_ctx // 64, T_lat // 64)
    barrier(P)
    P.release(S.m_persist)
    build_rwkv_post(P, S, W, dr, T_lat, ygdst)
    barrier(P)
    P.release(m0)


def build_rwkv_test(T_lat, T_ctx):
    nc = bass.Bass("TRN2", target_bir_lowering=False)
    hl = nc.dram_tensor("hl", [D, T_lat], F32, kind="ExternalInput").ap()
    hc = nc.dram_tensor("hc", [D, T_ctx], F32, kind="ExternalInput").ap()
    wd = {nm: nc.dram_tensor("rw_" + nm, shp, F32, kind="ExternalInput").ap() for nm, shp in RW_SHAPES.items()}
    yg = nc.dram_tensor("yg", [HC_, 128, T_lat], F32, kind="ExternalOutput").ap()
    P = Prog(nc)
    hl3 = hl.rearrange("(c p) t -> c p t", p=128)
    hc3 = hc.rearrange("(c p) t -> c p t", p=128)
    rwkv_phase(P, nc, wd, lambda c, t0, n: [(0, n, hl3[c, :, t0:t0 + n])], lambda c, t0, n: [(0, n, hc3[c, :, t0:t0 + n])],
               T_lat, T_ctx, lambda o, g0, nt: yg[o, :, g0:g0 + nt])
    P.finish()
    P.emit()
    return nc


LSEQ = 8192
NFFT = 2 * LSEQ
HY_BANDS = 16


def hyena_consts():
    f64 = np.float64
    n1 = np.arange(64, dtype=f64)[:, None]
    k = np.arange(128, dtype=f64)[None, :]
    n = np.arange(128, dtype=f64)[:, None]
    d = {}
    a = 2 * np.pi * n1 * k / 128
    d["FA"] = np.concatenate([np.cos(a), -np.sin(a)], axis=1)
    tw = 2 * np.pi * n * k / NFFT
    d["TWC"] = np.stack([np.cos(tw), np.cos(tw)], axis=1)
    d["TWS"] = np.stack([np.sin(tw), np.sin(tw)], axis=1)
    b = 2 * np.pi * n * k / 128
    d["FBC"] = np.cos(b); d["FBS"] = np.sin(b); d["FBNS"] = -np.sin(b)
    d["INV1"] = np.concatenate([np.cos(b), np.sin(b)], axis=1)
    d["INV2"] = np.concatenate([-np.sin(b), np.cos(b)], axis=1)
    c = 2 * np.pi * np.arange(128, dtype=f64)[:, None] * np.arange(64, dtype=f64)[None, :] / 128
    d["CNN"] = np.cos(c) / NFFT; d["NSNN"] = -np.sin(c) / NFFT
    f32 = np.float32
    L = LSEQ
    t = np.linspace(0.0, 1.0, L, dtype=f32)[:, None]
    ang = f32(2 * math.pi / L) * np.arange(L, dtype=f32)[:, None] * np.linspace(1e-4, HY_BANDS - 1, HY_BANDS, dtype=f32)[None]
    zf = np.concatenate([t, np.cos(ang), -np.sin(ang)], axis=-1).astype(f32)
    d["ZFT"] = zf.T
    d["TPOS"] = t.T
    return {k_: np.ascontiguousarray(v.astype(np.float32)) for k_, v in d.items()}


HYC_SHAPES = dict(FA=[64, 256], TWC=[128, 2, 128], TWS=[128, 2, 128], FBC=[128, 128], FBS=[128, 128],
                  FBNS=[128, 128], INV1=[128, 256], INV2=[128, 256], CNN=[128, 64], NSNN=[128, 64],
                  ZFT=[33, LSEQ], TPOS=[1, LSEQ])
HYW_SHAPES = dict(win=[128, KC, 1536], convw=[128, 3, 12], convb=[128, 12], fw1=[33, 64], fb1=[64, 1], ffreq=[64, 2],
                  fw2=[64, 64], fb2=[64, 1], fw3=[64, 1024], deltas=[128, 8], bias=[1, 1024])


def hyena_host_weights(inp, half):
    c0 = 512 * half
    f = lambda a: np.ascontiguousarray(np.asarray(a, dtype=np.float32))
    cols = np.concatenate([np.arange(c0, c0 + 512) + 1024 * i for i in range(3)])
    d = {}
    w_in = inp["hy_w_in"][0][:, cols]
    d["win"] = f(w_in.reshape(KC, 128, 1536).transpose(1, 0, 2))
    d["convw"] = f(inp["hy_conv_w"][0][:, cols].reshape(3, 12, 128).transpose(2, 0, 1))
    d["convb"] = f(inp["hy_conv_b"][0][cols].reshape(12, 128).T)
    d["fw1"] = f(inp["hy_f_w1"][0]); d["fb1"] = f(inp["hy_f_b1"][0][:, None])
    d["ffreq"] = f(inp["hy_f_freq"][0].T)
    d["fw2"] = f(inp["hy_f_w2"][0]); d["fb2"] = f(inp["hy_f_b2"][0][:, None])
    fcols = np.concatenate([np.arange(c0, c0 + 512) + 1024 * i for i in range(2)])
    d["fw3"] = f(inp["hy_f_w3"][0][:, fcols])
    d["deltas"] = f(inp["hy_deltas"][0][:, c0:c0 + 512].reshape(8, 128).T)
    d["bias"] = f(inp["hy_bias"][0][:, c0:c0 + 512].reshape(1, 1024))
    return d


def hyena_phase(P, nc, wd, cd, h_src, zdst):
    L = LSEQ
    m0 = P.mark()
    ew = EW()
    banks = Ring(P.banks())
    Ud = nc.dram_tensor("hy_ud", [12, 128, L + 2], F32).ap()
    UC = nc.dram_tensor("hy_uc", [12 * 128, L], F32).ap()
    FD = nc.dram_tensor("hy_fd", [8 * 128, L], F32).ap()
    Wt = {}
    rW = Res("hyW")
    i = 0
    for nm, shp in list(HYW_SHAPES.items()) + list(HYC_SHAPES.items()):
        if nm in ("ZFT", "TPOS", "bias", "win"):
            continue
        t = P.sb("hy_" + nm, shp)
        Wt[nm] = t
        src = wd[nm] if nm in wd else cd[nm]
        i += 1
        P.op("sp" if i % 2 else "pool", lambda e, t=t, src=src: e.dma_start(out=t[:], in_=src), writes=[Res()], dma=True)
    biasr = P.sb("hy_biasr", [64, 1024])
    P.op("sp", lambda e: e.dma_start(out=biasr[:], in_=wd["bias"].partition_broadcast(64)), writes=[Res()], dma=True)
    negpi = P.sb("hy_negpi", [128, 1])
    P.op("pool", lambda e: e.memset(negpi[:], -math.pi), writes=[Res()])
    barrier(P)
    fbs = P.sb("hy_fbs", [64, 2])
    nabsd = P.sb("hy_nabsd", [128, 8])
    P.op("dve", lambda e: e.tensor_tensor(fbs[:, 0:1], Wt["ffreq"][:, 0:1], Wt["fb1"][:, 0:1], ALU.mult), writes=[rW])
    P.op("dve", lambda e: e.tensor_tensor(fbs[:, 1:2], Wt["ffreq"][:, 1:2], Wt["fb2"][:, 0:1], ALU.mult), writes=[rW])
    P.op("dve", lambda e: e.tensor_scalar(nabsd[:], Wt["deltas"][:], -1.0, None, ALU.mult), writes=[rW])
    P.op("dve", lambda e: e.tensor_tensor(nabsd[:], nabsd[:], Wt["deltas"][:], ALU.max), reads=[rW], writes=[rW])
    P.op("dve", lambda e: e.tensor_scalar(nabsd[:], nabsd[:], -1.0, None, ALU.mult), reads=[rW], writes=[rW])
    m1 = P.mark()
    TBK = 512
    Wt["win"] = P.sb("hy_win", HYW_SHAPES["win"])
    P.op("sp", lambda e: e.dma_start(out=Wt["win"][:], in_=wd["win"]), writes=[rW], dma=True)
    XIN = P.sb("hy_xin", [128, KC, TBK]); r_XIN = [Res() for _ in range(KC)]
    UB = mk_ring([P.sb(f"hy_ub{i}", [128, 12, TBK]) for i in range(2)], 1)
    zt = P.sb("hy_zero", [128, 12, 1])
    P.op("pool", lambda e: e.memset(zt[:], 0.0), writes=[rW])
    P.op("sp", lambda e: e.dma_start(out=Ud[:, :, 0:1].rearrange("q p o -> p q o"), in_=zt[:], allow_slow_non_contiguous=True), reads=[rW], dma=True)
    P.op("sp", lambda e: e.dma_start(out=Ud[:, :, L + 1:L + 2].rearrange("q p o -> p q o"), in_=zt[:], allow_slow_non_contiguous=True), reads=[rW], dma=True)

    def d1_block(t0):
        for c in range(KC):
            P.op("sp" if c % 2 else "pool", lambda e, c=c: e.dma_start(out=XIN[:, c, :], in_=h_src(c, t0, TBK)),
                 writes=[r_XIN[c]], dma=True)
        ub, rub = UB.next(); rub = rub[0]
        for q in range(12):
            pt, pr = banks.next()
            for c in range(KC):
                P.op("pe", lambda e, pt=pt, q=q, c=c: e.matmul(pt[:, :], Wt["win"][:, c, q * 128:(q + 1) * 128], XIN[:, c, :],
                                                                start=(c == 0), stop=(c == KC - 1)),
                     reads=[r_XIN[c], rW], writes=[pr])
            P.op("act" if q % 2 else "dve",
                 (lambda e, pt=pt, q=q: e.activation(ub[:, q, :], pt[:, :], AF.Copy)) if q % 2 else
                 (lambda e, pt=pt, q=q: e.tensor_copy(ub[:, q, :], pt[:, :])), reads=[pr], writes=[rub])
        P.op("sp", lambda e: e.dma_start(out=Ud[:, :, 1 + t0:1 + t0 + TBK].rearrange("q p t -> p q t"), in_=ub[:]),
             reads=[rub], dma=True)
    for t0 in range(0, L, TBK):
        d1_block(t0)
    barrier(P)
    P.release(m1)
    CB = mk_ring([(P.sb(f"hy_cu{i}", [128, L + 2]), P.sb(f"hy_co{i}", [128, L])) for i in range(2)], 2)

    def conv_chunk(q):
        (cu, co), rr = CB.next()
        P.op("sp", lambda e: e.dma_start(out=cu[:], in_=Ud[q]), writes=[rr[0]], dma=True)
        cw = Wt["convw"]
        P.op("dve", lambda e: e.tensor_scalar(co[:], cu[:, 0:L], cw[:, 0, q:q + 1], Wt["convb"][:, q:q + 1], ALU.mult, ALU.add),
             reads=[rr[0], rW], writes=[rr[1]])
        P.op("dve", lambda e: e.scalar_tensor_tensor(co[:], cu[:, 1:L + 1], cw[:, 1, q:q + 1], co[:], ALU.mult, ALU.add),
             reads=[rr[0], rr[1]], writes=[rr[1]])
        P.op("dve", lambda e: e.scalar_tensor_tensor(co[:], cu[:, 2:L + 2], cw[:, 2, q:q + 1], co[:], ALU.mult, ALU.add),
             reads=[rr[0], rr[1]], writes=[rr[1]])
        P.op("pool", lambda e: e.dma_start(out=UC[q * 128:(q + 1) * 128, :], in_=co[:]), reads=[rr[1]], dma=True)
    for q in range(12):
        conv_chunk(q)
    barrier(P)
    P.release(m1)
    zft = P.sb("hy_zft", [33, TBK]); r_zft = Res()
    tps = P.sb("hy_tps", [128, TBK]); r_tps = Res()
    h1 = P.sb("hy_h1", [64, TBK]); r_h1 = Res()
    h2 = P.sb("hy_h2", [64, TBK]); r_h2 = Res()
    FB_ = mk_ring([(P.sb(f"hy_fw{i}", [128, TBK]), P.sb(f"hy_ff{i}", [128, TBK])) for i in range(2)], 2)
    TWO_PI = 2 * math.pi
    MAGIC = 12582912.0
    PI_LO = 3.1415925
    rr_ = P.sb("hy_rr", [64, TBK]); r_rr = Res()

    def sin_layer(pt, pr, dst, r_dst, k):
        P.op("dve", lambda e: e.tensor_scalar(dst[:], pt[:64, :], Wt["ffreq"][:, k:k + 1], fbs[:, k:k + 1], ALU.mult, ALU.add),
             reads=[pr, rW], writes=[r_dst])
        P.op("dve", lambda e: e.tensor_scalar(rr_[:], dst[:], 1.0 / TWO_PI, MAGIC, ALU.mult, ALU.add), reads=[r_dst], writes=[r_rr])
        P.op("dve", lambda e: e.tensor_scalar(rr_[:], rr_[:], MAGIC, None, ALU.subtract), reads=[r_rr], writes=[r_rr])
        P.op("dve", lambda e: e.scalar_tensor_tensor(dst[:], rr_[:], -TWO_PI, dst[:], ALU.mult, ALU.add), reads=[r_rr, r_dst], writes=[r_dst])
        P.op("dve", lambda e: e.tensor_scalar(dst[:], dst[:], PI_LO, -PI_LO, ALU.min, ALU.max), reads=[r_dst], writes=[r_dst])
        P.op("act", lambda e: e.activation(dst[:], dst[:], AF.Sin), reads=[r_dst], writes=[r_dst])

    def filt_block(t0):
        P.op("sp", lambda e: e.dma_start(out=zft[:], in_=cd["ZFT"][:, t0:t0 + TBK]), writes=[r_zft], dma=True)
        P.op("pool", lambda e: e.dma_start(out=tps[:], in_=cd["TPOS"][:, t0:t0 + TBK].partition_broadcast(128)),
             writes=[r_tps], dma=True)
        pt, pr = banks.next()
        P.op("pe", lambda e: e.matmul(pt[:64, :], Wt["fw1"][:, :], zft[:, :], start=True, stop=True), reads=[r_zft, rW], writes=[pr])
        sin_layer(pt, pr, h1, r_h1, 0)
        pt2, pr2 = banks.next()
        P.op("pe", lambda e: e.matmul(pt2[:64, :], Wt["fw2"][:, :], h1[:, :], start=True, stop=True), reads=[r_h1, rW], writes=[pr2])
        sin_layer(pt2, pr2, h2, r_h2, 1)
        for cc in range(8):
            (fw, ff), rr = FB_.next()
            pt3, pr3 = banks.next()
            P.op("pe", lambda e, cc=cc, pt3=pt3: e.matmul(pt3[:, :], Wt["fw3"][:, cc * 128:(cc + 1) * 128], h2[:, :], start=True, stop=True),
                 reads=[r_h2, rW], writes=[pr3])
            P.op("act", lambda e, cc=cc, fw=fw: e.activation(fw[:], tps[:], AF.Exp, scale=nabsd[:, cc:cc + 1]),
                 reads=[r_tps, rW], writes=[rr[0]])
            P.op("dve", lambda e, pt3=pt3, fw=fw, ff=ff: e.tensor_tensor(ff[:], pt3[:, :], fw[:], ALU.mult),
                 reads=[pr3, rr[0]], writes=[rr[1]])
            P.op("sp" if cc % 2 else "pool", lambda e, cc=cc, ff=ff: e.dma_start(out=FD[cc * 128:(cc + 1) * 128, t0:t0 + TBK], in_=ff[:]),
                 reads=[rr[1]], dma=True)
    for t0 in range(0, L, TBK):
        filt_block(t0)
    barrier(P)
    P.release(m1)
    CGR = 8
    NSET = 4
    IN_ = mk_ring([tuple(P.sb(f"hy_in{i}{j}", [64, CGR, 128]) for j in range(5)) for i in range(2)], 5)
    ZO = mk_ring([P.sb(f"hy_zo{i}", [64, CGR, 128]) for i in range(2)], 1)

    class BSet:
        def __init__(self, k):
            mk = lambda nm, shp: P.sb(f"hy_{nm}{k}", shp)
            self.PA = mk("pa", [128, 2, 2, 128]); self.r_PA = Res()
            self.Y1 = mk("y1", [128, 2, 2, 128]); self.r_Y1 = Res()
            self.TA = mk("ta", [128, 2, 128]); self.TB = mk("tb", [128, 2, 128]); self.r_T = [Res(), Res()]
            self.HT = [mk(f"ht{f}", [128, 2, 2, 128]) for f in range(2)]; self.r_HT = [Res(), Res()]
            self.XT = mk("xt", [128, 2, 2, 128]); self.r_XT = Res()
            self.ZZ = mk("zz", [128, 2, 2, 128]); self.r_ZZ = Res()
            self.PC = mk("pc", [128, 2, 2, 128]); self.r_PC = Res()
            self.TT = mk("tt", [128, 2, 2, 128]); self.r_TT = Res()
            self.Z1 = mk("z1", [64, 2, 128]); self.r_Z1 = Res()
            self.TE = mk("te", [64, 2, 128]); self.r_TE = Res()
    SETS = [BSet(k) for k in range(NSET)]

    def fwd_fft(B, src, r_src, dst, r_dst):
        PA, Y1, TA, TBb, r_PA, r_Y1, r_T = B.PA, B.Y1, B.TA, B.TB, B.r_PA, B.r_Y1, B.r_T
        pt, pr = banks.next()
        for s_ in range(2):
            P.op("pe", lambda e, s_=s_: e.matmul(pt[:, s_ * 256:(s_ + 1) * 256], src(s_), Wt["FA"][:, :], start=True, stop=True),
                 reads=[r_src, rW], writes=[pr])
        yield
        P.op("act", lambda e: e.activation(PA[:].rearrange("p s c k -> p (s c k)"), pt[:, :], AF.Copy), reads=[pr], writes=[r_PA])
        yield
        re = PA[:, :, 0, :]; im = PA[:, :, 1, :]
        c_, s2 = Wt["TWC"][:], Wt["TWS"][:]
        P.op(ew(), lambda e: e.tensor_tensor(TA[:], re, c_, ALU.mult), reads=[r_PA, rW], writes=[r_T[0]])
        P.op(ew(), lambda e: e.tensor_tensor(TBb[:], im, s2, ALU.mult), reads=[r_PA, rW], writes=[r_T[1]])
        yield
        P.op(ew(), lambda e: e.tensor_tensor(Y1[:, 0, :, :], TA[:], TBb[:], ALU.add), reads=r_T, writes=[r_Y1])
        yield
        P.op(ew(), lambda e: e.tensor_tensor(TA[:], im, c_, ALU.mult), reads=[r_PA, rW, r_Y1], writes=[r_T[0]])
        P.op(ew(), lambda e: e.tensor_tensor(TBb[:], re, s2, ALU.mult), reads=[r_PA, rW, r_Y1], writes=[r_T[1]])
        yield
        P.op(ew(), lambda e: e.tensor_tensor(Y1[:, 1, :, :], TA[:], TBb[:], ALU.subtract), reads=r_T, writes=[r_Y1])
        yield
        y_re = Y1[:, 0, :, :].rearrange("p s k -> p (s k)"); y_im = Y1[:, 1, :, :].rearrange("p s k -> p (s k)")
        pb, prb = banks.next()
        P.op("pe", lambda e: e.matmul(pb[:, 0:256], Wt["FBC"][:, :], y_re, start=True, stop=False), reads=[r_Y1, rW], writes=[prb])
        P.op("pe", lambda e: e.matmul(pb[:, 0:256], Wt["FBS"][:, :], y_im, start=False, stop=True), reads=[r_Y1, rW], writes=[prb])
        P.op("pe", lambda e: e.matmul(pb[:, 256:512], Wt["FBC"][:, :], y_im, start=True, stop=False), reads=[r_Y1, rW], writes=[prb])
        P.op("pe", lambda e: e.matmul(pb[:, 256:512], Wt["FBNS"][:, :], y_re, start=False, stop=True), reads=[r_Y1, rW], writes=[prb])
        yield
        P.op("act", lambda e: e.activation(dst[:].rearrange("p c s k -> p (c s k)"), pb[:, :], AF.Copy), reads=[prb], writes=[r_dst])
        yield

    def conv(B, src, r_src, f, out_psum_cb):
        XT, ZZ, PC, TT, TA, TBb = B.XT, B.ZZ, B.PC, B.TT, B.TA, B.TB
        r_XT, r_ZZ, r_PC, r_TT, r_T = B.r_XT, B.r_ZZ, B.r_PC, B.r_TT, B.r_T
        yield from fwd_fft(B, src, r_src, XT, r_XT)
        H = B.HT[f]; r_H = B.r_HT[f]
        xr, xi = XT[:, 0, :, :], XT[:, 1, :, :]
        hr, hi = H[:, 0, :, :], H[:, 1, :, :]
        P.op(ew(), lambda e: e.tensor_tensor(TA[:], xr, hr, ALU.mult), reads=[r_XT, r_H], writes=[r_T[0]])
        P.op(ew(), lambda e: e.tensor_tensor(TBb[:], xi, hi, ALU.mult), reads=[r_XT, r_H], writes=[r_T[1]])
        yield
        P.op(ew(), lambda e: e.tensor_tensor(ZZ[:, 0, :, :], TA[:], TBb[:], ALU.subtract), reads=r_T, writes=[r_ZZ])
        yield
        P.op(ew(), lambda e: e.tensor_tensor(TA[:], xr, hi, ALU.mult), reads=[r_XT, r_H, r_ZZ], writes=[r_T[0]])
        P.op(ew(), lambda e: e.tensor_tensor(TBb[:], xi, hr, ALU.mult), reads=[r_XT, r_H, r_ZZ], writes=[r_T[1]])
        yield
        P.op(ew(), lambda e: e.tensor_tensor(ZZ[:, 1, :, :], TA[:], TBb[:], ALU.add), reads=r_T, writes=[r_ZZ])
        yield
        pc, prc = banks.next()
        for s_ in range(2):
            P.op("pe", lambda e, s_=s_: e.matmul(pc[:, s_ * 256:(s_ + 1) * 256], ZZ[:, 0, s_, :], Wt["INV1"][:, :], start=True, stop=False),
                 reads=[r_ZZ, rW], writes=[prc])
            P.op("pe", lambda e, s_=s_: e.matmul(pc[:, s_ * 256:(s_ + 1) * 256], ZZ[:, 1, s_, :], Wt["INV2"][:, :], start=False, stop=True),
                 reads=[r_ZZ, rW], writes=[prc])
        yield
        P.op("act", lambda e: e.activation(PC[:].rearrange("p s c k -> p (s c k)"), pc[:, :], AF.Copy), reads=[prc], writes=[r_PC])
        yield
        re = PC[:, :, 0, :]; im = PC[:, :, 1, :]
        c_, s2 = Wt["TWC"][:], Wt["TWS"][:]
        P.op(ew(), lambda e: e.tensor_tensor(TA[:], re, c_, ALU.mult), reads=[r_PC, rW], writes=[r_T[0]])
        P.op(ew(), lambda e: e.tensor_tensor(TBb[:], im, s2, ALU.mult), reads=[r_PC, rW], writes=[r_T[1]])
        yield
        P.op(ew(), lambda e: e.tensor_tensor(TT[:, 0, :, :], TA[:], TBb[:], ALU.subtract), reads=r_T, writes=[r_TT])
        yield
        P.op(ew(), lambda e: e.tensor_tensor(TA[:], re, s2, ALU.mult), reads=[r_PC, rW, r_TT], writes=[r_T[0]])
        P.op(ew(), lambda e: e.tensor_tensor(TBb[:], im, c_, ALU.mult), reads=[r_PC, rW, r_TT], writes=[r_T[1]])
        yield
        P.op(ew(), lambda e: e.tensor_tensor(TT[:, 1, :, :], TA[:], TBb[:], ALU.add), reads=r_T, writes=[r_TT])
        yield
        pd, prd = banks.next()
        P.op("pe", lambda e: e.matmul(pd[:64, 0:256], Wt["CNN"][:, :], TT[:, 0, :, :].rearrange("p s k -> p (s k)"), start=True, stop=False),
             reads=[r_TT, rW], writes=[prd])
        P.op("pe", lambda e: e.matmul(pd[:64, 0:256], Wt["NSNN"][:, :], TT[:, 1, :, :].rearrange("p s k -> p (s k)"), start=False, stop=True),
             reads=[r_TT, rW], writes=[prd])
        yield
        out_psum_cb(pd, prd)
        yield

    def group(g):
        ch0 = g * CGR
        (F0, F1, X1, X2, Vv), rin = IN_.next()
        srcs = [FD[ch0:ch0 + CGR, :], FD[512 + ch0:512 + ch0 + CGR, :], UC[ch0:ch0 + CGR, :], UC[512 + ch0:512 + ch0 + CGR, :],
                UC[1024 + ch0:1024 + ch0 + CGR, :]]
        for k, (t, s_) in enumerate(zip((F0, F1, X1, X2, Vv), srcs)):
            P.op("sp" if k % 2 else "pool", lambda e, t=t, s_=s_: e.dma_start(out=t[:], in_=s_.rearrange("c (a b) -> a c b", b=128)),
                 writes=[rin[k]], dma=True)
        zo, rzo = ZO.next(); rzo = rzo[0]

        def pair(pi):
            B = SETS[pi % NSET]
            c_ = pi * 2
            Z1, TE, r_Z1, r_TE = B.Z1, B.TE, B.r_Z1, B.r_TE
            yield from fwd_fft(B, lambda s_: F0[:, c_ + s_, :], rin[0], B.HT[0], B.r_HT[0])
            yield from fwd_fft(B, lambda s_: F1[:, c_ + s_, :], rin[1], B.HT[1], B.r_HT[1])

            def ep1(pd, prd):
                for s_ in range(2):
                    col = ch0 + c_ + s_
                    P.op("dve", lambda e, s_=s_, col=col: e.scalar_tensor_tensor(TE[:, s_, :], Vv[:, c_ + s_, :], biasr[:, col:col + 1],
                                                                                 pd[:64, s_ * 128:(s_ + 1) * 128], ALU.mult, ALU.add),
                         reads=[rin[4], prd], writes=[r_TE])
                P.op("pool", lambda e: e.tensor_tensor(Z1[:], TE[:], X1[:, c_:c_ + 2, :], ALU.mult), reads=[r_TE, rin[2]], writes=[r_Z1])
            yield from conv(B, lambda s_: Vv[:, c_ + s_, :], rin[4], 0, ep1)

            def ep2(pd, prd):
                for s_ in range(2):
                    col = 512 + ch0 + c_ + s_
                    P.op("dve", lambda e, s_=s_, col=col: e.scalar_tensor_tensor(TE[:, s_, :], Z1[:, s_, :], biasr[:, col:col + 1],
                                                                                 pd[:64, s_ * 128:(s_ + 1) * 128], ALU.mult, ALU.add),
                         reads=[r_Z1, prd], writes=[r_TE])
                P.op("pool", lambda e: e.tensor_tensor(zo[:, c_:c_ + 2, :], TE[:], X2[:, c_:c_ + 2, :], ALU.mult),
                     reads=[r_TE, rin[3]], writes=[rzo])
            yield from conv(B, lambda s_: Z1[:, s_, :], r_Z1, 1, ep2)
        gens = [pair(pi) for pi in range(CGR // 2)]
        while gens:
            for g_ in list(gens):
                try:
                    next(g_)
                except StopIteration:
                    gens.remove(g_)
        P.op("sp", lambda e: e.dma_start(out=zdst[ch0:ch0 + CGR, :].rearrange("c (a b) -> a c b", b=128), in_=zo[:]),
             reads=[rzo], dma=True)
    for g in range(512 // CGR):
        group(g)
    barrier(P)
    P.release(m0)


def build_hyena_test():
    nc = bass.Bass("TRN2", target_bir_lowering=False)
    hl = nc.dram_tensor("hl", [D, LSEQ], F32, kind="ExternalInput").ap()
    wd = {nm: nc.dram_tensor("hy_" + nm, shp, F32, kind="ExternalInput").ap() for nm, shp in HYW_SHAPES.items()}
    cd = {nm: nc.dram_tensor("hc_" + nm, shp, F32, kind="ExternalInput").ap() for nm, shp in HYC_SHAPES.items()}
    zo = nc.dram_tensor("zo", [512, LSEQ], F32, kind="ExternalOutput").ap()
    P = Prog(nc)
    hl3 = hl.rearrange("(c p) t -> c p t", p=128)
    hyena_phase(P, nc, wd, cd, lambda c, t0, n: hl3[c, :, t0:t0 + n], zo)
    P.finish()
    P.emit()
    return nc


NT_CORE = 4096
NCTX_CORE = 128
T_LAT = 8192
T_CTX = 256
PAIRS = [[0, 1], [2, 3], [4, 5], [6, 7]]
SHARED = dict(modw0=(72, 128, KC, 128), modw1=(72, 128, KC, 128),
              wgu00=(FC, 128, 2, KC, 128), wgu01=(FC, 128, 2, KC, 128), wgu10=(FC, 128, 2, KC, 128), wgu11=(FC, 128, 2, KC, 128),
              wdn00=(KC, 128, FC, 128), wdn01=(KC, 128, FC, 128), wdn10=(KC, 128, FC, 128), wdn11=(KC, 128, FC, 128),
              wo=(KC, 128, KC, 128), wout=(KC, 128, KC, 128))
_LET = "abcdefgh"


def gather_shared(P, nc):
    out = {}
    for nm, shp in SHARED.items():
        n = int(np.prod(shp))
        m = n // 8 // 128
        src = nc.dram_tensor("sh_" + nm, [128, m], F32, kind="ExternalInput").ap()
        bnc = nc.dram_tensor("shb_" + nm, [128, m], F32)
        full = nc.dram_tensor("shg_" + nm, [8 * 128, m], F32)
        r1, r2 = Res(), Res()
        P.op("sp", lambda e, bnc=bnc, src=src: e.dma_start(out=bnc.ap(), in_=src), writes=[r1], dma=True)
        P.cc(lambda e, bnc=bnc, full=full: e.collective_compute("AllGather", ALU.bypass, replica_groups=[list(range(8))],
                                                               ins=[bnc.ap().opt()], outs=[full.ap().opt()]),
             reads=[r1], writes=[r2])
        dims = " ".join(_LET[i] for i in range(len(shp)))
        kw = {_LET[i]: shp[i] for i in range(1, len(shp))}
        view = full.ap().rearrange("x y -> (x y)").rearrange(f"({dims}) -> {dims}", **kw)
        out[nm] = (view, r2)
    return out


def pair_gather(P, nc, name, loc, shape):
    full = nc.dram_tensor(name, [2 * shape[0]] + list(shape[1:]), F32)
    r = Res()
    P.cc(lambda e: e.collective_compute("AllGather", ALU.bypass, replica_groups=PAIRS,
                                        ins=[loc.ap().opt()], outs=[full.ap().opt()]), writes=[r])
    return full, r


def with_res(P, res_list):
    return list(res_list)


def build_full(stop=None):
    nc = bass.Bass("TRN2", target_bir_lowering=False)
    TB = 512
    ext = lambda nm, shp: nc.dram_tensor(nm, list(shp), F32, kind="ExternalInput").ap()
    xT = ext("xT", [D, NT_CORE]); cxT = ext("cxT", [D, NCTX_CORE]); cvec = ext("cvec", [128, KC, 2])
    seld = ext("sel", [128, 2])
    modb = [ext("modb0", [128, 72]), ext("modb1", [128, 72])]
    normg = [ext("normg0", [128, 3 * KC]), ext("normg1", [128, 3 * KC])]
    finalg = ext("finalg", [128, KC])
    rwd = {nm: ext("rw_" + nm, shp) for nm, shp in RW_SHAPES.items()}
    hyd = {nm: ext("hy_" + nm, shp) for nm, shp in HYW_SHAPES.items()}
    hcd = {nm: ext("hc_" + nm, shp) for nm, shp in HYC_SHAPES.items()}
    outT = nc.dram_tensor("outT", [D, NT_CORE], F32, kind="ExternalOutput").ap()
    XLd = nc.dram_tensor("i_xl", [D, NT_CORE], F32).ap()
    HLloc = nc.dram_tensor("i_hl", [D, NT_CORE], F32)
    HCloc = nc.dram_tensor("i_hc", [D, NCTX_CORE], F32)
    YGloc = nc.dram_tensor("i_yg", [512, T_LAT], F32)
    HL1loc = nc.dram_tensor("i_hl1", [D, NT_CORE], F32)
    ZDloc = nc.dram_tensor("i_zd", [512, T_LAT], F32)

    P = Prog(nc)
    SH = gather_shared(P, nc)
    barrier(P)
    if stop == "W":
        for k_, nm in enumerate(("wout", "wo", "modw1", "modw0")):
            v_, r_ = SH[nm]
            P.op("sp", lambda e, v_=v_, k_=k_: e.dma_start(out=outT[:, k_ * 1024:(k_ + 1) * 1024].rearrange("(a p) (c k) -> a p c k", p=128, k=128),
                                                        in_=v_[0:8]), reads=[r_], dma=True)
        P.finish(); P.emit()
        return nc

    def row_setup(layer, ncol):
        C = row_common(P, TB)
        sc, r_sc = emit_silu_c(P, C, cvec, 2)
        ng, r_ng = load_vecs(P, normg[layer], 3 * KC, f"normg{layer}")
        mw, r_mw = SH[f"modw{layer}"]
        C.P_extra = [r_mw]
        mod, r_mod = emit_mod_g(C, mw, r_mw, modb[layer], sc, r_sc, 2, f"mod{layer}")
        vecs = [emit_modvecs(C, mod, r_mod, col, ng, r_ng, f"mv{layer}{col}") for col in range(ncol)]
        return C, mod, vecs

    def ffn(C, gs_k, sh_k, hg_k, r_vec, wname, nt):
        wg, r_wg = SH["wgu" + wname]
        wd_, r_wd = SH["wdn" + wname]
        emit_rstd(C, C.X, C.r_X, C.SQ, C.r_SQ, C.R, C.r_R, nt)
        emit_pre(C, C.X, C.r_X, C.XN, C.r_XN, C.R, C.r_R, gs_k, sh_k, r_vec, nt)
        emit_ffn(C, C.XN, C.r_XN, C.H, C.r_H, C.X, C.r_X, hg_k, r_vec, wg, wd_, nt, wres=[r_wg, r_wd])

    def shv(mod, k, col):
        return mod[:, (3 * k) * KC:(3 * k + 1) * KC, col]

    mA = P.mark()
    C, mod0, vecs0 = row_setup(0, 2)
    r_hl, r_hc, r_xl = Res(), Res(), Res()

    def blockA(src, t0, nt, col, xdst, hdst, r_hdst):
        gs, hg, r_vec = vecs0[col]
        xs = src.rearrange("(c p) t -> p c t", p=128)
        P.op("sp", lambda e: e.dma_start(out=C.X[:, :, :nt], in_=xs[:, :, t0:t0 + nt]), writes=[C.r_X], dma=True)
        ffn(C, gs[:, 0, :], shv(mod0, 0, col), hg[:, 0, :], r_vec, "00", nt)
        if xdst is not None:
            xd = xdst.rearrange("(c p) t -> p c t", p=128)
            P.op("sp", lambda e: e.dma_start(out=xd[:, :, t0:t0 + nt], in_=C.X[:, :, :nt]), reads=[C.r_X], writes=[r_xl], dma=True)
        emit_rstd(C, C.X, C.r_X, C.SQ, C.r_SQ, C.R, C.r_R, nt)
        emit_pre(C, C.X, C.r_X, C.XN, C.r_XN, C.R, C.r_R, gs[:, 1, :], shv(mod0, 1, col), r_vec, nt)
        hd = hdst.rearrange("(c p) t -> p c t", p=128)
        P.op("sp", lambda e: e.dma_start(out=hd[:, :, t0:t0 + nt], in_=C.XN[:, :, :nt]), reads=C.r_XN, writes=[r_hdst], dma=True)
    for t0 in range(0, NT_CORE, TB):
        blockA(xT, t0, TB, 0, XLd, HLloc.ap(), r_hl)
    blockA(cxT, 0, NCTX_CORE, 1, None, HCloc.ap(), r_hc)
    barrier(P)
    P.release(mA)
    HLg, r_HLg = pair_gather(P, nc, "g_hl", HLloc, [D, NT_CORE])
    HCg, r_HCg = pair_gather(P, nc, "g_hc", HCloc, [D, NCTX_CORE])
    barrier(P)
    if stop == "A":
        hw_ = NT_CORE // 2
        P.op("sp", lambda e: e.dma_start(out=outT[:, 0:hw_], in_=HLg.ap()[0:D, 0:hw_]), dma=True)
        P.op("sp", lambda e: e.dma_start(out=outT[:, hw_:2 * hw_], in_=HLg.ap()[D:2 * D, 0:hw_]), dma=True)
        P.finish(); P.emit()
        return nc
    hl4 = HLg.ap().rearrange("(h c p) t -> h c p t", h=2, p=128)
    hc4 = HCg.ap().rearrange("(h c p) t -> h c p t", h=2, p=128)

    def pieces(v4, hs):
        def f(c, t0, n):
            out = []
            t = t0
            while t < t0 + n:
                h = t // hs
                e_ = min(t0 + n, (h + 1) * hs)
                out.append((t - t0, e_ - t, v4[h, c, :, t - h * hs:e_ - h * hs]))
                t = e_
            return out
        return f
    ygl = YGloc.ap().rearrange("(o p) t -> o p t", p=128)
    rwkv_phase(P, nc, rwd, pieces(hl4, NT_CORE), pieces(hc4, NCTX_CORE), T_LAT, T_CTX,
               lambda o, g0, nt: ygl[o, :, g0:g0 + nt])
    YGg, r_YGg = pair_gather(P, nc, "g_yg", YGloc, [512, T_LAT])
    barrier(P)
    mC = P.mark()
    C, mod0, vecs0 = row_setup(0, 1)
    selt, r_sel = load_vecs(P, seld, 2, "sel")
    ng1, r_ng1 = load_vecs(P, normg[1], 3 * KC, "normg1b")
    sc1, r_sc1 = emit_silu_c(P, C, cvec, 2)
    mw1, r_mw1 = SH["modw1"]
    mod1, r_mod1 = emit_mod_g(C, mw1, r_mw1, modb[1], sc1, r_sc1, 2, "mod1c")
    vecs1 = [emit_modvecs(C, mod1, r_mod1, 0, ng1, r_ng1, "mv1c")]
    BL = mk_ring([(P.sb(f"bl{i}a", [128, TB]), P.sb(f"bl{i}b", [128, TB])) for i in range(2)], 2)

    def blend_in(G_ap, t0, nt):
        g3 = G_ap.rearrange("(c p) t -> c p t", p=128)
        for c in range(KC):
            (a0, a1), rr = BL.next()
            P.op("sp", lambda e, c=c, a0=a0: e.dma_start(out=a0[:, :nt], in_=g3[c, :, t0:t0 + nt]), writes=[rr[0]], dma=True)
            P.op("pool", lambda e, c=c, a1=a1: e.dma_start(out=a1[:, :nt], in_=g3[c, :, NT_CORE + t0:NT_CORE + t0 + nt]),
                 writes=[rr[1]], dma=True)
            P.op("pool", lambda e, a1=a1: e.tensor_scalar(a1[:, :nt], a1[:, :nt], selt[:, 1:2], None, ALU.mult),
                 reads=[rr[1], r_sel], writes=[rr[1]])
            P.op("dve", lambda e, c=c, a0=a0, a1=a1: e.scalar_tensor_tensor(C.XN[:, c, :nt], a0[:, :nt], selt[:, 0:1], a1[:, :nt],
                                                                            ALU.mult, ALU.add),
                 reads=[rr[0], rr[1], r_sel], writes=[C.r_XN[c]])

    def mixer_out(wname, gate, r_vec, nt):
        wv_, r_wv = SH[wname]

        def evac(o, pt, pr):
            P.op("dve", lambda e: e.scalar_tensor_tensor(C.X[:, o, :nt], pt[:, :nt], gate[:, o:o + 1], C.X[:, o, :nt], ALU.mult, ALU.add),
                 reads=[pr, r_vec, C.r_X], writes=[C.r_X])
        emit_proj(C, C.XN, C.r_XN, wv_, KC, KC, evac, nt, wres=[r_wv])
    xld3 = XLd.rearrange("(c p) t -> p c t", p=128)
    hl1d = HL1loc.ap().rearrange("(c p) t -> p c t", p=128)
    r_hl1 = Res()

    def blockC(t0):
        nt = TB
        gs0, hg0, rv0 = vecs0[0]
        gs1, hg1, rv1 = vecs1[0]
        P.op("sp", lambda e: e.dma_start(out=C.X[:, :, :nt], in_=xld3[:, :, t0:t0 + nt]), reads=[r_xl], writes=[C.r_X], dma=True)
        blend_in(YGg.ap(), t0, nt)
        mixer_out("wo", hg0[:, 1, :], rv0, nt)
        ffn(C, gs0[:, 2, :], shv(mod0, 2, 0), hg0[:, 2, :], rv0, "01", nt)
        ffn(C, gs1[:, 0, :], shv(mod1, 0, 0), hg1[:, 0, :], rv1, "10", nt)
        P.op("sp", lambda e: e.dma_start(out=xld3[:, :, t0:t0 + nt], in_=C.X[:, :, :nt]), reads=[C.r_X], writes=[r_xl], dma=True)
        emit_rstd(C, C.X, C.r_X, C.SQ, C.r_SQ, C.R, C.r_R, nt)
        emit_pre(C, C.X, C.r_X, C.XN, C.r_XN, C.R, C.r_R, gs1[:, 1, :], shv(mod1, 1, 0), rv1, nt)
        P.op("sp", lambda e: e.dma_start(out=hl1d[:, :, t0:t0 + nt], in_=C.XN[:, :, :nt]), reads=C.r_XN, writes=[r_hl1], dma=True)
    for t0 in range(0, NT_CORE, TB):
        blockC(t0)
    barrier(P)
    P.release(mC)
    HL1g, r_HL1g = pair_gather(P, nc, "g_hl1", HL1loc, [D, NT_CORE])
    barrier(P)
    h14 = HL1g.ap().rearrange("(h c p) t -> h c p t", h=2, p=128)
    hyena_phase(P, nc, hyd, hcd, lambda c, t0, n: h14[t0 // NT_CORE, c, :, t0 % NT_CORE:t0 % NT_CORE + n], ZDloc.ap())
    ZDg, r_ZDg = pair_gather(P, nc, "g_zd", ZDloc, [512, T_LAT])
    barrier(P)
    mE = P.mark()
    C = row_common(P, TB)
    selt, r_sel = load_vecs(P, seld, 2, "sel2")
    ng1, r_ng1 = load_vecs(P, normg[1], 3 * KC, "normg1e")
    fg, r_fg = load_vecs(P, finalg, KC, "finalg")
    sc1, r_sc1 = emit_silu_c(P, C, cvec, 2)
    mod1, r_mod1 = emit_mod_g(C, mw1, r_mw1, modb[1], sc1, r_sc1, 2, "mod1e")
    vecs1 = [emit_modvecs(C, mod1, r_mod1, 0, ng1, r_ng1, "mv1e")]
    BL = mk_ring([(P.sb(f"bm{i}a", [128, TB]), P.sb(f"bm{i}b", [128, TB])) for i in range(2)], 2)
    od3 = outT.rearrange("(c p) t -> p c t", p=128)

    def blockE(t0):
        nt = TB
        gs1, hg1, rv1 = vecs1[0]
        P.op("sp", lambda e: e.dma_start(out=C.X[:, :, :nt], in_=xld3[:, :, t0:t0 + nt]), reads=[r_xl], writes=[C.r_X], dma=True)
        blend_in(ZDg.ap(), t0, nt)
        mixer_out("wout", hg1[:, 1, :], rv1, nt)
        ffn(C, gs1[:, 2, :], shv(mod1, 2, 0), hg1[:, 2, :], rv1, "11", nt)
        emit_rstd(C, C.X, C.r_X, C.SQ, C.r_SQ, C.R, C.r_R, nt)
        for c in range(KC):
            P.op("dve", lambda e, c=c: e.scalar_tensor_tensor(C.XN[:, c, :nt], C.X[:, c, :nt], fg[:, c:c + 1], C.R[:, :nt], ALU.mult, ALU.mult),
                 reads=[C.r_X, C.r_R, r_fg], writes=[C.r_XN[c]])
        P.op("sp", lambda e: e.dma_start(out=od3[:, :, t0:t0 + nt], in_=C.XN[:, :, :nt]), reads=C.r_XN, dma=True)
    for t0 in range(0, NT_CORE, TB):
        blockE(t0)
    P.finish()
    P.emit()
    return nc


def emit_mod_g(C, modw_v, r_modw, modb_d, sc_t, sc_r, ncol, name):
    P = C.P
    mod = P.sb(name, [128, 72, ncol])
    modb, r_modb = load_vecs(P, modb_d, 72, name + "_b")
    r_mod = Res(name)
    for j in range(72):
        wt, wr = C.wsq.next()
        P.op(C.dma_eng(), lambda e, wt=wt, j=j: e.dma_start(out=wt[:], in_=modw_v[j]), reads=[r_modw], writes=[wr], dma=True)
        pt, pr = C.psum.next()
        for c in range(KC):
            P.op("pe", lambda e, pt=pt, wt=wt, c=c: e.matmul(pt[:, 0:ncol], wt[:, c, :], sc_t[:, c, :],
                                                             start=(c == 0), stop=(c == KC - 1)),
                 reads=[wr, sc_r], writes=[pr])
        P.op("dve", lambda e, pt=pt, j=j: e.tensor_scalar(mod[:, j, :], pt[:, 0:ncol], modb[:, j:j + 1], None, ALU.add),
             reads=[pr, r_modb], writes=[r_mod])
    return mod, r_mod


_NC_CACHE = {}


def kernel_fused(**inp):
    inp = {k: np.asarray(v) for k, v in inp.items()}
    f = lambda a: np.ascontiguousarray(np.asarray(a, dtype=np.float32))
    shared = {}
    for l in range(2):
        shared[f"modw{l}"] = wlay(inp["mod_w"][l])
        for i in range(2):
            shared[f"wgu{l}{i}"] = wgulay(inp["ffn_w_gu"][l, i])
            shared[f"wdn{l}{i}"] = wlay(inp["ffn_w_down"][l, i])
    shared["wo"] = wlay(inp["rw_w_o"][0])
    shared["wout"] = wlay(inp["hy_w_out"][0])
    shards = {nm: a.reshape(8, 128, -1) for nm, a in shared.items()}
    hyc = hyena_consts()
    rww = [rwkv_host_weights(inp, j) for j in range(2)]
    hyw = [hyena_host_weights(inp, j) for j in range(2)]
    in_maps = []
    for c in range(NCORES):
        b, j = c // 2, c % 2
        m = {}
        m["xT"] = fm(inp["x"][b, j * NT_CORE:(j + 1) * NT_CORE])
        m["cxT"] = fm(inp["ctx"][b, j * NCTX_CORE:(j + 1) * NCTX_CORE])
        m["cvec"] = f(np.stack([pvec(inp["c"][b]), pvec(inp["c_ctx"])], axis=-1))
        m["sel"] = f(np.tile(np.array([[1.0 - j, float(j)]], np.float32), (128, 1)))
        for l in range(2):
            m[f"modb{l}"] = pvec(f(inp["mod_b"][l]))
            m[f"normg{l}"] = pvec(f(inp["norm_g"][l]).reshape(-1))
        m["finalg"] = pvec(f(inp["final_g"]))
        for nm, a in shards.items():
            m["sh_" + nm] = f(a[c])
        for nm, a in rww[j].items():
            m["rw_" + nm] = a
        for nm, a in hyw[j].items():
            m["hy_" + nm] = a
        for nm, a in hyc.items():
            m["hc_" + nm] = a
        in_maps.append(m)
    if "nc" not in _NC_CACHE:
        import os
        _NC_CACHE["nc"] = build_full(os.environ.get("KSTOP"))
    res = run_bass_kernel_spmd(_NC_CACHE["nc"], in_maps, core_ids=list(range(NCORES)))
    out = np.empty((4, T_LAT, D), np.float32)
    for c in range(NCORES):
        b, j = c // 2, c % 2
        out[b, j * NT_CORE:(j + 1) * NT_CORE, :] = res.results[c]["outT"].T
    return out


def _ffn_ext(C, gs_k, sh_k, hg_k, r_vec, wg, wd_, nt):
    emit_rstd(C, C.X, C.r_X, C.SQ, C.r_SQ, C.R, C.r_R, nt)
    emit_pre(C, C.X, C.r_X, C.XN, C.r_XN, C.R, C.r_R, gs_k, sh_k, r_vec, nt)
    emit_ffn(C, C.XN, C.r_XN, C.H, C.r_H, C.X, C.r_X, hg_k, r_vec, wg, wd_, nt)


def _shv(mod, k, col):
    return mod[:, (3 * k) * KC:(3 * k + 1) * KC, col]


def _mixer_out(C, P, wv_, gate, r_vec, nt):
    def evac(o, pt, pr):
        P.op("dve", lambda e: e.scalar_tensor_tensor(C.X[:, o, :nt], pt[:, :nt], gate[:, o:o + 1], C.X[:, o, :nt], ALU.mult, ALU.add),
             reads=[pr, r_vec, C.r_X], writes=[C.r_X])
    emit_proj(C, C.XN, C.r_XN, wv_, KC, KC, evac, nt)


def build_lC(NT, TB=512):
    nc = bass.Bass("TRN2", target_bir_lowering=False)
    ext = lambda nm, shp: nc.dram_tensor(nm, list(shp), F32, kind="ExternalInput").ap()
    xT = ext("xT", [D, NT]); ygT = ext("ygT", [D, NT]); cvec = ext("cvec", [128, KC, 2])
    mod0d = ext("mod0", [128, 72, 2]); modw1 = ext("modw1", [72, 128, KC, 128]); modb1 = ext("modb1", [128, 72])
    normg0 = ext("normg0", [128, 3 * KC]); normg1 = ext("normg1", [128, 3 * KC])
    wo = ext("wo", [KC, 128, KC, 128])
    wgu01 = ext("wgu01", [FC, 128, 2, KC, 128]); wdn01 = ext("wdn01", [KC, 128, FC, 128])
    wgu10 = ext("wgu10", [FC, 128, 2, KC, 128]); wdn10 = ext("wdn10", [KC, 128, FC, 128])
    xo = nc.dram_tensor("xo", [D, NT], F32, kind="ExternalOutput").ap()
    h1o = nc.dram_tensor("h1o", [D, NT], F32, kind="ExternalOutput").ap()
    mod1o = nc.dram_tensor("mod1o", [128, 72, 2], F32, kind="ExternalOutput").ap()
    P = Prog(nc)
    C = row_common(P, TB)
    mod0 = P.sb("mod0t", [128, 72, 2]); r_mod0 = Res()
    P.op("sp", lambda e: e.dma_start(out=mod0[:], in_=mod0d), writes=[r_mod0], dma=True)
    ng0, r_ng0 = load_vecs(P, normg0, 3 * KC, "ng0")
    ng1, r_ng1 = load_vecs(P, normg1, 3 * KC, "ng1")
    sc, r_sc = emit_silu_c(P, C, cvec, 2)
    mod1, r_mod1 = emit_mod(C, modw1, modb1, sc, r_sc, 2, "mod1")
    P.op("sp", lambda e: e.dma_start(out=mod1o, in_=mod1[:]), reads=[r_mod1], dma=True)
    gs0, hg0, rv0 = emit_modvecs(C, mod0, r_mod0, 0, ng0, r_ng0, "mv0")
    gs1, hg1, rv1 = emit_modvecs(C, mod1, r_mod1, 0, ng1, r_ng1, "mv1")
    xs = xT.rearrange("(c p) t -> p c t", p=128); ys = ygT.rearrange("(c p) t -> p c t", p=128)
    xd = xo.rearrange("(c p) t -> p c t", p=128); hd = h1o.rearrange("(c p) t -> p c t", p=128)

    def block(t0, nt):
        P.op("sp", lambda e: e.dma_start(out=C.X[:, :, :nt], in_=xs[:, :, t0:t0 + nt]), writes=[C.r_X], dma=True)
        P.op("sp", lambda e: e.dma_start(out=C.XN[:, :, :nt], in_=ys[:, :, t0:t0 + nt]), writes=C.r_XN, dma=True)
        _mixer_out(C, P, wo, hg0[:, 1, :], rv0, nt)
        _ffn_ext(C, gs0[:, 2, :], _shv(mod0, 2, 0), hg0[:, 2, :], rv0, wgu01, wdn01, nt)
        _ffn_ext(C, gs1[:, 0, :], _shv(mod1, 0, 0), hg1[:, 0, :], rv1, wgu10, wdn10, nt)
        P.op("sp", lambda e: e.dma_start(out=xd[:, :, t0:t0 + nt], in_=C.X[:, :, :nt]), reads=[C.r_X], dma=True)
        emit_rstd(C, C.X, C.r_X, C.SQ, C.r_SQ, C.R, C.r_R, nt)
        emit_pre(C, C.X, C.r_X, C.XN, C.r_XN, C.R, C.r_R, gs1[:, 1, :], _shv(mod1, 1, 0), rv1, nt)
        P.op("sp", lambda e: e.dma_start(out=hd[:, :, t0:t0 + nt], in_=C.XN[:, :, :nt]), reads=C.r_XN, dma=True)
    for t0 in range(0, NT, TB):
        block(t0, min(TB, NT - t0))
    P.finish()
    P.emit()
    return nc


def build_lE(NT, TB=512):
    nc = bass.Bass("TRN2", target_bir_lowering=False)
    ext = lambda nm, shp: nc.dram_tensor(nm, list(shp), F32, kind="ExternalInput").ap()
    xT = ext("xT", [D, NT]); zT = ext("zT", [D, NT])
    mod1d = ext("mod1", [128, 72, 2]); normg1 = ext("normg1", [128, 3 * KC]); finalg = ext("finalg", [128, KC])
    wout = ext("wout", [KC, 128, KC, 128])
    wgu11 = ext("wgu11", [FC, 128, 2, KC, 128]); wdn11 = ext("wdn11", [KC, 128, FC, 128])
    outT = nc.dram_tensor("outT", [D, NT], F32, kind="ExternalOutput").ap()
    P = Prog(nc)
    C = row_common(P, TB)
    mod1 = P.sb("mod1t", [128, 72, 2]); r_mod1 = Res()
    P.op("sp", lambda e: e.dma_start(out=mod1[:], in_=mod1d), writes=[r_mod1], dma=True)
    ng1, r_ng1 = load_vecs(P, normg1, 3 * KC, "ng1")
    fg, r_fg = load_vecs(P, finalg, KC, "fg")
    gs1, hg1, rv1 = emit_modvecs(C, mod1, r_mod1, 0, ng1, r_ng1, "mv1")
    xs = xT.rearrange("(c p) t -> p c t", p=128); zs = zT.rearrange("(c p) t -> p c t", p=128)
    od = outT.rearrange("(c p) t -> p c t", p=128)

    def block(t0, nt):
        P.op("sp", lambda e: e.dma_start(out=C.X[:, :, :nt], in_=xs[:, :, t0:t0 + nt]), writes=[C.r_X], dma=True)
        P.op("sp", lambda e: e.dma_start(out=C.XN[:, :, :nt], in_=zs[:, :, t0:t0 + nt]), writes=C.r_XN, dma=True)
        _mixer_out(C, P, wout, hg1[:, 1, :], rv1, nt)
        _ffn_ext(C, gs1[:, 2, :], _shv(mod1, 2, 0), hg1[:, 2, :], rv1, wgu11, wdn11, nt)
        emit_rstd(C, C.X, C.r_X, C.SQ, C.r_SQ, C.R, C.r_R, nt)
        for c in range(KC):
            P.op("dve", lambda e, c=c: e.scalar_tensor_tensor(C.XN[:, c, :nt], C.X[:, c, :nt], fg[:, c:c + 1], C.R[:, :nt], ALU.mult, ALU.mult),
                 reads=[C.r_X, C.r_R, r_fg], writes=[C.r_XN[c]])
        P.op("sp", lambda e: e.dma_start(out=od[:, :, t0:t0 + nt], in_=C.XN[:, :, :nt]), reads=C.r_XN, dma=True)
    for t0 in range(0, NT, TB):
        block(t0, min(TB, NT - t0))
    P.finish()
    P.emit()
    return nc


def kernel_multi(**inp):
    inp = {k: np.asarray(v) for k, v in inp.items()}
    f = lambda a: np.ascontiguousarray(np.asarray(a, dtype=np.float32))
    cores = list(range(NCORES))
    run = lambda nc, ims: run_bass_kernel_spmd(nc, ims, core_ids=cores).results
    cv = [f(np.stack([pvec(inp["c"][c // 2]), pvec(inp["c_ctx"])], axis=-1)) for c in cores]
    ng = [pvec(f(inp["norm_g"][l]).reshape(-1)) for l in range(2)]
    wA = dict(modw=wlay(inp["mod_w"][0]), modb=pvec(f(inp["mod_b"][0])), normg=ng[0],
              wgu=wgulay(inp["ffn_w_gu"][0, 0]), wdn=wlay(inp["ffn_w_down"][0, 0]))
    ims = []
    for c in cores:
        b, j = c // 2, c % 2
        m = dict(wA)
        m["xT"] = fm(inp["x"][b, j * NT_CORE:(j + 1) * NT_CORE]); m["cxT"] = fm(inp["ctx"][b, j * NCTX_CORE:(j + 1) * NCTX_CORE])
        m["cvec"] = cv[c]
        ims.append(m)
    rA = run(build_l1(NT_CORE, NCTX_CORE), ims)
    del wA, ims
    rww = [rwkv_host_weights(inp, j) for j in range(2)]
    ims = []
    for c in cores:
        b, j = c // 2, c % 2
        m = {"rw_" + k: v for k, v in rww[j].items()}
        m["hl"] = np.ascontiguousarray(np.concatenate([rA[2 * b]["ho"], rA[2 * b + 1]["ho"]], axis=1))
        m["hc"] = np.ascontiguousarray(np.concatenate([rA[2 * b]["hco"], rA[2 * b + 1]["hco"]], axis=1))
        ims.append(m)
    rB = run(build_rwkv_test(T_LAT, T_CTX), ims)
    del ims
    wC = dict(modw1=wlay(inp["mod_w"][1]), modb1=pvec(f(inp["mod_b"][1])), normg0=ng[0], normg1=ng[1],
              wo=wlay(inp["rw_w_o"][0]), wgu01=wgulay(inp["ffn_w_gu"][0, 1]), wdn01=wlay(inp["ffn_w_down"][0, 1]),
              wgu10=wgulay(inp["ffn_w_gu"][1, 0]), wdn10=wlay(inp["ffn_w_down"][1, 0]))
    ims = []
    for c in cores:
        b, j = c // 2, c % 2
        m = dict(wC)
        m["xT"] = rA[c]["xo"]
        yg = np.concatenate([rB[2 * b]["yg"].reshape(512, T_LAT), rB[2 * b + 1]["yg"].reshape(512, T_LAT)], axis=0)
        m["ygT"] = np.ascontiguousarray(yg[:, j * NT_CORE:(j + 1) * NT_CORE])
        m["cvec"] = cv[c]; m["mod0"] = rA[c]["modo"]
        ims.append(m)
    del rB
    rC = run(build_lC(NT_CORE), ims)
    del wC, ims, rA
    hyc = {"hc_" + k: v for k, v in hyena_consts().items()}
    hyw = [hyena_host_weights(inp, j) for j in range(2)]
    ims = []
    for c in cores:
        b, j = c // 2, c % 2
        m = {"hy_" + k: v for k, v in hyw[j].items()}
        m.update(hyc)
        m["hl"] = np.ascontiguousarray(np.concatenate([rC[2 * b]["h1o"], rC[2 * b + 1]["h1o"]], axis=1))
        ims.append(m)
    rD = run(build_hyena_test(), ims)
    del ims
    wE = dict(normg1=ng[1], finalg=pvec(f(inp["final_g"])), wout=wlay(inp["hy_w_out"][0]),
              wgu11=wgulay(inp["ffn_w_gu"][1, 1]), wdn11=wlay(inp["ffn_w_down"][1, 1]))
    ims = []
    for c in cores:
        b, j = c // 2, c % 2
        m = dict(wE)
        m["xT"] = rC[c]["xo"]; m["mod1"] = rC[c]["mod1o"]
        z = np.concatenate([rD[2 * b]["zo"], rD[2 * b + 1]["zo"]], axis=0)
        m["zT"] = np.ascontiguousarray(z[:, j * NT_CORE:(j + 1) * NT_CORE])
        ims.append(m)
    del rD
    rE = run(build_lE(NT_CORE), ims)
    out = np.empty((4, T_LAT, D), np.float32)
    for c in cores:
        b, j = c // 2, c % 2
        out[b, j * NT_CORE:(j + 1) * NT_CORE, :] = rE[c]["outT"].T
    return out


def kernel(**inp):
    return kernel_multi(**inp)
```
